# Optimizing a Trainium2 kernel written in Bass

```python
import math
import jax, jax.numpy as jnp
from jax import lax
import numpy as np

D_MODEL = 1024
BATCH = 8
SEQ = 8192
DEPTH = 1

ATTN_WIDTH = 512
N_HEADS = 8
HEAD_DIM = 64
N_KV_GROUPS = 2
HEADS_PER_GROUP = N_HEADS // N_KV_GROUPS
KV_WIDTH = N_KV_GROUPS * HEAD_DIM
CMP_LEN = 32
CMP_STRIDE = 16
CMP_HIDDEN = 256
SEL_BLOCK = 64
SEL_TOPN = 16
WINDOW = 512
Q_BLOCK = 64
N_BRANCH = 3
ROPE_THETA = 10000.0
SSM_WIDTH = D_MODEL - ATTN_WIDTH
SSM_GROUP = 16
SSM_GROUPS = SSM_WIDTH // SSM_GROUP
SSM_STATE = 64
SSM_CHUNK = 128
DT_MIN = 1e-3
DT_MAX = 1e-1
EPS = 1e-6
IN_WIDTH = ATTN_WIDTH + 6 * KV_WIDTH + ATTN_WIDTH + N_BRANCH * N_HEADS + 2 * SSM_WIDTH

kernel_name = "hymba_nsa_s5_hybrid"


def rms_norm(x, g):
    xf = x.astype(jnp.float32)
    y = xf * lax.rsqrt(jnp.mean(xf * xf, axis=-1, keepdims=True) + EPS)
    return (y * g.astype(jnp.float32)).astype(x.dtype)


def rope_tables(T, dtype):
    inv = 1.0 / (ROPE_THETA ** (jnp.arange(0, HEAD_DIM, 2, dtype=jnp.float32) / HEAD_DIM))
    ang = jnp.arange(T, dtype=jnp.float32)[:, None] * inv[None, :]
    return jnp.cos(ang).astype(dtype), jnp.sin(ang).astype(dtype)


def apply_rope(x, cos, sin):
    x1, x2 = jnp.split(x, 2, axis=-1)
    c = cos[None, :, None, :]
    s = sin[None, :, None, :]
    return jnp.concatenate([x1 * c - x2 * s, x2 * c + x1 * s], axis=-1)


def masked_softmax(s, mask):
    s = jnp.where(mask, s.astype(jnp.float32), -jnp.inf)
    m = jnp.max(s, axis=-1, keepdims=True)
    m = jnp.where(jnp.isfinite(m), m, 0.0)
    e = jnp.where(mask, jnp.exp(s - m), 0.0)
    return e / jnp.maximum(jnp.sum(e, axis=-1, keepdims=True), 1e-30)


def compress(kv_raw, pos, w1, w2):
    B, T = kv_raw.shape[0], kv_raw.shape[1]
    nc = (T - CMP_LEN) // CMP_STRIDE + 1
    idx = jnp.arange(nc)[:, None] * CMP_STRIDE + jnp.arange(CMP_LEN)[None, :]
    blocks = kv_raw[:, idx] + pos[None, None, :, None, :]
    blocks = blocks.transpose(0, 1, 3, 2, 4).reshape(B, nc, N_KV_GROUPS, CMP_LEN * HEAD_DIM)
    return jax.nn.gelu(blocks @ w1) @ w2


def block_importance(p_cmp, nsb):
    r = SEL_BLOCK // CMP_STRIDE
    ov = CMP_LEN // CMP_STRIDE
    nc = p_cmp.shape[-1]
    left = ov - 1
    right = r * nsb + r - nc
    pp = jnp.pad(p_cmp, [(0, 0)] * (p_cmp.ndim - 1) + [(left, right)])
    terms = []
    for m in range(r):
        for n in range(ov):
            start = m - n + left
            terms.append(pp[..., start:start + r * (nsb - 1) + 1:r])
    return sum(terms[1:], terms[0])


def nsa_attention(q, kc, vc, k_slc, v_slc, k_win, v_win, gates):
    B, T = q.shape[0], q.shape[1]
    cos, sin = rope_tables(T, q.dtype)
    q_r = apply_rope(q, cos, sin)
    k_slc = apply_rope(k_slc, cos, sin)
    k_win = apply_rope(k_win, cos, sin)
    q_g = q.reshape(B, T, N_KV_GROUPS, HEADS_PER_GROUP, HEAD_DIM)
    q_rg = q_r.reshape(B, T, N_KV_GROUPS, HEADS_PER_GROUP, HEAD_DIM)
    gates = gates.reshape(B, T, N_KV_GROUPS, HEADS_PER_GROUP, N_BRANCH)
    nsb = T // SEL_BLOCK
    n_sel = min(SEL_TOPN, nsb)
    nc = kc.shape[1]
    k_blocks = k_slc.reshape(B, nsb, SEL_BLOCK, N_KV_GROUPS, HEAD_DIM).transpose(0, 3, 1, 2, 4)
    v_blocks = v_slc.reshape(B, nsb, SEL_BLOCK, N_KV_GROUPS, HEAD_DIM).transpose(0, 3, 1, 2, 4)
    k_win_p = jnp.pad(k_win, ((0, 0), (WINDOW, 0), (0, 0), (0, 0)))
    v_win_p = jnp.pad(v_win, ((0, 0), (WINDOW, 0), (0, 0), (0, 0)))
    cmp_end = jnp.arange(nc) * CMP_STRIDE + CMP_LEN - 1
    blk = jnp.arange(nsb)
    scale = HEAD_DIM ** -0.5
    gather = jax.vmap(jax.vmap(lambda kb, ix: kb[ix]))
    n_keys = n_sel * SEL_BLOCK

    def block_fn(qb):
        t0 = qb * Q_BLOCK
        t = t0 + jnp.arange(Q_BLOCK)
        qc = lax.dynamic_slice_in_dim(q_g, t0, Q_BLOCK, 1)
        qr = lax.dynamic_slice_in_dim(q_rg, t0, Q_BLOCK, 1)
        g = lax.dynamic_slice_in_dim(gates, t0, Q_BLOCK, 1)
        s = jnp.einsum('bqghd,bngd->bghqn', qc, kc) * scale
        p = masked_softmax(s, cmp_end[None, :] <= t[:, None])
        o_cmp = jnp.einsum('bghqn,bngd->bqghd', p.astype(vc.dtype), vc)
        imp = block_importance(jnp.sum(p, axis=2), nsb)
        cur = t // SEL_BLOCK
        forced = (blk[None, :] == 0) | (blk[None, :] == cur[:, None]) | (blk[None, :] == cur[:, None] - 1)
        valid = blk[None, :] * SEL_BLOCK <= t[:, None]
        imp = jnp.where(forced, jnp.inf, jnp.where(valid, imp, -jnp.inf))
        _, idx = lax.top_k(imp, n_sel)
        ks = gather(k_blocks, idx).reshape(B, N_KV_GROUPS, Q_BLOCK, n_keys, HEAD_DIM)
        vs = gather(v_blocks, idx).reshape(B, N_KV_GROUPS, Q_BLOCK, n_keys, HEAD_DIM)
        kpos = (idx[..., None] * SEL_BLOCK + jnp.arange(SEL_BLOCK)).reshape(B, N_KV_GROUPS, Q_BLOCK, n_keys)
        s = jnp.einsum('bqghd,bgqkd->bghqk', qr, ks) * scale
        p = masked_softmax(s, (kpos <= t[:, None])[:, :, None])
        o_slc = jnp.einsum('bghqk,bgqkd->bqghd', p.astype(vs.dtype), vs)
        kw = lax.dynamic_slice_in_dim(k_win_p, t0, WINDOW + Q_BLOCK, 1)
        vw = lax.dynamic_slice_in_dim(v_win_p, t0, WINDOW + Q_BLOCK, 1)
        wpos = t0 - WINDOW + jnp.arange(WINDOW + Q_BLOCK)
        dist = t[:, None] - wpos[None, :]
        wmask = (dist >= 0) & (dist < WINDOW) & (wpos[None, :] >= 0)
        s = jnp.einsum('bqghd,bkgd->bghqk', qr, kw) * scale
        p = masked_softmax(s, wmask)
        o_win = jnp.einsum('bghqk,bkgd->bqghd', p.astype(vw.dtype), vw)
        return g[..., 0:1] * o_cmp + g[..., 1:2] * o_slc + g[..., 2:3] * o_win

    out = lax.map(block_fn, jnp.arange(T // Q_BLOCK))
    return out.transpose(1, 0, 2, 3, 4, 5).reshape(B, T, ATTN_WIDTH)


def s5_mixer(u, a_re, a_im, log_dt, b_re, b_im, c_re, c_im, d_skip):
    B, T = u.shape[0], u.shape[1]
    f32 = jnp.float32
    uf = u.astype(f32)
    lam = lax.complex(a_re.astype(f32), a_im.astype(f32))
    dt = jnp.exp(log_dt.astype(f32))[:, None]
    lam_bar = jnp.exp(lam * dt)
    b_bar = ((lam_bar - 1.0) / lam)[..., None] * lax.complex(b_re.astype(f32), b_im.astype(f32))
    c_mat = lax.complex(c_re.astype(f32), c_im.astype(f32))
    nch = T // SSM_CHUNK
    uc = uf.reshape(B, nch, SSM_CHUNK, SSM_GROUPS, SSM_GROUP).transpose(1, 0, 2, 3, 4)

    def binop(e1, e2):
        a1, b1 = e1
        a2, b2 = e2
        return a1 * a2, a2 * b1 + b2

    def step(carry, u_chunk):
        bu = jnp.einsum('btgc,gpc->btgp', u_chunk.astype(jnp.complex64), b_bar)
        a = jnp.broadcast_to(lam_bar, bu.shape)
        a_cum, b_cum = lax.associative_scan(binop, (a, bu), axis=1)
        s = a_cum * carry[:, None] + b_cum
        y = jnp.real(jnp.einsum('gcp,btgp->btgc', c_mat, s))
        return s[:, -1], y

    carry0 = jnp.zeros((B, SSM_GROUPS, SSM_STATE), jnp.complex64)
    _, y = lax.scan(step, carry0, uc)
    y = y.transpose(1, 0, 2, 3, 4).reshape(B, T, SSM_WIDTH)
    return y + d_skip.astype(f32) * uf


def hybrid_layer(x, c, w_ada, b_ada, norm_g, w_in, q_norm_g, k_cmp_norm_g, k_slc_norm_g, k_win_norm_g,
                 cmp_pos_k, cmp_pos_v, cmp_w1_k, cmp_w2_k, cmp_w1_v, cmp_w2_v,
                 ssm_a_re, ssm_a_im, ssm_log_dt, ssm_b_re, ssm_b_im, ssm_c_re, ssm_c_im, ssm_d,
                 glu_w, glu_b, w_out):
    B, T = x.shape[0], x.shape[1]
    mod = jax.nn.silu(c) @ w_ada + b_ada
    shift, scale, gate = jnp.split(mod, 3, axis=-1)
    h = rms_norm(x, norm_g) * (1 + scale[:, None, :]) + shift[:, None, :]
    proj = h @ w_in
    widths = [ATTN_WIDTH] + [KV_WIDTH] * 6 + [ATTN_WIDTH, N_BRANCH * N_HEADS, SSM_WIDTH, SSM_WIDTH]
    offs = [int(v) for v in np.cumsum(widths)[:-1]]
    q, kcr, vcr, ksl, vsl, kwn, vwn, z_a, g_br, u, z_s = jnp.split(proj, offs, axis=-1)

    def kv_heads(t):
        return t.reshape(B, T, N_KV_GROUPS, HEAD_DIM)

    q = rms_norm(q.reshape(B, T, N_HEADS, HEAD_DIM), q_norm_g)
    kc = rms_norm(compress(kv_heads(kcr), cmp_pos_k, cmp_w1_k, cmp_w2_k), k_cmp_norm_g)
    vc = compress(kv_heads(vcr), cmp_pos_v, cmp_w1_v, cmp_w2_v)
    k_slc = rms_norm(kv_heads(ksl), k_slc_norm_g)
    k_win = rms_norm(kv_heads(kwn), k_win_norm_g)
    gates = jax.nn.sigmoid(g_br).reshape(B, T, N_HEADS, N_BRANCH)
    attn = nsa_attention(q, kc, vc, k_slc, kv_heads(vsl), k_win, kv_heads(vwn), gates)
    attn = attn * jax.nn.silu(z_a)

    y = s5_mixer(u, ssm_a_re, ssm_a_im, ssm_log_dt, ssm_b_re, ssm_b_im, ssm_c_re, ssm_c_im, ssm_d)
    y = jax.nn.gelu(y)
    y = y * jax.nn.sigmoid(y @ glu_w.astype(jnp.float32) + glu_b.astype(jnp.float32))
    ssm = y.astype(x.dtype) * jax.nn.silu(z_s)

    mix = jnp.concatenate([attn, ssm], axis=-1) @ w_out
    return x + gate[:, None, :] * mix


def setup_inputs(seed: int = 0) -> dict:
    key = jax.random.key(seed)
    ks = jax.random.split(key, 32)
    f32 = jnp.float32

    def nrm(k, shape, s):
        return jax.random.normal(k, shape, f32) * s

    D = D_MODEL
    L = DEPTH
    ng, p, cg = SSM_GROUPS, SSM_STATE, SSM_GROUP
    return {
        "x": nrm(ks[0], (BATCH, SEQ, D), 1.0),
        "c": nrm(ks[1], (BATCH, D), 1.0),
        "w_ada": nrm(ks[2], (L, D, 3 * D), 0.5 * D ** -0.5),
        "b_ada": nrm(ks[3], (L, 3 * D), 0.01),
        "norm_g": 1.0 + nrm(ks[4], (L, D), 0.02),
        "w_in": nrm(ks[5], (L, D, IN_WIDTH), D ** -0.5),
        "q_norm_g": 1.0 + nrm(ks[6], (L, HEAD_DIM), 0.02),
        "k_cmp_norm_g": 1.0 + nrm(ks[7], (L, HEAD_DIM), 0.02),
        "k_slc_norm_g": 1.0 + nrm(ks[8], (L, HEAD_DIM), 0.02),
        "k_win_norm_g": 1.0 + nrm(ks[9], (L, HEAD_DIM), 0.02),
        "cmp_pos_k": nrm(ks[10], (L, CMP_LEN, HEAD_DIM), 0.1),
        "cmp_pos_v": nrm(ks[11], (L, CMP_LEN, HEAD_DIM), 0.1),
        "cmp_w1_k": nrm(ks[12], (L, CMP_LEN * HEAD_DIM, CMP_HIDDEN), (CMP_LEN * HEAD_DIM) ** -0.5),
        "cmp_w2_k": nrm(ks[13], (L, CMP_HIDDEN, HEAD_DIM), CMP_HIDDEN ** -0.5),
        "cmp_w1_v": nrm(ks[14], (L, CMP_LEN * HEAD_DIM, CMP_HIDDEN), (CMP_LEN * HEAD_DIM) ** -0.5),
        "cmp_w2_v": nrm(ks[15], (L, CMP_HIDDEN, HEAD_DIM), CMP_HIDDEN ** -0.5),
        "ssm_a_re": -0.5 + nrm(ks[16], (L, ng, p), 0.01),
        "ssm_a_im": math.pi * jnp.arange(p, dtype=f32)[None, None, :] + nrm(ks[17], (L, ng, p), 0.01),
        "ssm_log_dt": jax.random.uniform(ks[18], (L, ng), f32, math.log(DT_MIN), math.log(DT_MAX)),
        "ssm_b_re": nrm(ks[19], (L, ng, p, cg), (2 * cg) ** -0.5),
        "ssm_b_im": nrm(ks[20], (L, ng, p, cg), (2 * cg) ** -0.5),
        "ssm_c_re": nrm(ks[21], (L, ng, cg, p), p ** -0.5),
        "ssm_c_im": nrm(ks[22], (L, ng, cg, p), p ** -0.5),
        "ssm_d": nrm(ks[23], (L, SSM_WIDTH), 1.0),
        "glu_w": nrm(ks[24], (L, SSM_WIDTH, SSM_WIDTH), SSM_WIDTH ** -0.5),
        "glu_b": nrm(ks[25], (L, SSM_WIDTH), 0.01),
        "w_out": nrm(ks[26], (L, D, D), D ** -0.5),
    }


def reference(x, c, w_ada, b_ada, norm_g, w_in, q_norm_g, k_cmp_norm_g, k_slc_norm_g, k_win_norm_g,
              cmp_pos_k, cmp_pos_v, cmp_w1_k, cmp_w2_k, cmp_w1_v, cmp_w2_v,
              ssm_a_re, ssm_a_im, ssm_log_dt, ssm_b_re, ssm_b_im, ssm_c_re, ssm_c_im, ssm_d,
              glu_w, glu_b, w_out):
    for l in range(DEPTH):
        x = hybrid_layer(x, c, w_ada[l], b_ada[l], norm_g[l], w_in[l], q_norm_g[l], k_cmp_norm_g[l],
                         k_slc_norm_g[l], k_win_norm_g[l], cmp_pos_k[l], cmp_pos_v[l], cmp_w1_k[l],
                         cmp_w2_k[l], cmp_w1_v[l], cmp_w2_v[l], ssm_a_re[l], ssm_a_im[l], ssm_log_dt[l],
                         ssm_b_re[l], ssm_b_im[l], ssm_c_re[l], ssm_c_im[l], ssm_d[l], glu_w[l], glu_b[l],
                         w_out[l])
    return x
```

```python
import math
import numpy as np
import ml_dtypes
from contextlib import ExitStack
import concourse.bass as bass
import concourse.mybir as mybir
from concourse.bass_utils import run_bass_kernel_spmd

F32 = mybir.dt.float32
BF16 = mybir.dt.bfloat16
I32 = mybir.dt.int32
AF = mybir.ActivationFunctionType
ALU = mybir.AluOpType
AX = mybir.AxisListType

T = 8192
D = 1024
CH = 512
NCH = T // CH
EPS = 1e-6
BIG = 30000.0
NW1 = 1312
WITH_ATTN = True
WITH_SSM = True
NCH_RUN = NCH
STAGE = 99

ENGS = ["pe", "act", "dve", "pool", "sp"]


class Sched:
    def __init__(self, nc, ctx, n_dma_sems=12):
        self.nc = nc
        self.prog = {e: [] for e in ENGS}
        self.sem = {e: ctx.enter_context(nc.semaphore("s_" + e)) for e in ENGS}
        self.cnt = {e: 0 for e in ENGS}
        self.seen = {e: {} for e in ENGS}
        self.last_w = {}
        self.readers = {}
        self.dsem, self.dcnt, self.dnext = {}, {}, {}
        for q in ["sp", "act"]:
            self.dsem[q] = [ctx.enter_context(nc.semaphore("d_%s%d" % (q, i))) for i in range(n_dma_sems)]
            self.dcnt[q] = [0] * n_dma_sems
            self.dnext[q] = 0
        self.semobj = {}
        for e in ENGS:
            self.semobj[("e", e)] = self.sem[e]
        for q in self.dsem:
            for i, s in enumerate(self.dsem[q]):
                self.semobj[("d", q, i)] = s

    def _waits_for(self, eng, toks):
        need = {}
        for (k, v) in toks:
            if k == ("e", eng) and (v > self.cnt[eng] or eng == "pe"):
                continue
            if self.seen[eng].get(k, 0) < v:
                need[k] = max(need.get(k, 0), v)
        for k, v in need.items():
            self.seen[eng][k] = v
        return list(need.items())

    def _deps(self, reads, writes):
        toks = []
        for r in reads:
            t = self.last_w.get(r)
            if t is not None:
                toks.append(t)
        for w in writes:
            t = self.last_w.get(w)
            if t is not None:
                toks.append(t)
            toks.extend(self.readers.get(w, []))
        return toks

    def _commit(self, tok, reads, writes):
        for r in reads:
            self.readers.setdefault(r, []).append(tok)
        for w in writes:
            self.last_w[w] = tok
            self.readers[w] = []

    def op(self, eng, fn, reads=(), writes=(), inc=True):
        toks = self._deps(reads, writes)
        waits = self._waits_for(eng, toks)
        if inc:
            self.cnt[eng] += 1
            tok = (("e", eng), self.cnt[eng])
        else:
            tok = (("e", eng), self.cnt[eng] + 1)
        self.prog[eng].append((waits, fn, ("e", eng) if inc else None, 1))
        self._commit(tok, reads, writes)
        return tok

    def dma(self, q, fn, reads=(), writes=()):
        toks = self._deps(reads, writes)
        i = self.dnext[q]
        self.dnext[q] = (i + 1) % len(self.dsem[q])
        key = ("d", q, i)
        if self.dcnt[q][i] > 0:
            toks.append((key, 16 * self.dcnt[q][i]))
        waits = self._waits_for(q, toks)
        self.dcnt[q][i] += 1
        tok = (key, 16 * self.dcnt[q][i])
        self.prog[q].append((waits, fn, key, 16))
        self._commit(tok, reads, writes)
        return tok

    def final_wait(self, eng, toks):
        waits = self._waits_for(eng, toks)
        self.prog[eng].append((waits, None, None, 0))

    def barrier(self):
        toks = [(("e", e), self.cnt[e]) for e in ENGS if self.cnt[e] > 0]
        for q in self.dsem:
            for i in range(len(self.dsem[q])):
                if self.dcnt[q][i] > 0:
                    toks.append((("d", q, i), 16 * self.dcnt[q][i]))
        for e in ENGS:
            self.final_wait(e, toks)

    def emit(self):
        nc = self.nc
        prog = self.prog
        semobj = self.semobj

        def run(e_obj, lst):
            for waits, fn, key, amt in lst:
                for k, v in waits:
                    e_obj.wait_ge(semobj[k], v)
                if fn is None:
                    continue
                ins = fn(e_obj)
                if key is not None:
                    ins.then_inc(semobj[key], amt)

        with nc.Block() as block:
            @block.tensor
            def _(e):
                run(e, prog["pe"])

            @block.scalar
            def _(e):
                run(e, prog["act"])

            @block.vector
            def _(e):
                run(e, prog["dve"])

            @block.gpsimd
            def _(e):
                run(e, prog["pool"])

            @block.sync
            def _(e):
                run(e, prog["sp"])
        self.prog = {e: [] for e in ENGS}


def _bf(a):
    return np.ascontiguousarray(a).astype(ml_dtypes.bfloat16)


def make_consts():
    c = {}
    c["ident_bf"] = _bf(np.eye(128, dtype=np.float32))
    c["ident_f"] = np.eye(128, dtype=np.float32)
    ob = np.zeros((128, 128), np.float32)
    ob[:64, :64] = 1.0
    ob[64:, 64:] = 1.0
    c["onesblk_f"] = ob
    pm = np.zeros((128, 128), np.float32)
    for j in range(64):
        pm[64 + j, j] = -1.0
        pm[j, 64 + j] = 1.0
    c["pm_f"] = pm
    c["mrow"] = np.tile(np.arange(128, dtype=np.float32)[None, :], (128, 1))
    inv = 1.0 / (10000.0 ** (np.arange(0, 64, 2, dtype=np.float32) / 64.0))
    t = np.arange(T, dtype=np.float32)
    ang = t[None, :] * inv[:, None].astype(np.float32)
    cos32 = np.cos(ang).astype(np.float32)
    sin32 = np.sin(ang).astype(np.float32)
    cos64 = np.concatenate([cos32, cos32], 0)
    sin64 = np.concatenate([-sin32, sin32], 0)
    cosT = np.concatenate([cos64, cos64], 0)
    sinS = np.concatenate([sin64, sin64], 0)
    c["cosT"] = np.ascontiguousarray(cosT.reshape(128, NCH, CH).transpose(1, 0, 2))
    c["sinS"] = np.ascontiguousarray(sinS.reshape(128, NCH, CH).transpose(1, 0, 2))
    kl = np.arange(128)[:, None]
    tl = np.arange(CH)[None, :]
    cm = np.zeros((4, 128, CH), np.float32)
    wl = np.zeros((4, 128, CH), np.float32)
    pmk = np.zeros((4, 128, CH), np.float32)
    for v in range(4):
        cm[v] = np.where(128 * v + kl > tl, -BIG, 0.0)
        wl[v] = np.where(128 * v + kl <= tl, -BIG, 0.0)
        pmk[v] = np.where(16 * kl + 31 > 512 * v + tl, -BIG, 0.0)
    c["cmask"] = _bf(cm)
    c["wlo"] = _bf(wl)
    c["cmpmask"] = _bf(pmk)
    A = np.zeros((512, 128), np.float32)
    for j in range(128):
        for m in range(4):
            for n in range(2):
                idx = 4 * j + m - n
                if 0 <= idx < 511:
                    A[idx, j] += 1.0
    c["ZA"] = _bf(A.reshape(4, 128, 128))
    r = np.arange(128)[:, None]
    rel = np.arange(256)[None, :] - 126
    cur = r // 64
    fadd = np.zeros((128, 256), np.float32)
    fadd = np.where(rel > cur, -1e4, fadd)
    fadd = np.where((rel == cur) | (rel == cur - 1), 1e4, fadd)
    c["Fadd"] = fadd.astype(np.float32)
    c["Finv"] = np.where(rel > cur, -BIG, 0.0).astype(np.float32)
    key = np.arange(T)[None, :]
    jj = np.arange(64)[:, None]
    c["Epat"] = _bf(((key // 64) % 64 == jj).astype(np.float32))
    gs = np.zeros((32, 24, 64), np.float32)
    for k in range(24):
        gs[k, k, :] = 1.0
    c["Gsel"] = gs
    c["ones_row"] = np.ones((1, 128), np.float32)
    pr = np.zeros((128, 128), np.float32)
    for e in range(2):
        for d in range(64):
            pr[e * 64 + (d + 32) % 64, e * 64 + d] = 1.0
    c["Prot"] = _bf(pr)
    return c


def make_params(inp):
    p = {}
    f = lambda a: np.ascontiguousarray(np.asarray(a, dtype=np.float32))
    l = 0
    w_in = f(inp["w_in"][l])
    q = w_in[:, 0:512]
    kcr = w_in[:, 512:640]
    vcr = w_in[:, 640:768]
    ksl = w_in[:, 768:896]
    vsl = w_in[:, 896:1024]
    kwn = w_in[:, 1024:1152]
    vwn = w_in[:, 1152:1280]
    z_a = w_in[:, 1280:1792]
    g_br = w_in[:, 1792:1816]
    u = w_in[:, 1816:2328]
    z_s = w_in[:, 2328:2840]

    gpad = np.concatenate([g_br, np.zeros((1024, 8), np.float32)], 1)
    W1 = np.concatenate([q, ksl, kwn, kcr, vcr, gpad, vsl, vwn], 1)
    assert W1.shape[1] == NW1
    p["W1"] = f(W1)
    p["W2"] = f(np.concatenate([u, z_s, z_a], 1))
    p["w_ada"] = f(inp["w_ada"][l])
    p["bada_col"] = f(inp["b_ada"][l].reshape(24, 128).T)
    p["bada_grow"] = f(inp["b_ada"][l][2048:3072].reshape(1, 1024))
    p["ng_col"] = f(inp["norm_g"][l].reshape(8, 128).T)

    def gcols(g):
        g = np.asarray(g, np.float32)
        return f(np.stack([np.tile(g, 2), np.tile(g[(np.arange(64) + 32) % 64], 2)], 1))

    p["gq"] = gcols(inp["q_norm_g"][l])
    p["gks"] = gcols(inp["k_slc_norm_g"][l])
    p["gkw"] = gcols(inp["k_win_norm_g"][l])
    p["gkc_bc"] = f(np.tile(np.asarray(inp["k_cmp_norm_g"][l], np.float32)[None, :], (128, 1)))
    p["posk_col"] = f(np.asarray(inp["cmp_pos_k"][l]).reshape(16, 128).T)
    p["posv_col"] = f(np.asarray(inp["cmp_pos_v"][l]).reshape(16, 128).T)
    p["w1k"] = f(inp["cmp_w1_k"][l])
    p["w1v"] = f(inp["cmp_w1_v"][l])
    p["w2k"] = f(inp["cmp_w2_k"][l])
    p["w2v"] = f(inp["cmp_w2_v"][l])
    a_re = np.asarray(inp["ssm_a_re"][l], np.float32)
    a_im = np.asarray(inp["ssm_a_im"][l], np.float32)
    ldt = np.asarray(inp["ssm_log_dt"][l], np.float32)
    p["are2"] = f(np.concatenate([a_re.T, a_re.T], 0))
    p["aim2"] = f(np.concatenate([a_im.T, a_im.T], 0))
    p["ldt2"] = f(np.tile(ldt[None, :], (128, 1)))
    p["b_are"] = f(np.tile(a_re[None, :, :], (128, 1, 1)))
    p["b_aim"] = f(np.tile(a_im[None, :, :], (128, 1, 1)))
    p["b_ldt"] = f(np.tile(ldt[None, :, None], (128, 1, 64)))
    b_re = np.asarray(inp["ssm_b_re"][l], np.float32)
    b_im = np.asarray(inp["ssm_b_im"][l], np.float32)
    bre_l = np.zeros((128, 32, 64), np.float32)
    bim_l = np.zeros((128, 32, 64), np.float32)
    for g in range(32):
        k0 = 16 * (g % 8)
        bre_l[k0:k0 + 16, g, :] = b_re[g].T
        bim_l[k0:k0 + 16, g, :] = b_im[g].T
    p["b_bre"] = bre_l
    p["b_bim"] = bim_l
    c_re = np.asarray(inp["ssm_c_re"][l], np.float32)
    c_im = np.asarray(inp["ssm_c_im"][l], np.float32)
    p["cw_l"] = f(np.concatenate([c_re.transpose(2, 0, 1), c_im.transpose(2, 0, 1)], 0))
    p["dcol"] = f(np.asarray(inp["ssm_d"][l]).reshape(4, 128).T)
    p["glub_col"] = f(np.asarray(inp["glu_b"][l]).reshape(4, 128).T)
    p["glu_w"] = f(inp["glu_w"][l])
    p["w_out"] = f(inp["w_out"][l])
    return p


IN_SPECS = None


def input_specs(consts, params):
    specs = {"x": ((T, D), F32), "c_col": ((128, 8), F32)}
    for d in (consts, params):
        for k, v in d.items():
            specs[k] = (tuple(v.shape), BF16 if v.dtype == ml_dtypes.bfloat16 else F32)
    return specs


def build(specs, dbg=None):
    nc = bass.Bass("TRN2", target_bir_lowering=False)
    dr = {}
    for name, (shape, dt) in specs.items():
        dr[name] = nc.dram_tensor(name, list(shape), dt, kind="ExternalInput").ap()
    out = nc.dram_tensor("out", [T, D], F32, kind="ExternalOutput").ap()
    attn_scr = nc.dram_tensor("attn_scr", [512, T], BF16, kind="Internal").ap()
    dr["_zscr"] = nc.dram_tensor("zscr", [4, 512], F32, kind="Internal").ap()
    dr["_gscr"] = nc.dram_tensor("gscr", [2, 32, 512], F32, kind="Internal").ap()
    dbg_out = {}
    if dbg:
        for name, (shape, dt) in dbg.items():
            dbg_out[name] = nc.dram_tensor(name, list(shape), dt, kind="ExternalOutput").ap()

    with ExitStack() as ctx0:
        S = Sched(nc, ctx0)
        uid = [0]

        def U(prefix):
            uid[0] += 1
            return "%s_%d" % (prefix, uid[0])

        def load(ctx, name, shape, dt, src_ap, q="sp"):
            t = ctx.enter_context(nc.sbuf_tensor("sb_" + name, list(shape), dt))
            S.dma(q, lambda e: e.dma_start(out=t[:], in_=src_ap), writes=[name])
            return t

        def load_cast(t, name, shape3, src_w, stage, col0=0):
            kcs, ncols = shape3[1], shape3[2]
            srcv = src_w.rearrange("(kc p) n -> p kc n", p=128)
            per = max(1, 2048 // kcs)
            c = 0
            pi = 0
            while c < ncols:
                w = min(per, ncols - c)
                st = stage[pi % 2]
                rn = "stage%d" % (pi % 2)
                S.dma("sp", lambda e, st=st, c=c, w=w: e.dma_start(out=st[:, 0:kcs * w].rearrange("p (k n) -> p k n", k=kcs), in_=srcv[:, :, col0 + c:col0 + c + w]), writes=[rn])
                eng = ["dve", "pool", "act"][pi % 3]
                if eng == "act":
                    S.op(eng, lambda e, st=st, c=c, w=w: e.copy(t[:, :, c:c + w], st[:, 0:kcs * w].rearrange("p (k n) -> p k n", k=kcs)), reads=[rn], writes=[name])
                else:
                    S.op(eng, lambda e, st=st, c=c, w=w: e.tensor_copy(t[:, :, c:c + w], st[:, 0:kcs * w].rearrange("p (k n) -> p k n", k=kcs)), reads=[rn], writes=[name + "_%d" % pi])
                c += w
                pi += 1
            return t, [name] + [name + "_%d" % j for j in range(pi)]

        ident_bf = load(ctx0, "ident_bf", [128, 128], BF16, dr["ident_bf"][:, :])
        ident_f = load(ctx0, "ident_f", [128, 128], F32, dr["ident_f"][:, :])
        c_col = load(ctx0, "c_col", [128, 8], F32, dr["c_col"][:, :])
        bada_col = load(ctx0, "bada_col", [128, 24], F32, dr["bada_col"][:, :])
        ng_col = load(ctx0, "ng_col", [128, 8], F32, dr["ng_col"][:, :])
        bada_grow = load(ctx0, "bada_grow", [1, 1024], F32, dr["bada_grow"][:, :])
        ones_row = load(ctx0, "ones_row", [1, 128], F32, dr["ones_row"][:, :])
        gs_col = ctx0.enter_context(nc.sbuf_tensor("gs_col", [128, 8], F32))
        sh_col = ctx0.enter_context(nc.sbuf_tensor("sh_col", [128, 8], F32))
        gate_bc = ctx0.enter_context(nc.sbuf_tensor("gate_bc", [128, 1024], F32))

        with ExitStack() as c0:
            sc_col = c0.enter_context(nc.sbuf_tensor("sc_col", [128, 8], F32))
            mod_col = c0.enter_context(nc.sbuf_tensor("mod_col", [128, 24], F32))
            grow = c0.enter_context(nc.sbuf_tensor("grow", [1, 1024], F32))
            wst = [c0.enter_context(nc.sbuf_tensor("wst%d" % i, [128, 8, 128], F32)) for i in range(2)]
            pmod = c0.enter_context(nc.psum_tensor("pmod", [128, 512], F32))
            prow = c0.enter_context(nc.psum_tensor("prow", [128, 512], F32))
            pbc = c0.enter_context(nc.psum_tensor("pbc", [128, 512], F32))
            S.op("act", lambda e: e.activation(sc_col[:], c_col[:], AF.Silu), reads=["c_col"], writes=["sc_col"])
            wv = dr["w_ada"].rearrange("(kc p) n -> p kc n", p=128)
            for jc in range(24):
                st = wst[jc % 2]
                rn = "wst%d" % (jc % 2)
                S.dma("sp", lambda e, st=st, jc=jc: e.dma_start(out=st[:], in_=wv[:, :, jc * 128:(jc + 1) * 128]), writes=[rn])
                for kc in range(8):
                    S.op("pe", lambda e, st=st, jc=jc, kc=kc: e.matmul(pmod[:, jc:jc + 1], st[:, kc, :], sc_col[:, kc:kc + 1], start=(kc == 0), stop=(kc == 7)),
                         reads=[rn, "sc_col"], writes=["pmod"], inc=(kc == 7))
                if jc >= 16:
                    j0 = (jc - 16) * 128
                    for kc in range(8):
                        S.op("pe", lambda e, st=st, j0=j0, kc=kc: e.matmul(prow[0:1, (j0 % 512):(j0 % 512) + 128], sc_col[:, kc:kc + 1], st[:, kc, :], start=(kc == 0), stop=(kc == 7)),
                             reads=[rn, "sc_col"], writes=["prow"], inc=(kc == 7))
                    if jc in (19, 23):
                        h0 = 0 if jc == 19 else 512
                        S.op("dve", lambda e, h0=h0: e.tensor_tensor(grow[0:1, h0:h0 + 512], prow[0:1, 0:512], bada_grow[0:1, h0:h0 + 512], ALU.add),
                             reads=["prow", "bada_grow"], writes=["grow"])
            S.op("dve", lambda e: e.tensor_tensor(mod_col[:], pmod[:, 0:24], bada_col[:], ALU.add), reads=["pmod", "bada_col"], writes=["mod_col"])
            S.op("dve", lambda e: e.scalar_tensor_tensor(gs_col[:], mod_col[:, 8:16], 1.0, ng_col[:], ALU.add, ALU.mult), reads=["mod_col", "ng_col"], writes=["gs_col"])
            S.op("dve", lambda e: e.tensor_copy(sh_col[:], mod_col[:, 0:8]), reads=["mod_col"], writes=["sh_col"])
            for h0 in (0, 512):
                S.op("pe", lambda e, h0=h0: e.matmul(pbc[:, 0:512], ones_row[0:1, :], grow[0:1, h0:h0 + 512], start=True, stop=True), reads=["ones_row", "grow"], writes=["pbc"])
                S.op("dve", lambda e, h0=h0: e.tensor_copy(gate_bc[:, h0:h0 + 512], pbc[:, 0:512]), reads=["pbc"], writes=["gate_bc"])
            S.barrier()
            S.emit()

        def front(i, xt_tiles, xt_names, hT, hname, W, evac_eng, tts=(0, 1, 2, 3)):
            for tt in tts:
                gt = 4 * i + tt
                xt = xt_tiles[tt]
                xn_ = xt_names[tt]
                S.dma("sp", lambda e, xt=xt, gt=gt: e.dma_start(out=xt[:], in_=dr["x"][gt * 128:(gt + 1) * 128, :]), writes=[xn_])
                S.op("act", lambda e, xt=xt: e.activation(W["junk"][:], xt[:], AF.Square, accum_out=W["ssq"][:, 0:1]), reads=[xn_], writes=["xn", "ssq"])
                S.op("dve", lambda e: e.tensor_scalar(W["ssq"][:, 1:2], W["ssq"][:, 0:1], 1.0 / D, EPS, ALU.mult, ALU.add), reads=["ssq"], writes=["ssq1"])
                S.op("act", lambda e: e.activation(W["ssq"][:, 2:3], W["ssq"][:, 1:2], AF.Sqrt), reads=["ssq1"], writes=["ssq2"])
                S.op("dve", lambda e: e.reciprocal(W["ssq"][:, 3:4], W["ssq"][:, 2:3]), reads=["ssq2"], writes=["ssq3"])
                S.op("dve", lambda e, xt=xt: e.tensor_scalar(W["xn"][:], xt[:], W["ssq"][:, 3:4], None, ALU.mult), reads=[xn_, "ssq3"], writes=["xn"])
                for half in range(2):
                    for j in range(4):
                        kc = half * 4 + j
                        S.op("pe", lambda e, kc=kc, j=j: e.matmul(W["ptr"][:, j * 128:(j + 1) * 128], W["xn"][:, kc * 128:(kc + 1) * 128], ident_bf[:], start=True, stop=True),
                             reads=["xn", "ident_bf"], writes=[W.get("ptrn", "ptr")], inc=(j == 3))
                    for j in range(4):
                        kc = half * 4 + j
                        eng = evac_eng[kc % len(evac_eng)]
                        if eng == "act":
                            S.op("act", lambda e, kc=kc, j=j, tt=tt: e.activation(hT[:, kc, tt * 128:(tt + 1) * 128], W["ptr"][:, j * 128:(j + 1) * 128], AF.Identity,
                                                                                 bias=sh_col[:, kc:kc + 1], scale=gs_col[:, kc:kc + 1]),
                                 reads=[W.get("ptrn", "ptr"), "gs_col", "sh_col"], writes=[(hname, "act")])
                        else:
                            S.op("dve", lambda e, kc=kc, j=j, tt=tt: e.tensor_scalar(hT[:, kc, tt * 128:(tt + 1) * 128], W["ptr"][:, j * 128:(j + 1) * 128],
                                                                                    gs_col[:, kc:kc + 1], sh_col[:, kc:kc + 1], ALU.mult, ALU.add),
                                 reads=[W.get("ptrn", "ptr"), "gs_col", "sh_col"], writes=[(hname, "dve")])
            return [(hname, "act"), (hname, "dve")]

        if WITH_ATTN:
            pass1(nc, S, dr, attn_scr, dbg_out, front, load, load_cast, ident_bf, ident_f)

        pass2(nc, S, dr, out, attn_scr, dbg_out, front, load, load_cast, ident_bf, ident_f, gate_bc)
    return nc


def pass1(nc, S, dr, attn_scr, dbg_out, front, load, load_cast, ident_bf, ident_f):
    with ExitStack() as c1:
        def sb(name, shape, dt=F32):
            return c1.enter_context(nc.sbuf_tensor("a_" + name, list(shape), dt))

        def psb(name, shape=(128, 512), dt=F32):
            return c1.enter_context(nc.psum_tensor("a_" + name, list(shape), dt))

        w1s = sb("w1s", [128, 8, NW1], BF16)
        cw1 = [sb("cw1k", [128, 16, 256], BF16), sb("cw1v", [128, 16, 256], BF16)]
        cw2 = [sb("cw2k", [128, 2, 64], BF16), sb("cw2v", [128, 2, 64], BF16)]
        Kaug = [sb("Kaug0", [128, T], BF16), sb("Kaug1", [128, T], BF16)]
        Vs = sb("Vs", [128, 64, 2, 65], BF16)
        Kw = sb("Kw", [128, 2, 1024], BF16)
        Vw = sb("Vw", [128, 8, 2, 65], BF16)
        kcT = sb("kcT", [128, 2, 512], BF16)
        Vc = sb("Vc", [128, 4, 2, 65], BF16)
        Xs = [sb("Xk", [128, 2, 1056], BF16), sb("Xv", [128, 2, 1056], BF16)]
        posb = sb("posb", [128, 2, 2])
        ones64 = sb("ones64", [128, 64])
        cmask = load(c1, "cmask", [128, 4, 512], BF16, dr["cmask"].rearrange("v p t -> p v t"))
        wlo = load(c1, "wlo", [128, 4, 512], BF16, dr["wlo"].rearrange("v p t -> p v t"))
        ZA = load(c1, "ZA", [128, 4, 128], BF16, dr["ZA"].rearrange("c p j -> p c j"))
        Fadd = load(c1, "Fadd", [128, 256], F32, dr["Fadd"][:, :])
        Finv = load(c1, "Finv", [128, 256], F32, dr["Finv"][:, :])
        Prot = load(c1, "Prot", [128, 128], BF16, dr["Prot"][:, :])
        onesblk = load(c1, "onesblk_f", [128, 128], F32, dr["onesblk_f"][:, :])
        gcol = [load(c1, nm, [128, 2], F32, dr[nm][:, :]) for nm in ("gq", "gks", "gkw")]
        gkc_bc = load(c1, "gkc_bc", [128, 64], F32, dr["gkc_bc"][:, :])
        posc = [load(c1, "posk_col", [128, 16], F32, dr["posk_col"][:, :]), load(c1, "posv_col", [128, 16], F32, dr["posv_col"][:, :])]
        posc_b = [sb("posk_b", [128, 16], BF16), sb("posv_b", [128, 16], BF16)]

        csu = ExitStack()
        stage = [csu.enter_context(nc.sbuf_tensor("a_stage0", [128, 2048], F32)), csu.enter_context(nc.sbuf_tensor("a_stage1", [128, 2048], F32))]
        ppos = csu.enter_context(nc.psum_tensor("a_ppos", [128, 512], F32))
        load_cast(w1s, "w1s", [128, 8, NW1], dr["W1"], stage)
        load_cast(cw1[0], "cw1k", [128, 16, 256], dr["w1k"], stage)
        load_cast(cw1[1], "cw1v", [128, 16, 256], dr["w1v"], stage)
        load_cast(cw2[0], "cw2k", [128, 2, 64], dr["w2k"], stage)
        load_cast(cw2[1], "cw2v", [128, 2, 64], dr["w2v"], stage)
        S.barrier()
        for kv in range(2):
            S.op("dve", lambda e, kv=kv: e.tensor_copy(posc_b[kv][:], posc[kv][:]), writes=["posc_b%d" % kv])
            for hc in range(2):
                for lp in range(16):
                    S.op("pe", lambda e, kv=kv, hc=hc, lp=lp: e.matmul(ppos[:, kv * 2 + hc:kv * 2 + hc + 1], cw1[kv][:, lp, hc * 128:(hc + 1) * 128], posc_b[kv][:, lp:lp + 1], start=(lp == 0), stop=(lp == 15)),
                         reads=["posc_b%d" % kv], writes=["ppos"], inc=(lp == 15))
        S.op("dve", lambda e: e.tensor_copy(posb[:, :, :].rearrange("p a b -> p (a b)"), ppos[:, 0:4]), reads=["ppos"], writes=["posb"])
        S.op("pool", lambda e: e.memset(ones64[:], 1.0), writes=["ones64"])
        S.op("pool", lambda e: e.memset(kcT[:], 0.0), writes=["kcT"])
        S.op("pool", lambda e: e.memset(Vc[:], 0.0), writes=["Vc"])
        S.op("pool", lambda e: e.memset(Vc[:, :, :, 64:65], 1.0), writes=["Vc"])
        S.op("pool", lambda e: e.memset(Vs[:, :, :, 64:65], 1.0), writes=["Vs"])
        S.op("pool", lambda e: e.memset(Vw[:], 0.0), writes=["Vw"])
        S.op("pool", lambda e: e.memset(Vw[:, :, :, 64:65], 1.0), writes=["Vw"])
        S.op("pool", lambda e: e.memset(Kw[:], 0.0), writes=["Kw"])
        for kv in range(2):
            S.op("pool", lambda e, kv=kv: e.memset(Xs[kv][:], 0.0), writes=["X%d" % kv])
        for g in range(2):
            S.dma("sp", lambda e, g=g: e.dma_start(out=Kaug[g][64:128, :], in_=dr["Epat"][:, :]), writes=["Kaug%d" % g])
        S.barrier()
        S.emit()
        csu.close()

        xts = [sb("xt0", [128, 1024]), sb("xt1", [128, 1024])]
        xn_t = sb("xn", [128, 1024], BF16)
        Wf = {"junk": xn_t, "ssq": sb("ssq", [128, 4]), "xn": xn_t, "ptr": None}
        hT = sb("hT", [128, 8, 512], BF16)
        cs_t = sb("cs_t", [128, 512])
        sn_t = sb("sn_t", [128, 512])
        cmm_t = sb("cmm_t", [128, 512], BF16)
        tA = sb("tA", [128, 512])
        tB = sb("tB", [128, 512])
        tC = sb("tC", [128, 512])
        tD = sb("tD", [128, 512])
        gqb = sb("gqb", [128, 512], BF16)
        qrp = sb("qrp", [128, 512], BF16)
        qnp = sb("qnp", [128, 512], BF16)
        Qaug = sb("Qaug", [128, 2, 4, 512], BF16)
        qn = sb("qn", [128, 4, 512], BF16)
        gsb = sb("gsb", [32, 512])
        pTb = [sb("pTb%d" % j, [128, 512], BF16) for j in range(4)]
        zrow = sb("zrow", [128, 2, 512])
        rzbs = [sb("rzb0", [64, 512]), sb("rzb1", [64, 512])]
        tO = sb("tO", [64, 512])
        gbs = [sb("gb%d" % j, [64, 512]) for j in range(3)]
        accH = sb("accH", [64, 4, 512], BF16)
        impg = sb("impg", [128, 4, 128])
        impt = sb("impt", [128, 4, 128])
        rs = sb("rs", [128, 8])
        sc = sb("sc", [128, 128])
        sc2 = sb("sc2", [128, 128])
        m8 = sb("m8", [128, 16])
        selb = sb("selb", [128, 4, 2, 128], BF16)
        selT = sb("selT", [128, 2, 512], BF16)
        hidb = sb("hidb", [128, 2, 2, 32], BF16)
        kcn = sb("kcn", [32, 2, 64], BF16)
        kst = sb("kst", [32, 8])
        pproj = psb("pproj")
        Wf["ptr"] = pproj
        Wf["ptrn"] = "pproj"
        pmisc = psb("pmisc")
        psc = [psb("psc0"), psb("psc1"), psb("psc2")]
        pacc = [psb("pacc0"), psb("pacc1")]
        pimp = psb("pimp")
        cnt = {"sc": 0, "acc": 0, "pt": 0, "pp": 0, "ep": 0, "gb": 0}

        def norm_rope(pproj, ppn, gc, want_qn):
            S.op("act", lambda e: e.activation(tA[:], pproj[:, :], AF.Square), reads=[ppn], writes=["tA"])
            S.op("dve", lambda e: e.tensor_scalar(gqb[:], pproj[:, :], gc[:, 0:1], None, ALU.mult), reads=[ppn, "tA"], writes=["gqb"])
            S.op("pe", lambda e: e.matmul(pmisc[:, :], onesblk[:], tA[:], start=True, stop=True), reads=["tA"], writes=["pmisc"])
            S.op("act", lambda e: e.activation(tB[:], pmisc[:, :], AF.Sqrt, bias=EPS_AP[:, 0:1], scale=1.0 / 64), reads=["pmisc", "eps_ap"], writes=["tB"])
            S.op("dve", lambda e: e.reciprocal(tB[:], tB[:]), reads=["tB"], writes=["tB"])
            S.op("pe", lambda e: e.matmul(pmisc[:, :], Prot[:], gqb[:], start=True, stop=True), reads=["gqb"], writes=["pmisc"])
            S.op("dve", lambda e: e.tensor_tensor(tC[:], gqb[:], cs_t[:], ALU.mult), reads=["gqb", "cs_t"], writes=["tC"])
            S.op("dve", lambda e: e.tensor_tensor(tD[:], pmisc[:, :], sn_t[:], ALU.mult), reads=["pmisc", "sn_t"], writes=["tD"])
            S.op("dve", lambda e: e.tensor_tensor(tC[:], tC[:], tD[:], ALU.add), reads=["tC", "tD"], writes=["tC"])
            S.op("dve", lambda e: e.tensor_tensor(qrp[:], tC[:], tB[:], ALU.mult), reads=["tC", "tB"], writes=["qrp"])
            if want_qn:
                S.op("dve", lambda e: e.tensor_tensor(qnp[:], gqb[:], tB[:], ALU.mult), reads=["gqb", "tB"], writes=["qnp"])

        EPS_AP = sb("eps_ap", [128, 1])
        S.op("pool", lambda e: e.memset(EPS_AP[:], EPS), writes=["eps_ap"])

        pps = [(pproj, "pproj"), (pproj, "pproj")]

        def proj_tile(col0, ncols, hnames):
            pp, ppn = pps[cnt["pp"] % 2]
            cnt["pp"] += 1
            for kc in range(8):
                S.op("pe", lambda e, kc=kc: e.matmul(pp[0:ncols, :], w1s[:, kc, col0:col0 + ncols], hT[:, kc, :], start=(kc == 0), stop=(kc == 7)),
                     reads=hnames, writes=[ppn], inc=(kc == 7))
            return pp, ppn

        def next_sc():
            j = cnt["sc"] % 3
            cnt["sc"] += 1
            return psc[j], "psc%d" % j

        def next_pt():
            j = cnt["pt"] % 4
            cnt["pt"] += 1
            return pTb[j], "pTb%d" % j

        def epilogue(pa, pan, g, hh, br, first, clamp):
            h = 4 * g + hh
            k = 3 * h + br
            if clamp:
                S.op("dve", lambda e: e.tensor_scalar(zrow[64:65, :], pa[64:65, :], 1e-30, None, ALU.max), reads=[pan], writes=["zrow"])
                S.op("dve", lambda e: e.reciprocal(zrow[64:65, :], zrow[64:65, :]), reads=["zrow"], writes=["zrow"])
            else:
                S.op("dve", lambda e: e.reciprocal(zrow[64:65, :], pa[64:65, :]), reads=[pan], writes=["zrow"])
            S.op("pe", lambda e: e.matmul(pmisc[0:64, :], ones64[64:65, 0:64], zrow[64:65, :], start=True, stop=True), reads=["zrow"], writes=["pmisc"])
            S.op("act", lambda e: e.copy(rzb[:], pmisc[0:64, :]), reads=["pmisc"], writes=["rzb"])
            S.op("dve", lambda e: e.tensor_tensor(tO[:], pa[0:64, :], rzb[:], ALU.mult), reads=[pan, "rzb"], writes=["tO"])
            S.op("pe", lambda e, k=k: e.matmul(pmisc[0:64, :], Gsel[:, k, :], gsb[:, :], start=True, stop=True), reads=["gsb", "rzb"], writes=["pmisc"])
            if first:
                S.op("dve", lambda e: e.tensor_tensor(accA[:], pmisc[0:64, :], tO[:], ALU.mult), reads=["pmisc", "tO"], writes=["accA"])
            else:
                S.op("dve", lambda e: e.tensor_tensor(tO2[:], pmisc[0:64, :], tO[:], ALU.mult), reads=["pmisc", "tO"], writes=["tO2"])
                S.op("pool", lambda e: e.tensor_tensor(accA[:], accA[:], tO2[:], ALU.add), reads=["accA", "tO2"], writes=["accA"])

        for i in range(NCH_RUN):
            t0 = i * CH
            hnames = [("hT", "act"), ("hT", "dve")]
            if i == 0:
                front(0, [xts[tt % 2] for tt in range(4)], ["xt%d" % (tt % 2) for tt in range(4)], hT, "hT", Wf, ["dve"])
            S.dma("sp", lambda e, i=i: e.dma_start(out=cs_t[:], in_=dr["cosT"][i, :, :]), writes=["cs_t"])
            S.dma("sp", lambda e, i=i: e.dma_start(out=sn_t[:], in_=dr["sinS"][i, :, :]), writes=["sn_t"])
            S.dma("sp", lambda e, i=i: e.dma_start(out=cmm_t[:], in_=dr["cmpmask"][i % 4, :, :]), writes=["cmm_t"])
            pp, ppn = proj_tile(512, 128, hnames)
            norm_rope(pp, ppn, gcol[1], False)
            S.op("dve", lambda e, t0=t0: e.tensor_copy(Kaug[0][0:64, t0:t0 + CH], qrp[0:64, :]), reads=["qrp"], writes=["Kaug0"])
            S.op("dve", lambda e, t0=t0: e.tensor_copy(Kaug[1][0:64, t0:t0 + CH], qrp[64:128, :]), reads=["qrp"], writes=["Kaug1"])
            pp, ppn = proj_tile(640, 128, hnames)
            norm_rope(pp, ppn, gcol[2], False)
            w0 = (i % 2) * 512
            S.op("dve", lambda e, w0=w0: e.tensor_copy(Kw[0:64, 0, w0:w0 + CH], qrp[0:64, :]), reads=["qrp"], writes=["Kw"])
            S.op("dve", lambda e, w0=w0: e.tensor_copy(Kw[0:64, 1, w0:w0 + CH], qrp[64:128, :]), reads=["qrp"], writes=["Kw"])
            for kv in range(2):
                X = Xs[kv]
                xn_ = "X%d" % kv
                S.op("dve", lambda e, X=X: e.tensor_copy(X[:, :, 0:512], X[:, :, 512:1024]), reads=[xn_], writes=[xn_])
                pp, ppn = proj_tile(768 + 128 * kv, 128, hnames)
                S.op("act", lambda e, X=X, pp=pp: e.copy(X[0:64, 0, 512:1024], pp[0:64, :]), reads=[ppn], writes=[xn_])
                S.op("act", lambda e, X=X, pp=pp: e.copy(X[64:128, 0, 511:1023], pp[0:64, :]), reads=[ppn], writes=[xn_])
                S.op("act", lambda e, X=X, pp=pp: e.copy(X[0:64, 1, 512:1024], pp[64:128, :]), reads=[ppn], writes=[xn_])
                S.op("act", lambda e, X=X, pp=pp: e.copy(X[64:128, 1, 511:1023], pp[64:128, :]), reads=[ppn], writes=[xn_])
            pp, ppn = proj_tile(1024, 32, hnames)
            S.op("act", lambda e, pp=pp: e.activation(gsb[:, :], pp[0:32, :], AF.Sigmoid), reads=[ppn], writes=["gsb"])
            S.dma("sp", lambda e, i=i: e.dma_start(out=dr["_gscr"][i % 2, :, :], in_=gsb[:, :]), reads=["gsb"], writes=["gscr%d" % (i % 2)])
            for ts in range(4):
                kt = 4 * i + ts
                pp, ppn = pps[cnt["pp"] % 2]
                cnt["pp"] += 1
                for kc in range(8):
                    S.op("pe", lambda e, kc=kc, ts=ts, pp=pp: e.matmul(pp[:, 0:256], hT[:, kc, ts * 128:(ts + 1) * 128], w1s[:, kc, 1056:1312], start=(kc == 0), stop=(kc == 7)),
                         reads=hnames, writes=[ppn], inc=(kc == 7))
                S.op("act", lambda e, kt=kt, pp=pp: e.copy(Vs[:, kt, :, 0:64], pp[:, 0:128].rearrange("p (g d) -> p g d", g=2)), reads=[ppn], writes=["Vs"])
                S.op("act", lambda e, kt=kt, pp=pp: e.copy(Vw[:, kt % 8, :, 0:64], pp[:, 128:256].rearrange("p (g d) -> p g d", g=2)), reads=[ppn], writes=["Vw"])
            for kv in range(2):
                X = Xs[kv]
                xn_ = "X%d" % kv
                for q in ([i - 1, i] if i > 0 else [i]):
                    pos0 = 0 if q == i - 1 else 512
                    for g in range(2):
                        for hc in range(2):
                            c0 = (g * 2 + hc) * 32
                            for lp in range(16):
                                S.op("pe", lambda e, kv=kv, X=X, g=g, hc=hc, lp=lp, pos0=pos0, c0=c0: e.matmul(pmisc[:, c0:c0 + 32], cw1[kv][:, lp, hc * 128:(hc + 1) * 128], X[:, g, pos0 + 2 * lp:pos0 + 2 * lp + 512:16], start=(lp == 0), stop=(lp == 15)),
                                     reads=[xn_], writes=["pmisc"], inc=(lp == 15))
                    for hc in range(2):
                        S.op("act", lambda e, kv=kv, hc=hc: e.activation(hidb[:, :, hc, :], pmisc[:, 0:128].rearrange("p (g h n) -> p g h n", g=2, h=2)[:, :, hc, :], AF.Gelu, bias=posb[:, kv, hc:hc + 1]),
                             reads=["pmisc", "posb"], writes=["hidb"])
                    for g in range(2):
                        for hc in range(2):
                            S.op("pe", lambda e, kv=kv, g=g, hc=hc: e.matmul(pmisc[0:32, 128 + g * 64:128 + (g + 1) * 64], hidb[:, g, hc, :], cw2[kv][:, hc, :], start=(hc == 0), stop=(hc == 1)),
                                 reads=["hidb"], writes=["pmisc"], inc=(hc == 1))
                    cq = q // 4
                    pq = 32 * (q % 4)
                    if kv == 1:
                        S.op("act", lambda e, cq=cq, pq=pq: e.copy(Vc[pq:pq + 32, cq, :, 0:64], pmisc[0:32, 128:256].rearrange("p (g d) -> p g d", g=2)), reads=["pmisc"], writes=["Vc"])
                    else:
                        for g in range(2):
                            S.op("act", lambda e, g=g: e.activation(kcn[:, g, :], pmisc[0:32, 128 + g * 64:128 + (g + 1) * 64], AF.Square, accum_out=kst[:, g:g + 1]), reads=["pmisc"], writes=["kcn", "kst"])
                        S.op("dve", lambda e: e.tensor_scalar(kst[:, 2:4], kst[:, 0:2], 1.0 / 64, EPS, ALU.mult, ALU.add), reads=["kst"], writes=["kst"])
                        S.op("act", lambda e: e.activation(kst[:, 4:6], kst[:, 2:4], AF.Sqrt), reads=["kst"], writes=["kst"])
                        S.op("dve", lambda e: e.reciprocal(kst[:, 6:8], kst[:, 4:6]), reads=["kst"], writes=["kst"])
                        for g in range(2):
                            S.op("dve", lambda e, g=g: e.scalar_tensor_tensor(kcn[:, g, :], pmisc[0:32, 128 + g * 64:128 + (g + 1) * 64], kst[:, 6 + g:7 + g], gkc_bc[0:32, :], ALU.mult, ALU.mult),
                                 reads=["pmisc", "kst"], writes=["kcn"])
                        for g in range(2):
                            S.op("pe", lambda e, g=g: e.matmul(pmisc[0:64, 256 + g * 32:256 + (g + 1) * 32], kcn[:, g, :], ident_bf[0:32, 0:32], start=True, stop=True), reads=["kcn"], writes=["pmisc"], inc=(g == 1))
                        n0 = 32 * q
                        S.op("dve", lambda e, n0=n0: e.tensor_copy(kcT[0:64, :, n0:n0 + 32], pmisc[0:64, 256:320].rearrange("p (g n) -> p g n", g=2)), reads=["pmisc"], writes=["kcT"])
            for g in range(2):
                for j2 in range(2):
                    pp, ppn = proj_tile((2 * g + j2) * 128, 128, hnames)
                    norm_rope(pp, ppn, gcol[0], True)
                    for e2 in range(2):
                        hh = 2 * j2 + e2
                        rows = slice(64 * e2, 64 * e2 + 64)
                        S.op("dve", lambda e, hh=hh, rows=rows: e.tensor_copy(Qaug[0:64, 0, hh, :], qrp[rows, :]), reads=["qrp"], writes=["Qaug"])
                        S.op("dve", lambda e, hh=hh, rows=rows: e.tensor_copy(Qaug[0:64, 1, hh, :], qrp[rows, :]), reads=["qrp"], writes=["Qaug"])
                        S.op("dve", lambda e, hh=hh, rows=rows: e.tensor_copy(qn[0:64, hh, :], qnp[rows, :]), reads=["qnp"], writes=["qn"])
                cmax = i // 4

                def new_acc():
                    j = cnt["acc"] % 2
                    cnt["acc"] += 1
                    return pacc[j], "pacc%d" % j

                def mk_epilogue(pa, pan, g, hh, br, first, clamp, final, t0):
                    h = 4 * g + hh
                    k = 3 * h + br
                    an = "accH%d" % hh
                    ei = cnt["ep"]
                    cnt["ep"] += 1
                    zs = ei % 2
                    zsl = ei % 4
                    rzb, rzn = rzbs[zs], "rzb%d" % zs
                    gj = cnt["gb"] % 3
                    cnt["gb"] += 1
                    gb_, gbn = gbs[gj], "gb%d" % gj
                    isl = (t0 // CH) % 2

                    def s0():
                        S.dma("sp", lambda e: e.dma_start(out=gb_[:], in_=dr["_gscr"][isl, k:k + 1, :].partition_broadcast(64)), reads=["gscr%d" % isl], writes=[gbn])
                        if clamp:
                            S.op("dve", lambda e: e.tensor_scalar(zrow[64:65, zs, :], pa[64:65, :], 1e-30, None, ALU.max), reads=[pan], writes=["zrow%d" % zs])
                            S.op("dve", lambda e: e.reciprocal(zrow[64:65, zs, :], zrow[64:65, zs, :]), reads=["zrow%d" % zs], writes=["zrow%d" % zs])
                        else:
                            S.op("dve", lambda e: e.reciprocal(zrow[64:65, zs, :], pa[64:65, :]), reads=[pan], writes=["zrow%d" % zs])
                        S.dma("sp", lambda e: e.dma_start(out=dr["_zscr"][zsl:zsl + 1, :], in_=zrow[64:65, zs, :]), reads=["zrow%d" % zs], writes=["zscr%d" % zsl])
                        S.dma("sp", lambda e: e.dma_start(out=rzb[:], in_=dr["_zscr"][zsl:zsl + 1, :].partition_broadcast(64)), reads=["zscr%d" % zsl], writes=[rzn])

                    def s1():
                        S.op("dve", lambda e: e.tensor_tensor(tO[:], pa[0:64, :], rzb[:], ALU.mult), reads=[pan, rzn], writes=["tO"])
                        if first:
                            S.op("dve", lambda e: e.tensor_tensor(accH[:, hh, :], tO[:], gb_[:], ALU.mult), reads=["tO", gbn], writes=[an])
                        else:
                            S.op("dve", lambda e: e.tensor_tensor(tO[:], tO[:], gb_[:], ALU.mult), reads=["tO", gbn], writes=["tO"])
                            S.op("pool", lambda e: e.tensor_tensor(accH[:, hh, :], accH[:, hh, :], tO[:], ALU.add), reads=[an, "tO"], writes=[an])
                        if final:
                            S.dma("sp", lambda e: e.dma_start(out=attn_scr[h * 64:(h + 1) * 64, t0:t0 + CH], in_=accH[:, hh, :]), reads=[an], writes=["attn_scr"])
                    return [(0, s0), (2, s1)]

                def mk_imp_post(hh):
                    def f():
                        S.op("dve", lambda e: e.tensor_reduce(rs[:, 0:4], pimp[:, :].rearrange("p (s j) -> p s j", s=4), AX.X, ALU.add), reads=["pimp"], writes=["rs"])
                        S.op("dve", lambda e: e.tensor_scalar(rs[:, 4:8], rs[:, 0:4], 0.5, 1e-30, ALU.mult, ALU.max), reads=["rs"], writes=["rs"])
                        S.op("dve", lambda e: e.reciprocal(rs[:, 4:8], rs[:, 4:8]), reads=["rs"], writes=["rs"])
                        tgt = impg if hh == 0 else impt
                        tgn = "impg" if hh == 0 else "impt"
                        S.op("dve", lambda e: e.tensor_tensor(tgt[:], pimp[:, :].rearrange("p (s j) -> p s j", s=4), rs[:, 4:8].rearrange("p (s o) -> p s o", o=1).to_broadcast([128, 4, 128]), ALU.mult),
                             reads=["pimp", "rs"], writes=[tgn])
                        if hh > 0:
                            S.op("pool", lambda e: e.tensor_tensor(impg[:], impg[:], impt[:], ALU.add), reads=["impg", "impt"], writes=["impg"])
                    return f

                def mk_item(kind, g, hh, idx, npairs, pa, pan, arg, i):
                    st = {}

                    def score():
                        ps_, psn = next_sc()
                        pt, ptn = next_pt()
                        st["pt"], st["ptn"] = pt, ptn
                        if kind == "cmp":
                            c = arg
                            last = (c == npairs - 1)
                            S.op("pe", lambda e: e.matmul(ps_[:, :], kcT[0:64, g, c * 128:(c + 1) * 128], qn[0:64, hh, :], start=True, stop=(not last)), reads=["kcT", "qn"], writes=[psn], inc=(not last))
                            if last:
                                S.op("pe", lambda e: e.matmul(ps_[:, :], ident_bf[:], cmm_t[:], start=False, stop=True), reads=["cmm_t"], writes=[psn])
                        elif kind == "slc":
                            kt = arg
                            H = kt // 32
                            diag = kt >= 4 * i
                            S.op("pe", lambda e: e.matmul(ps_[:, :], Kaug[g][:, kt * 128:(kt + 1) * 128], Qaug[:, H, hh, :], start=True, stop=(not diag)), reads=["Kaug%d" % g, "Qaug"], writes=[psn], inc=(not diag))
                            if diag:
                                S.op("pe", lambda e: e.matmul(ps_[:, :], ident_bf[:], cmask[:, kt - 4 * i, :], start=False, stop=True), writes=[psn])
                        else:
                            kt = arg
                            sl = (kt % 8) * 128
                            mk = cmask[:, kt - 4 * i, :] if kt >= 4 * i else wlo[:, kt - 4 * i + 4, :]
                            S.op("pe", lambda e: e.matmul(ps_[:, :], Kw[0:64, g, sl:sl + 128], Qaug[0:64, 0, hh, :], start=True, stop=False), reads=["Kw", "Qaug"], writes=[psn], inc=False)
                            S.op("pe", lambda e: e.matmul(ps_[:, :], ident_bf[:], mk, start=False, stop=True), writes=[psn])
                        S.op("act", lambda e: e.activation(pt[:], ps_[:, :], AF.Exp, scale=0.125), reads=[psn], writes=[ptn])

                    def pv():
                        pt, ptn = st["pt"], st["ptn"]
                        first_ = (idx == 0)
                        last_ = (idx == npairs - 1)
                        if kind == "cmp":
                            c = arg
                            S.op("pe", lambda e: e.matmul(pa[0:65, :], Vc[:, c, g, :], pt[:], start=first_, stop=last_), reads=[ptn, "Vc"], writes=[pan])
                            for ts in range(4):
                                S.op("pe", lambda e, ts=ts: e.matmul(pimp[:, ts * 128:(ts + 1) * 128], pt[:, ts * 128:(ts + 1) * 128], ZA[:, c, :], start=(first_ and ts == 0), stop=(last_ and ts == 3), skip_group_check=True),
                                     reads=[ptn], writes=["pimp"], inc=(ts == 3))
                        elif kind == "slc":
                            kt = arg
                            S.op("pe", lambda e: e.matmul(pa[0:65, :], Vs[:, kt, g, :], pt[:], start=first_, stop=last_), reads=[ptn, "Vs"], writes=[pan])
                        else:
                            kt = arg
                            S.op("pe", lambda e: e.matmul(pa[0:65, :], Vw[:, kt % 8, g, :], pt[:], start=first_, stop=last_), reads=[ptn, "Vw"], writes=[pan])
                    return {"score": score, "pv": pv, "post": []}

                def run_stream(items, D=2):
                    pending = []
                    N = len(items)
                    for n in range(N + D):
                        if n < N:
                            items[n]["score"]()
                        still = []
                        for (due, fn) in pending:
                            if due <= n:
                                fn()
                            else:
                                still.append((due, fn))
                        pending = still
                        if n - D >= 0:
                            it = items[n - D]
                            it["pv"]()
                            for (dl, fn) in it["post"]:
                                if dl == 0:
                                    fn()
                                else:
                                    pending.append((n + dl, fn))
                    for (due, fn) in pending:
                        fn()

                items = []
                for hh in range(4):
                    pa, pan = new_acc()
                    for c in range(cmax + 1):
                        it = mk_item("cmp", g, hh, c, cmax + 1, pa, pan, c, i)
                        if c == cmax:
                            it["post"] = [(0, mk_imp_post(hh))] + mk_epilogue(pa, pan, g, hh, 0, True, True, False, t0)
                        items.append(it)
                run_stream(items)
                for ts in range(4):
                    tsg = 4 * i + ts
                    off = 126 - 2 * tsg
                    S.op("dve", lambda e, ts=ts, off=off: e.tensor_tensor(sc[:], impg[:, ts, :], Fadd[:, off:off + 128], ALU.add), reads=["impg"], writes=["sc"])
                    S.op("dve", lambda e: e.tensor_scalar(sc[:, 0:1], sc[:, 0:1], 1e4, None, ALU.add), reads=["sc"], writes=["sc"])
                    S.op("dve", lambda e: e.max(out=m8[:, 0:8], in_=sc[:]), reads=["sc"], writes=["m8"])
                    S.op("dve", lambda e: e.match_replace(out=sc2[:], in_to_replace=m8[:, 0:8], in_values=sc[:], imm_value=-3e4), reads=["sc", "m8"], writes=["sc2"])
                    S.op("dve", lambda e: e.max(out=m8[:, 8:16], in_=sc2[:]), reads=["sc2"], writes=["m8"])
                    S.op("dve", lambda e: e.tensor_scalar(sc2[:], sc[:], m8[:, 15:16], BIG, ALU.is_ge, ALU.mult), reads=["sc", "m8"], writes=["sc2"])
                    S.op("dve", lambda e, off=off: e.scalar_tensor_tensor(sc2[:], sc2[:], -BIG, Finv[:, off:off + 128], ALU.add, ALU.add), reads=["sc2"], writes=["sc2"])
                    S.op("dve", lambda e, ts=ts: e.tensor_copy(selb[:, ts, 0, :], sc2[:]), reads=["sc2"], writes=["selb"])
                    S.op("dve", lambda e, ts=ts: e.tensor_copy(selb[:, ts, 1, 0:64], sc2[:, 64:128]), reads=["sc2"], writes=["selb"])
                    S.op("dve", lambda e, ts=ts: e.tensor_copy(selb[:, ts, 1, 64:128], sc2[:, 0:64]), reads=["sc2"], writes=["selb"])
                items = []
                kts = [kt for kt in range(4 * i - 4, 4 * i + 4) if kt >= 0]
                for hh in range(4):
                    pa, pan = new_acc()
                    for idx, kt in enumerate(kts):
                        it = mk_item("win", g, hh, idx, len(kts), pa, pan, kt, i)
                        if idx == len(kts) - 1:
                            it["post"] = mk_epilogue(pa, pan, g, hh, 2, False, False, False, t0)
                        items.append(it)
                run_stream(items)
                for ts in range(4):
                    S.op("pe", lambda e, ts=ts: e.matmul(pmisc[:, 0:128], selb[:, ts, 1, :], ident_bf[:], start=True, stop=True), reads=["selb"], writes=["pmisc"], inc=False)
                    S.op("pe", lambda e, ts=ts: e.matmul(pmisc[:, 128:256], selb[:, ts, 0, :], ident_bf[:], start=True, stop=True), reads=["selb"], writes=["pmisc"])
                    S.op("dve", lambda e, ts=ts: e.tensor_copy(selT[64:128, :, ts * 128:(ts + 1) * 128], pmisc[64:128, 0:256].rearrange("p (h t) -> p h t", h=2)), reads=["pmisc"], writes=["selT"])
                for hh in range(4):
                    S.op("dve", lambda e, hh=hh: e.tensor_copy(Qaug[64:128, :, hh, :], selT[64:128, :, :]), reads=["selT"], writes=["Qaug"])
                items = []
                nkt = 4 * i + 4
                for hh in range(4):
                    pa, pan = new_acc()
                    for kt in range(nkt):
                        it = mk_item("slc", g, hh, kt, nkt, pa, pan, kt, i)
                        if kt == nkt - 1:
                            it["post"] = mk_epilogue(pa, pan, g, hh, 1, False, False, True, t0)
                        items.append(it)
                if g == 1 and i + 1 < NCH_RUN:
                    for tt in range(4):
                        pos = min(len(items) - 1, (tt * len(items)) // 4 + 1)
                        items[pos]["post"] = [(0, (lambda tt=tt, i=i: front(i + 1, [xts[t_ % 2] for t_ in range(4)], ["xt%d" % (t_ % 2) for t_ in range(4)], hT, "hT", Wf, ["dve"], tts=(tt,))))] + items[pos]["post"]
                run_stream(items)
        S.barrier()
        S.emit()


def pass2(nc, S, dr, out, attn_scr, dbg_out, front, load, load_cast, ident_bf, ident_f, gate_bc):
    PI = math.pi
    with ExitStack() as c2:
        def sb(name, shape, dt=F32):
            return c2.enter_context(nc.sbuf_tensor(name, list(shape), dt))

        def psb(name, shape=(128, 512), dt=F32):
            return c2.enter_context(nc.psum_tensor(name, list(shape), dt))

        csu = ExitStack()
        def sbt(name, shape, dt=F32):
            return csu.enter_context(nc.sbuf_tensor(name, list(shape), dt))
        w2 = sb("w2", [128, 8, 1536], BF16)
        wout = sb("wout", [128, 8, 1024], BF16)
        glw = sb("glw", [128, 4, 512], BF16)
        pm_f = load(c2, "pm_f", [128, 128], F32, dr["pm_f"][:, :])
        mrow = load(c2, "mrow", [128, 128], F32, dr["mrow"][:, :])
        are2 = load(c2, "are2", [128, 32], F32, dr["are2"][:, :])
        aim2 = load(c2, "aim2", [128, 32], F32, dr["aim2"][:, :])
        ldt2 = load(c2, "ldt2", [128, 32], F32, dr["ldt2"][:, :])
        cw_l = load(c2, "cw_l", [128, 32, 16], F32, dr["cw_l"][:, :, :])
        dcol = load(c2, "dcol", [128, 4], F32, dr["dcol"][:, :])
        glub = load(c2, "glub", [128, 4], F32, dr["glub_col"][:, :])

        Cm = sb("Cm", [128, 32, 128])
        Sm = sb("Sm", [128, 32, 128])
        rho = sb("rho", [128, 32])
        th = sb("th", [128, 32])
        dt2 = sb("dt2", [128, 32])
        c128 = sb("c128", [128, 32])
        s128 = sb("s128", [128, 32])
        BT = sb("BT", [128, 32, 128], BF16)
        BTs = sb("BTs", [128, 32, 128], BF16)
        Cw = sb("Cw", [128, 32, 16], BF16)
        carry = sb("carry", [128, 32])
        Rlast = sb("Rlast", [128, 32])

        stage = [sbt("stage0", [128, 2048]), sbt("stage1", [128, 2048])]
        _, w2n = load_cast(w2, "w2", [128, 8, 1536], dr["W2"], stage)
        _, woutn = load_cast(wout, "wout", [128, 8, 1024], dr["w_out"], stage)
        _, glwn = load_cast(glw, "glw", [128, 4, 512], dr["glu_w"], stage)
        tA = sbt("tA", [128, 1024])
        tB = sbt("tB", [128, 1024])
        tC = sbt("tC", [128, 1024])
        tD = sbt("tD", [128, 1024])
        tE = sbt("tE", [128, 1024])
        tF = sbt("tF", [128, 1024])
        tG = sbt("tG", [128, 1024])
        tI = sbt("tI", [128, 1024], I32)

        def sin_of(dst_ap, arg_ap, n, shift, names_r, name_w):
            a = tF[:, 0:n]
            S.op("dve", lambda e: e.tensor_scalar(a, arg_ap, 1.0, shift, ALU.mult, ALU.add), reads=names_r, writes=["tF"])
            S.op("dve", lambda e: e.tensor_scalar(tI[:, 0:n], a, 1.0 / (2 * PI), None, ALU.mult), reads=["tF"], writes=["tI"])
            S.op("dve", lambda e: e.tensor_copy(tG[:, 0:n], tI[:, 0:n]), reads=["tI"], writes=["tG"])
            S.op("dve", lambda e: e.scalar_tensor_tensor(tG[:, 0:n], tG[:, 0:n], -2 * PI, a, ALU.mult, ALU.add), reads=["tG", "tF"], writes=["tG"])
            S.op("dve", lambda e: e.tensor_scalar(tG[:, 0:n], tG[:, 0:n], 3.14159, -3.14159, ALU.min, ALU.max), reads=["tG"], writes=["tG"])
            S.op("act", lambda e: e.activation(dst_ap, tG[:, 0:n], AF.Sin), reads=["tG"], writes=[name_w])

        S.op("act", lambda e: e.activation(dt2[:], ldt2[:], AF.Exp), reads=["ldt2"], writes=["dt2"])
        S.op("dve", lambda e: e.tensor_tensor(th[:], aim2[:], dt2[:], ALU.mult), reads=["aim2", "dt2"], writes=["th"])
        S.op("dve", lambda e: e.tensor_tensor(rho[:], are2[:], dt2[:], ALU.mult), reads=["are2", "dt2"], writes=["rho"])
        S.op("act", lambda e: e.activation(rho[:], rho[:], AF.Exp), reads=["rho"], writes=["rho"])
        S.op("dve", lambda e: e.tensor_scalar(tA[:, 0:32], th[:], 128.0, None, ALU.mult), reads=["th"], writes=["tA"])
        sin_of(s128[:], tA[:, 0:32], 32, 0.0, ["tA"], "s128")
        sin_of(c128[:], tA[:, 0:32], 32, PI / 2, ["tA"], "c128")
        for gq in range(4):
            for gi in range(8):
                g = gq * 8 + gi
                S.op("dve", lambda e, g=g, gi=gi: e.tensor_scalar(tA[:, gi * 128:(gi + 1) * 128], mrow[:], th[:, g:g + 1], None, ALU.mult), reads=["mrow", "th"], writes=["tA"])
            sin_of(Sm[:, gq * 8:(gq + 1) * 8, :].rearrange("p g m -> p (g m)"), tA[:, 0:1024], 1024, 0.0, ["tA"], "Sm")
            sin_of(Cm[:, gq * 8:(gq + 1) * 8, :].rearrange("p g m -> p (g m)"), tA[:, 0:1024], 1024, PI / 2, ["tA"], "Cm")
        S.op("dve", lambda e: e.tensor_copy(Cw[0:64, :, :], cw_l[0:64, :, :]), reads=["cw_l"], writes=["Cw0"])
        S.op("dve", lambda e: e.tensor_scalar(Cw[64:128, :, :], cw_l[64:128, :, :], -1.0, None, ALU.mult), reads=["cw_l"], writes=["Cw1"])
        for gq in range(4):
            gsl = slice(gq * 8, (gq + 1) * 8)
            n = 512
            a_re_t, a_im_t, ldt_t, bre_t, bim_t = tA[:, 0:n], tA[:, n:2 * n], tB[:, 0:n], tB[:, n:2 * n], tC[:, 0:n]
            for (dst, nm, rn) in ((a_re_t, "b_are", "tA"), (a_im_t, "b_aim", "tA"), (ldt_t, "b_ldt", "tB"), (bre_t, "b_bre", "tB"), (bim_t, "b_bim", "tC")):
                S.dma("sp", lambda e, dst=dst, nm=nm, gsl=gsl: e.dma_start(out=dst.rearrange("p (g m) -> p g m", g=8), in_=dr[nm][:, gsl, :]), writes=[rn])
            dtt, rl, tht, ee = tC[:, n:2 * n], tD[:, 0:n], tD[:, n:2 * n], tE[:, 0:n]
            sn, cs = tE[:, n:2 * n], tC[:, n:2 * n]
            S.op("act", lambda e: e.activation(dtt, ldt_t, AF.Exp), reads=["tB"], writes=["tC"])
            S.op("dve", lambda e: e.tensor_tensor(rl, a_re_t, dtt, ALU.mult), reads=["tA", "tC"], writes=["tD"])
            S.op("dve", lambda e: e.tensor_tensor(tht, a_im_t, dtt, ALU.mult), reads=["tA", "tC"], writes=["tD"])
            S.op("act", lambda e: e.activation(ee, rl, AF.Exp), reads=["tD"], writes=["tE"])
            sin_of(sn, tht, n, 0.0, ["tD"], "tE")
            sin_of(cs, tht, n, PI / 2, ["tD"], "tC")
            lbr1, lbi = rl, tht
            S.op("dve", lambda e: e.tensor_tensor(lbi, ee, sn, ALU.mult), reads=["tE"], writes=["tD"])
            S.op("dve", lambda e: e.tensor_tensor(lbr1, ee, cs, ALU.mult), reads=["tE", "tC"], writes=["tD"])
            S.op("dve", lambda e: e.tensor_scalar(lbr1, lbr1, -1.0, None, ALU.add), reads=["tD"], writes=["tD"])
            den = ee
            S.op("dve", lambda e: e.tensor_tensor(den, a_re_t, a_re_t, ALU.mult), reads=["tA", "tE"], writes=["tE"])
            S.op("dve", lambda e: e.tensor_tensor(sn, a_im_t, a_im_t, ALU.mult), reads=["tA", "tE"], writes=["tE"])
            S.op("dve", lambda e: e.tensor_tensor(den, den, sn, ALU.add), reads=["tE"], writes=["tE"])
            S.op("dve", lambda e: e.reciprocal(den, den), reads=["tE"], writes=["tE"])
            fr, fi, tmp = sn, cs, ldt_t
            S.op("dve", lambda e: e.tensor_tensor(fr, lbr1, a_re_t, ALU.mult), reads=["tD", "tA"], writes=["tE"])
            S.op("dve", lambda e: e.tensor_tensor(tmp, lbi, a_im_t, ALU.mult), reads=["tD", "tA"], writes=["tB"])
            S.op("dve", lambda e: e.tensor_tensor(fr, fr, tmp, ALU.add), reads=["tE", "tB"], writes=["tE"])
            S.op("dve", lambda e: e.tensor_tensor(fr, fr, den, ALU.mult), reads=["tE"], writes=["tE"])
            S.op("dve", lambda e: e.tensor_tensor(fi, lbi, a_re_t, ALU.mult), reads=["tD", "tA"], writes=["tC"])
            S.op("dve", lambda e: e.tensor_tensor(tmp, lbr1, a_im_t, ALU.mult), reads=["tD", "tA"], writes=["tB"])
            S.op("dve", lambda e: e.tensor_tensor(fi, fi, tmp, ALU.subtract), reads=["tC", "tB"], writes=["tC"])
            S.op("dve", lambda e: e.tensor_tensor(fi, fi, den, ALU.mult), reads=["tC", "tE"], writes=["tC"])
            br_, bi_ = lbr1, lbi
            S.op("dve", lambda e: e.tensor_tensor(br_, fr, bre_t, ALU.mult), reads=["tE", "tB"], writes=["tD"])
            S.op("dve", lambda e: e.tensor_tensor(tmp, fi, bim_t, ALU.mult), reads=["tC"], writes=["tB"])
            S.op("dve", lambda e: e.tensor_tensor(br_, br_, tmp, ALU.subtract), reads=["tD", "tB"], writes=["tD"])
            S.op("dve", lambda e: e.tensor_tensor(bi_, fr, bim_t, ALU.mult), reads=["tE", "tC"], writes=["tD"])
            S.op("dve", lambda e: e.tensor_tensor(tmp, fi, bre_t, ALU.mult), reads=["tC", "tB"], writes=["tB"])
            S.op("dve", lambda e: e.tensor_tensor(bi_, bi_, tmp, ALU.add), reads=["tD", "tB"], writes=["tD"])
            v3 = lambda ap: ap.rearrange("p (g m) -> p g m", g=8)
            S.op("dve", lambda e, gsl=gsl: e.tensor_copy(BT[:, gsl, 0:64], v3(br_)), reads=["tD"], writes=["BT"])
            S.op("dve", lambda e, gsl=gsl: e.tensor_copy(BT[:, gsl, 64:128], v3(bi_)), reads=["tD"], writes=["BT"])
            S.op("dve", lambda e, gsl=gsl: e.tensor_copy(BTs[:, gsl, 0:64], v3(bi_)), reads=["tD"], writes=["BTs"])
            S.op("dve", lambda e, gsl=gsl: e.tensor_scalar(BTs[:, gsl, 64:128], v3(br_), -1.0, None, ALU.mult), reads=["tD"], writes=["BTs"])
        S.op("dve", lambda e: e.memset(carry[:], 0.0), writes=["carry"])
        S.barrier()
        S.emit()
        csu.close()

        xts = [sb("xt%d" % i, [128, 1024]) for i in range(2)]
        xrs = [sb("xr%d" % i, [128, 1024]) for i in range(1)]
        xn_t = sb("xn", [128, 1024], BF16)
        Wf = {"junk": xn_t, "ssq": sb("ssq", [128, 4]), "xn": xn_t,
              "ptr": None}
        hTs = [sb("hT%d" % j, [128, 8, 512], BF16) for j in range(2)]
        uT_bfs = [sb("uT_bf%d" % j, [128, 4, 512], BF16) for j in range(2)]
        uT_fs = uT_bfs
        szss = [sb("szs%d" % j, [128, 4, 512], BF16) for j in range(2)]
        szas = [sb("sza%d" % j, [128, 4, 512], BF16) for j in range(2)]
        bufA = [sb("bufA%d" % j, [128, 256]) for j in range(2)]
        bufB = [sb("bufB%d" % j, [128, 256]) for j in range(2)]
        bB16 = [sb("bB16_%d" % j, [128, 256], BF16) for j in range(2)]
        bC16 = [sb("bC16_%d" % j, [128, 256], BF16) for j in range(2)]
        bufC = [None, None]
        Rts = [sb("Rt%d" % j, [128, 2, 128]) for j in range(2)]
        Rtbs = [sb("Rtb%d" % j, [128, 256], BF16) for j in range(2)]
        pm_b = sb("pm_b", [128, 128], BF16)
        S.op("dve", lambda e: e.tensor_copy(pm_b[:], pm_f[:]), reads=["pm_f"], writes=["pm_b"])
        ysb = sb("ysb", [128, 512])
        y2 = sb("y2", [128, 4, 128])
        gy_f = sb("gy_f", [128, 4, 512])
        gy_b = sb("gy_b", [128, 4, 512], BF16)
        sig = sb("sig", [128, 512])
        m1 = sb("m1", [128, 512])
        mix = sb("mix", [128, 8, 512], BF16)
        res = sb("res", [128, 1024])
        ct1 = sb("ct1", [128, 32])
        ct2 = sb("ct2", [128, 32])
        pproj = psb("pproj")
        Wf["ptr"] = pproj
        Wf["ptrn"] = "pproj"
        pVW = [psb("pVW0"), psb("pVW1")]
        pRs = [psb("pR0"), psb("pR1")]
        pY = psb("pY")
        pT = psb("pT")
        pO = [psb("pO0"), pproj]
        pOn = ["pO0", "pproj"]

        if not WITH_ATTN:
            S.op("pool", lambda e: e.memset(mix[:, 0:4, :], 0.0), writes=["mix_attn"])

        def make_units(i):
            cp = i % 2
            hT, uT_bf, uT_f, szs, sza = hTs[cp], uT_bfs[cp], uT_fs[cp], szss[cp], szas[cp]
            hn = "hT%d" % cp
            xs_ = [xts[tt % 2] for tt in range(4)]
            xn_ = ["xt%d" % (tt % 2) for tt in range(4)]
            hnames = [(hn, "act"), (hn, "dve")]
            units = []
            for tt in range(4):
                units.append(lambda tt=tt: front(i, xs_, xn_, hT, hn, Wf, ["act"], tts=(tt,)))

            def proj_unit(ct):
                for kc in range(8):
                    S.op("pe", lambda e, kc=kc: e.matmul(pproj[:, :], w2[:, kc, ct * 128:(ct + 1) * 128], hT[:, kc, :], start=(kc == 0), stop=(kc == 7)),
                         reads=hnames, writes=["pproj"], inc=(kc == 7))
                if ct < 4:
                    S.op("act", lambda e: e.copy(uT_bf[:, ct, :], pproj[:, :]), reads=["pproj"], writes=["uT_bf%d" % cp])
                elif ct < 8:
                    S.op("act", lambda e: e.activation(szs[:, ct - 4, :], pproj[:, :], AF.Silu), reads=["pproj"], writes=["szs%d" % cp])
                else:
                    S.op("act", lambda e: e.activation(sza[:, ct - 8, :], pproj[:, :], AF.Silu), reads=["pproj"], writes=["sza%d" % cp])
            for ct in range(12):
                units.append(lambda ct=ct: proj_unit(ct))
            return units

        for u_ in make_units(0):
            u_()
        for i in range(NCH_RUN):
            t0 = i * CH
            cp = i % 2
            hT, uT_bf, uT_f, szs, sza = hTs[cp], uT_bfs[cp], uT_fs[cp], szss[cp], szas[cp]
            uTbn, uTfn, szsn, szan = "uT_bf%d" % cp, "uT_bf%d" % cp, "szs%d" % cp, "sza%d" % cp
            units = make_units(i + 1) if i + 1 < NCH_RUN else []
            if STAGE <= 2 or STAGE in (20, 21, 22, 23):
                continue
            if WITH_ATTN:
                av = attn_scr.rearrange("(kc p) t -> p kc t", p=128)
                S.dma("sp", lambda e, t0=t0: e.dma_start(out=mix[:, 0:4, :], in_=av[:, :, t0:t0 + CH]), reads=["attn_scr"], writes=["mix_attn"])
                S.op("pool", lambda e, sza=sza: e.tensor_tensor(mix[:, 0:4, :], mix[:, 0:4, :], sza[:, :, :], ALU.mult), reads=["mix_attn", szan], writes=["mix_attn"])
            for f in range(4):
                fl = f * 128

                def stageA1(gb, fl=fl, uT_bf=uT_bf):
                    par = gb % 2
                    po = par * 256
                    g0 = gb * 2
                    An, Bn = "bA%d" % par, "bB%d" % par
                    pVn, pWn = "pV%d" % par, "pW%d" % par
                    bA, bB = bufA[par], bufB[par]
                    for gi in range(2):
                        g = g0 + gi
                        ctg = g // 8
                        base = 64 * ((g % 8) // 4)
                        S.op("pe", lambda e, g=g, gi=gi, ctg=ctg, base=base: e.matmul(pVW[par][:, gi * 128:(gi + 1) * 128], BT[base:base + 64, g, :], uT_bf[base:base + 64, ctg, fl:fl + 128], start=True, stop=True),
                             reads=[uTbn], writes=[pVn], inc=(gi == 1))
                    for gi in range(2):
                        g = g0 + gi
                        ctg = g // 8
                        base = 64 * ((g % 8) // 4)
                        S.op("pe", lambda e, g=g, gi=gi, ctg=ctg, base=base: e.matmul(pVW[par][:, 256 + gi * 128:256 + (gi + 1) * 128], BTs[base:base + 64, g, :], uT_bf[base:base + 64, ctg, fl:fl + 128], start=True, stop=True),
                             reads=[uTbn], writes=[pWn], inc=(gi == 1))
                    cmv = Cm[:, g0:g0 + 2, :].rearrange("p g m -> p (g m)")
                    smv = Sm[:, g0:g0 + 2, :].rearrange("p g m -> p (g m)")
                    S.op("dve", lambda e: e.tensor_tensor(bA[:], pVW[par][:, 0:256], cmv, ALU.mult), reads=[pVn, pWn], writes=[An])
                    S.op("act", lambda e: e.copy(bB[:], pVW[par][:, 256:512]), reads=[pVn, pWn, An], writes=[Bn])
                    S.op("pool", lambda e: e.tensor_tensor(bB[:], bB[:], smv, ALU.mult), reads=[Bn], writes=[Bn])
                    S.op("pool", lambda e: e.tensor_tensor(bA[:], bA[:], bB[:], ALU.add), reads=[An, Bn], writes=[An])

                def stageA2(gb):
                    par = gb % 2
                    g0 = gb * 2
                    An, Rn, Rbn = "bA%d" % par, "Rt%d" % par, "Rtb%d" % par
                    bA, Rt_, Rtb_ = bufA[par], Rts[par], Rtbs[par]
                    for gi in range(2):
                        g = g0 + gi
                        S.op("dve", lambda e, g=g, gi=gi: e.tensor_tensor_scan(Rt_[:, gi, :], rho[:, g:g + 1].to_broadcast([128, 128]), bA[:, gi * 128:(gi + 1) * 128], carry[:, g:g + 1], ALU.mult, ALU.add),
                             reads=[An, "carry"], writes=[Rn])
                    S.op("act", lambda e: e.copy(Rlast[:, g0:g0 + 2], Rt_[:, :, 127]), reads=[Rn], writes=["Rlast"])
                    rtv = Rt_[:, :, :].rearrange("p g m -> p (g m)")
                    S.op("act", lambda e: e.copy(Rtb_[:], rtv), reads=[Rn], writes=[Rbn])

                def stageB(gb):
                    par = gb % 2
                    po = par * 256
                    g0 = gb * 2
                    Rn, Rbn, pRn = "Rt%d" % par, "Rtb%d" % par, "pR%d" % par
                    Rt_, Rtb_ = Rts[par], Rtbs[par]
                    cmv = Cm[:, g0:g0 + 2, :].rearrange("p g m -> p (g m)")
                    smv = Sm[:, g0:g0 + 2, :].rearrange("p g m -> p (g m)")
                    rtv = Rt_[:, :, :].rearrange("p g m -> p (g m)")
                    S.op("pe", lambda e: e.matmul(pRs[par][:, 0:256], pm_b[:], Rtb_[:], start=True, stop=True), reads=[Rbn], writes=[pRn])
                    S.op("pool", lambda e: e.tensor_tensor(bB16[par][:], rtv, cmv, ALU.mult), reads=[Rn], writes=["bB16_%d" % par])
                    S.op("dve", lambda e: e.tensor_tensor(bC16[par][:], pRs[par][:, 0:256], smv, ALU.mult), reads=[pRn], writes=["bC16_%d" % par])
                    for gi in range(2):
                        g = g0 + gi
                        S.op("pe", lambda e, g=g, gi=gi: e.matmul(pY[:, g * 16:(g + 1) * 16], bB16[par][:, gi * 128:(gi + 1) * 128], Cw[:, g, :], start=True, stop=False),
                             reads=["bB16_%d" % par], writes=["pY"], inc=False)
                        S.op("pe", lambda e, g=g, gi=gi: e.matmul(pY[:, g * 16:(g + 1) * 16], bC16[par][:, gi * 128:(gi + 1) * 128], Cw[:, g, :], start=False, stop=True),
                             reads=["bC16_%d" % par], writes=["pY"], inc=(gi == 1))

                NB = 16
                stageA1(0)
                stageA2(0)
                for gb in range(NB):
                    if gb + 1 < NB:
                        stageA1(gb + 1)
                    stageB(gb)
                    if gb + 1 < NB:
                        stageA2(gb + 1)
                    if gb % 4 == 3 and units:
                        units.pop(0)()
                S.op("pe", lambda e: e.matmul(pRs[0][:, 0:32], pm_f[:], Rlast[:], start=True, stop=True), reads=["pm_f", "Rlast"], writes=["pR0"])
                S.op("dve", lambda e: e.tensor_tensor(ct1[:], Rlast[:], c128[:], ALU.mult), reads=["Rlast", "c128"], writes=["ct1"])
                S.op("dve", lambda e: e.tensor_tensor(ct2[:], pRs[0][:, 0:32], s128[:], ALU.mult), reads=["pR0", "s128"], writes=["ct2"])
                S.op("dve", lambda e: e.tensor_tensor(carry[:], ct1[:], ct2[:], ALU.add), reads=["ct1", "ct2"], writes=["carry"])
                S.op("act", lambda e: e.copy(ysb[:], pY[:, :]), reads=["pY"], writes=["ysb"])
                for ct in range(4):
                    S.op("pe", lambda e, ct=ct: e.transpose(pT[:, ct * 128:(ct + 1) * 128], ysb[:, ct * 128:(ct + 1) * 128], ident_f[:]), reads=["ysb", "ident_f"], writes=["pT"], inc=(ct == 3))
                for ct in range(4):
                    S.op("dve", lambda e, ct=ct, fl=fl, uT_f=uT_f: e.scalar_tensor_tensor(y2[:, ct, :], uT_f[:, ct, fl:fl + 128], dcol[:, ct:ct + 1], pT[:, ct * 128:(ct + 1) * 128], ALU.mult, ALU.add),
                         reads=[uTfn, "dcol", "pT"], writes=["y2"])
                S.op("act", lambda e, fl=fl: e.activation(gy_f[:, :, fl:fl + 128], y2[:, :, :], AF.Gelu), reads=["y2"], writes=["gy_f"])
                S.op("act", lambda e, fl=fl: e.copy(gy_b[:, :, fl:fl + 128], gy_f[:, :, fl:fl + 128]), reads=["gy_f"], writes=["gy_b"])
            while units:
                units.pop(0)()
            for co in range(4):
                for ct in range(4):
                    S.op("pe", lambda e, co=co, ct=ct: e.matmul(pT[:, :], glw[:, ct, co * 128:(co + 1) * 128], gy_b[:, ct, :], start=(ct == 0), stop=(ct == 3)),
                         reads=glwn + ["gy_b"], writes=["pT"], inc=(ct == 3))
                S.op("act", lambda e, co=co: e.activation(sig[:], pT[:, :], AF.Sigmoid, bias=glub[:, co:co + 1]), reads=["pT", "glub"], writes=["sig"])
                S.op("dve", lambda e, co=co: e.tensor_tensor(m1[:], gy_f[:, co, :], sig[:], ALU.mult), reads=["gy_f", "sig"], writes=["m1"])
                S.op("pool", lambda e, co=co, szs=szs: e.tensor_tensor(mix[:, 4 + co, :], m1[:], szs[:, co, :], ALU.mult), reads=["m1", szsn], writes=["mix_ssm"])
            if STAGE <= 4:
                continue
            for tt in range(4):
                gt = 4 * i + tt
                for hf in range(2):
                    for kc in range(8):
                        S.op("pe", lambda e, tt=tt, hf=hf, kc=kc: e.matmul(pO[hf][:, :], mix[:, kc, tt * 128:(tt + 1) * 128], wout[:, kc, hf * 512:(hf + 1) * 512], start=(kc == 0), stop=(kc == 7)),
                             reads=woutn + ["mix_attn", "mix_ssm"], writes=[pOn[hf]], inc=(kc == 7))
                    S.op("dve", lambda e, hf=hf: e.tensor_tensor(res[:, hf * 512:(hf + 1) * 512], pO[hf][:, :], gate_bc[:, hf * 512:(hf + 1) * 512], ALU.mult),
                         reads=[pOn[hf], "gate_bc"], writes=["res%d" % hf])
                xr = xrs[0]
                xrn = "xr0"
                S.dma("sp", lambda e, xr=xr, gt=gt: e.dma_start(out=xr[:], in_=dr["x"][gt * 128:(gt + 1) * 128, :]), writes=[xrn])
                S.op("dve", lambda e, xr=xr: e.tensor_tensor(res[:], res[:], xr[:], ALU.add), reads=["res0", "res1", xrn], writes=["res0", "res1"])
                S.dma("sp", lambda e, gt=gt: e.dma_start(out=out[gt * 128:(gt + 1) * 128, :], in_=res[:]), reads=["res0", "res1"], writes=["out"])
        S.barrier()
        S.emit()


_CACHE = {}


def kernel(**inputs):
    consts = make_consts()
    params = make_params(inputs)
    specs = input_specs(consts, params)
    nc = build(specs)
    x = np.asarray(inputs["x"], np.float32)
    c = np.asarray(inputs["c"], np.float32)
    in_maps = []
    for b in range(8):
        m = {"x": np.ascontiguousarray(x[b]), "c_col": np.ascontiguousarray(c[b].reshape(8, 128).T)}
        m.update(consts)
        m.update(params)
        in_maps.append(m)
    res = run_bass_kernel_spmd(nc, in_maps, core_ids=list(range(8)))
    return np.stack([np.asarray(r["out"], np.float32) for r in res.results], 0)
```

```python
import math
import numpy as np
import ml_dtypes
from contextlib import ExitStack
import concourse.bass as bass
import concourse.mybir as mybir
from concourse.bass_utils import run_bass_kernel_spmd

F32 = mybir.dt.float32
BF16 = mybir.dt.bfloat16
I32 = mybir.dt.int32
AF = mybir.ActivationFunctionType
ALU = mybir.AluOpType
AX = mybir.AxisListType

T = 8192
D = 1024
CH = 512
NCH = T // CH
EPS = 1e-6
BIG = 30000.0
NW1 = 1312
WITH_ATTN = True
WITH_SSM = True
NCH_RUN = NCH
STAGE = 99

ENGS = ["pe", "act", "dve", "pool", "sp"]


class Sched:
    def __init__(self, nc, ctx, n_dma_sems=12):
        self.nc = nc
        self.prog = {e: [] for e in ENGS}
        self.sem = {e: ctx.enter_context(nc.semaphore("s_" + e)) for e in ENGS}
        self.cnt = {e: 0 for e in ENGS}
        self.seen = {e: {} for e in ENGS}
        self.last_w = {}
        self.readers = {}
        self.dsem, self.dcnt, self.dnext = {}, {}, {}
        for q in ["sp", "act"]:
            self.dsem[q] = [ctx.enter_context(nc.semaphore("d_%s%d" % (q, i))) for i in range(n_dma_sems)]
            self.dcnt[q] = [0] * n_dma_sems
            self.dnext[q] = 0
        self.semobj = {}
        for e in ENGS:
            self.semobj[("e", e)] = self.sem[e]
        for q in self.dsem:
            for i, s in enumerate(self.dsem[q]):
                self.semobj[("d", q, i)] = s

    def _waits_for(self, eng, toks):
        need = {}
        for (k, v) in toks:
            if k == ("e", eng) and (v > self.cnt[eng] or eng == "pe"):
                continue
            if self.seen[eng].get(k, 0) < v:
                need[k] = max(need.get(k, 0), v)
        for k, v in need.items():
            self.seen[eng][k] = v
        return list(need.items())

    def _deps(self, reads, writes):
        toks = []
        for r in reads:
            t = self.last_w.get(r)
            if t is not None:
                toks.append(t)
        for w in writes:
            t = self.last_w.get(w)
            if t is not None:
                toks.append(t)
            toks.extend(self.readers.get(w, []))
        return toks

    def _commit(self, tok, reads, writes):
        for r in reads:
            self.readers.setdefault(r, []).append(tok)
        for w in writes:
            self.last_w[w] = tok
            self.readers[w] = []

    def op(self, eng, fn, reads=(), writes=(), inc=True):
        toks = self._deps(reads, writes)
        waits = self._waits_for(eng, toks)
        if inc:
            self.cnt[eng] += 1
            tok = (("e", eng), self.cnt[eng])
        else:
            tok = (("e", eng), self.cnt[eng] + 1)
        self.prog[eng].append((waits, fn, ("e", eng) if inc else None, 1))
        self._commit(tok, reads, writes)
        return tok

    def dma(self, q, fn, reads=(), writes=()):
        toks = self._deps(reads, writes)
        i = self.dnext[q]
        self.dnext[q] = (i + 1) % len(self.dsem[q])
        key = ("d", q, i)
        if self.dcnt[q][i] > 0:
            toks.append((key, 16 * self.dcnt[q][i]))
        waits = self._waits_for(q, toks)
        self.dcnt[q][i] += 1
        tok = (key, 16 * self.dcnt[q][i])
        self.prog[q].append((waits, fn, key, 16))
        self._commit(tok, reads, writes)
        return tok

    def final_wait(self, eng, toks):
        waits = self._waits_for(eng, toks)
        self.prog[eng].append((waits, None, None, 0))

    def barrier(self):
        toks = [(("e", e), self.cnt[e]) for e in ENGS if self.cnt[e] > 0]
        for q in self.dsem:
            for i in range(len(self.dsem[q])):
                if self.dcnt[q][i] > 0:
                    toks.append((("d", q, i), 16 * self.dcnt[q][i]))
        for e in ENGS:
            self.final_wait(e, toks)

    def emit(self):
        nc = self.nc
        prog = self.prog
        semobj = self.semobj

        def run(e_obj, lst):
            for waits, fn, key, amt in lst:
                for k, v in waits:
                    e_obj.wait_ge(semobj[k], v)
                if fn is None:
                    continue
                ins = fn(e_obj)
                if key is not None:
                    ins.then_inc(semobj[key], amt)

        with nc.Block() as block:
            @block.tensor
            def _(e):
                run(e, prog["pe"])

            @block.scalar
            def _(e):
                run(e, prog["act"])

            @block.vector
            def _(e):
                run(e, prog["dve"])

            @block.gpsimd
            def _(e):
                run(e, prog["pool"])

            @block.sync
            def _(e):
                run(e, prog["sp"])
        self.prog = {e: [] for e in ENGS}


def _bf(a):
    return np.ascontiguousarray(a).astype(ml_dtypes.bfloat16)


def make_consts():
    c = {}
    c["ident_bf"] = _bf(np.eye(128, dtype=np.float32))
    c["ident_f"] = np.eye(128, dtype=np.float32)
    ob = np.zeros((128, 128), np.float32)
    ob[:64, :64] = 1.0
    ob[64:, 64:] = 1.0
    c["onesblk_f"] = ob
    pm = np.zeros((128, 128), np.float32)
    for j in range(64):
        pm[64 + j, j] = -1.0
        pm[j, 64 + j] = 1.0
    c["pm_f"] = pm
    c["mrow"] = np.tile(np.arange(128, dtype=np.float32)[None, :], (128, 1))
    inv = 1.0 / (10000.0 ** (np.arange(0, 64, 2, dtype=np.float32) / 64.0))
    t = np.arange(T, dtype=np.float32)
    ang = t[None, :] * inv[:, None].astype(np.float32)
    cos32 = np.cos(ang).astype(np.float32)
    sin32 = np.sin(ang).astype(np.float32)
    cos64 = np.concatenate([cos32, cos32], 0)
    sin64 = np.concatenate([-sin32, sin32], 0)
    cosT = np.concatenate([cos64, cos64], 0)
    sinS = np.concatenate([sin64, sin64], 0)
    c["cosT"] = np.ascontiguousarray(cosT.reshape(128, NCH, CH).transpose(1, 0, 2))
    c["sinS"] = np.ascontiguousarray(sinS.reshape(128, NCH, CH).transpose(1, 0, 2))
    kl = np.arange(128)[:, None]
    tl = np.arange(CH)[None, :]
    cm = np.zeros((4, 128, CH), np.float32)
    wl = np.zeros((4, 128, CH), np.float32)
    pmk = np.zeros((4, 128, CH), np.float32)
    for v in range(4):
        cm[v] = np.where(128 * v + kl > tl, -BIG, 0.0)
        wl[v] = np.where(128 * v + kl <= tl, -BIG, 0.0)
        pmk[v] = np.where(16 * kl + 31 > 512 * v + tl, -BIG, 0.0)
    c["cmask"] = _bf(cm)
    c["wlo"] = _bf(wl)
    c["cmpmask"] = _bf(pmk)
    A = np.zeros((512, 128), np.float32)
    for j in range(128):
        for m in range(4):
            for n in range(2):
                idx = 4 * j + m - n
                if 0 <= idx < 511:
                    A[idx, j] += 1.0
    c["ZA"] = _bf(A.reshape(4, 128, 128))
    r = np.arange(128)[:, None]
    rel = np.arange(256)[None, :] - 126
    cur = r // 64
    fadd = np.zeros((128, 256), np.float32)
    fadd = np.where(rel > cur, -1e4, fadd)
    fadd = np.where((rel == cur) | (rel == cur - 1), 1e4, fadd)
    c["Fadd"] = fadd.astype(np.float32)
    c["Finv"] = np.where(rel > cur, -BIG, 0.0).astype(np.float32)
    key = np.arange(T)[None, :]
    jj = np.arange(64)[:, None]
    c["Epat"] = _bf(((key // 64) % 64 == jj).astype(np.float32))
    gs = np.zeros((32, 24, 64), np.float32)
    for k in range(24):
        gs[k, k, :] = 1.0
    c["Gsel"] = gs
    c["ones_row"] = np.ones((1, 128), np.float32)
    pr = np.zeros((128, 128), np.float32)
    for e in range(2):
        for d in range(64):
            pr[e * 64 + (d + 32) % 64, e * 64 + d] = 1.0
    c["Prot"] = _bf(pr)
    return c


def make_params(inp):
    p = {}
    f = lambda a: np.ascontiguousarray(np.asarray(a, dtype=np.float32))
    l = 0
    w_in = f(inp["w_in"][l])
    q = w_in[:, 0:512]
    kcr = w_in[:, 512:640]
    vcr = w_in[:, 640:768]
    ksl = w_in[:, 768:896]
    vsl = w_in[:, 896:1024]
    kwn = w_in[:, 1024:1152]
    vwn = w_in[:, 1152:1280]
    z_a = w_in[:, 1280:1792]
    g_br = w_in[:, 1792:1816]
    u = w_in[:, 1816:2328]
    z_s = w_in[:, 2328:2840]

    gpad = np.concatenate([g_br, np.zeros((1024, 8), np.float32)], 1)
    W1 = np.concatenate([q, ksl, kwn, kcr, vcr, gpad, vsl, vwn], 1)
    assert W1.shape[1] == NW1
    p["W1"] = f(W1)
    p["W2"] = f(np.concatenate([u, z_s, z_a], 1))
    p["w_ada"] = f(inp["w_ada"][l])
    p["bada_col"] = f(inp["b_ada"][l].reshape(24, 128).T)
    p["bada_grow"] = f(inp["b_ada"][l][2048:3072].reshape(1, 1024))
    p["ng_col"] = f(inp["norm_g"][l].reshape(8, 128).T)

    def gcols(g):
        g = np.asarray(g, np.float32)
        return f(np.stack([np.tile(g, 2), np.tile(g[(np.arange(64) + 32) % 64], 2)], 1))

    p["gq"] = gcols(inp["q_norm_g"][l])
    p["gks"] = gcols(inp["k_slc_norm_g"][l])
    p["gkw"] = gcols(inp["k_win_norm_g"][l])
    p["gkc_bc"] = f(np.tile(np.asarray(inp["k_cmp_norm_g"][l], np.float32)[None, :], (128, 1)))
    p["posk_col"] = f(np.asarray(inp["cmp_pos_k"][l]).reshape(16, 128).T)
    p["posv_col"] = f(np.asarray(inp["cmp_pos_v"][l]).reshape(16, 128).T)
    p["w1k"] = f(inp["cmp_w1_k"][l])
    p["w1v"] = f(inp["cmp_w1_v"][l])
    p["w2k"] = f(inp["cmp_w2_k"][l])
    p["w2v"] = f(inp["cmp_w2_v"][l])
    a_re = np.asarray(inp["ssm_a_re"][l], np.float32)
    a_im = np.asarray(inp["ssm_a_im"][l], np.float32)
    ldt = np.asarray(inp["ssm_log_dt"][l], np.float32)
    p["are2"] = f(np.concatenate([a_re.T, a_re.T], 0))
    p["aim2"] = f(np.concatenate([a_im.T, a_im.T], 0))
    p["ldt2"] = f(np.tile(ldt[None, :], (128, 1)))
    p["b_are"] = f(np.tile(a_re[None, :, :], (128, 1, 1)))
    p["b_aim"] = f(np.tile(a_im[None, :, :], (128, 1, 1)))
    p["b_ldt"] = f(np.tile(ldt[None, :, None], (128, 1, 64)))
    b_re = np.asarray(inp["ssm_b_re"][l], np.float32)
    b_im = np.asarray(inp["ssm_b_im"][l], np.float32)
    bre_l = np.zeros((128, 32, 64), np.float32)
    bim_l = np.zeros((128, 32, 64), np.float32)
    for g in range(32):
        k0 = 16 * (g % 8)
        bre_l[k0:k0 + 16, g, :] = b_re[g].T
        bim_l[k0:k0 + 16, g, :] = b_im[g].T
    p["b_bre"] = bre_l
    p["b_bim"] = bim_l
    c_re = np.asarray(inp["ssm_c_re"][l], np.float32)
    c_im = np.asarray(inp["ssm_c_im"][l], np.float32)
    p["cw_l"] = f(np.concatenate([c_re.transpose(2, 0, 1), c_im.transpose(2, 0, 1)], 0))
    p["dcol"] = f(np.asarray(inp["ssm_d"][l]).reshape(4, 128).T)
    p["glub_col"] = f(np.asarray(inp["glu_b"][l]).reshape(4, 128).T)
    p["glu_w"] = f(inp["glu_w"][l])
    p["w_out"] = f(inp["w_out"][l])
    return p


IN_SPECS = None


def input_specs(consts, params):
    specs = {"x": ((T, D), F32), "c_col": ((128, 8), F32)}
    for d in (consts, params):
        for k, v in d.items():
            specs[k] = (tuple(v.shape), BF16 if v.dtype == ml_dtypes.bfloat16 else F32)
    return specs


def build(specs, dbg=None):
    nc = bass.Bass("TRN2", target_bir_lowering=False)
    dr = {}
    for name, (shape, dt) in specs.items():
        dr[name] = nc.dram_tensor(name, list(shape), dt, kind="ExternalInput").ap()
    out = nc.dram_tensor("out", [T, D], F32, kind="ExternalOutput").ap()
    attn_scr = nc.dram_tensor("attn_scr", [512, T], BF16, kind="Internal").ap()
    dr["_zscr"] = nc.dram_tensor("zscr", [4, 512], F32, kind="Internal").ap()
    dr["_gscr"] = nc.dram_tensor("gscr", [2, 32, 512], F32, kind="Internal").ap()
    dbg_out = {}
    if dbg:
        for name, (shape, dt) in dbg.items():
            dbg_out[name] = nc.dram_tensor(name, list(shape), dt, kind="ExternalOutput").ap()

    with ExitStack() as ctx0:
        S = Sched(nc, ctx0)
        uid = [0]

        def U(prefix):
            uid[0] += 1
            return "%s_%d" % (prefix, uid[0])

        def load(ctx, name, shape, dt, src_ap, q="sp"):
            t = ctx.enter_context(nc.sbuf_tensor("sb_" + name, list(shape), dt))
            S.dma(q, lambda e: e.dma_start(out=t[:], in_=src_ap), writes=[name])
            return t

        def load_cast(t, name, shape3, src_w, stage, col0=0):
            kcs, ncols = shape3[1], shape3[2]
            srcv = src_w.rearrange("(kc p) n -> p kc n", p=128)
            per = max(1, 2048 // kcs)
            c = 0
            pi = 0
            while c < ncols:
                w = min(per, ncols - c)
                st = stage[pi % 2]
                rn = "stage%d" % (pi % 2)
                S.dma("sp", lambda e, st=st, c=c, w=w: e.dma_start(out=st[:, 0:kcs * w].rearrange("p (k n) -> p k n", k=kcs), in_=srcv[:, :, col0 + c:col0 + c + w]), writes=[rn])
                eng = ["dve", "pool", "act"][pi % 3]
                if eng == "act":
                    S.op(eng, lambda e, st=st, c=c, w=w: e.copy(t[:, :, c:c + w], st[:, 0:kcs * w].rearrange("p (k n) -> p k n", k=kcs)), reads=[rn], writes=[name])
                else:
                    S.op(eng, lambda e, st=st, c=c, w=w: e.tensor_copy(t[:, :, c:c + w], st[:, 0:kcs * w].rearrange("p (k n) -> p k n", k=kcs)), reads=[rn], writes=[name + "_%d" % pi])
                c += w
                pi += 1
            return t, [name] + [name + "_%d" % j for j in range(pi)]

        ident_bf = load(ctx0, "ident_bf", [128, 128], BF16, dr["ident_bf"][:, :])
        ident_f = load(ctx0, "ident_f", [128, 128], F32, dr["ident_f"][:, :])
        c_col = load(ctx0, "c_col", [128, 8], F32, dr["c_col"][:, :])
        bada_col = load(ctx0, "bada_col", [128, 24], F32, dr["bada_col"][:, :])
        ng_col = load(ctx0, "ng_col", [128, 8], F32, dr["ng_col"][:, :])
        bada_grow = load(ctx0, "bada_grow", [1, 1024], F32, dr["bada_grow"][:, :])
        ones_row = load(ctx0, "ones_row", [1, 128], F32, dr["ones_row"][:, :])
        gs_col = ctx0.enter_context(nc.sbuf_tensor("gs_col", [128, 8], F32))
        sh_col = ctx0.enter_context(nc.sbuf_tensor("sh_col", [128, 8], F32))
        gate_bc = ctx0.enter_context(nc.sbuf_tensor("gate_bc", [128, 1024], F32))

        with ExitStack() as c0:
            sc_col = c0.enter_context(nc.sbuf_tensor("sc_col", [128, 8], F32))
            mod_col = c0.enter_context(nc.sbuf_tensor("mod_col", [128, 24], F32))
            grow = c0.enter_context(nc.sbuf_tensor("grow", [1, 1024], F32))
            wst = [c0.enter_context(nc.sbuf_tensor("wst%d" % i, [128, 8, 128], F32)) for i in range(2)]
            pmod = c0.enter_context(nc.psum_tensor("pmod", [128, 512], F32))
            prow = c0.enter_context(nc.psum_tensor("prow", [128, 512], F32))
            pbc = c0.enter_context(nc.psum_tensor("pbc", [128, 512], F32))
            S.op("act", lambda e: e.activation(sc_col[:], c_col[:], AF.Silu), reads=["c_col"], writes=["sc_col"])
            wv = dr["w_ada"].rearrange("(kc p) n -> p kc n", p=128)
            for jc in range(24):
                st = wst[jc % 2]
                rn = "wst%d" % (jc % 2)
                S.dma("sp", lambda e, st=st, jc=jc: e.dma_start(out=st[:], in_=wv[:, :, jc * 128:(jc + 1) * 128]), writes=[rn])
                for kc in range(8):
                    S.op("pe", lambda e, st=st, jc=jc, kc=kc: e.matmul(pmod[:, jc:jc + 1], st[:, kc, :], sc_col[:, kc:kc + 1], start=(kc == 0), stop=(kc == 7)),
                         reads=[rn, "sc_col"], writes=["pmod"], inc=(kc == 7))
                if jc >= 16:
                    j0 = (jc - 16) * 128
                    for kc in range(8):
                        S.op("pe", lambda e, st=st, j0=j0, kc=kc: e.matmul(prow[0:1, (j0 % 512):(j0 % 512) + 128], sc_col[:, kc:kc + 1], st[:, kc, :], start=(kc == 0), stop=(kc == 7)),
                             reads=[rn, "sc_col"], writes=["prow"], inc=(kc == 7))
                    if jc in (19, 23):
                        h0 = 0 if jc == 19 else 512
                        S.op("dve", lambda e, h0=h0: e.tensor_tensor(grow[0:1, h0:h0 + 512], prow[0:1, 0:512], bada_grow[0:1, h0:h0 + 512], ALU.add),
                             reads=["prow", "bada_grow"], writes=["grow"])
            S.op("dve", lambda e: e.tensor_tensor(mod_col[:], pmod[:, 0:24], bada_col[:], ALU.add), reads=["pmod", "bada_col"], writes=["mod_col"])
            S.op("dve", lambda e: e.scalar_tensor_tensor(gs_col[:], mod_col[:, 8:16], 1.0, ng_col[:], ALU.add, ALU.mult), reads=["mod_col", "ng_col"], writes=["gs_col"])
            S.op("dve", lambda e: e.tensor_copy(sh_col[:], mod_col[:, 0:8]), reads=["mod_col"], writes=["sh_col"])
            for h0 in (0, 512):
                S.op("pe", lambda e, h0=h0: e.matmul(pbc[:, 0:512], ones_row[0:1, :], grow[0:1, h0:h0 + 512], start=True, stop=True), reads=["ones_row", "grow"], writes=["pbc"])
                S.op("dve", lambda e, h0=h0: e.tensor_copy(gate_bc[:, h0:h0 + 512], pbc[:, 0:512]), reads=["pbc"], writes=["gate_bc"])
            S.barrier()
            S.emit()

        def front(i, xt_tiles, xt_names, hT, hname, W, evac_eng, tts=(0, 1, 2, 3)):
            for tt in tts:
                gt = 4 * i + tt
                xt = xt_tiles[tt]
                xn_ = xt_names[tt]
                S.dma("sp", lambda e, xt=xt, gt=gt: e.dma_start(out=xt[:], in_=dr["x"][gt * 128:(gt + 1) * 128, :]), writes=[xn_])
                S.op("act", lambda e, xt=xt: e.activation(W["junk"][:], xt[:], AF.Square, accum_out=W["ssq"][:, 0:1]), reads=[xn_], writes=["xn", "ssq"])
                S.op("dve", lambda e: e.tensor_scalar(W["ssq"][:, 1:2], W["ssq"][:, 0:1], 1.0 / D, EPS, ALU.mult, ALU.add), reads=["ssq"], writes=["ssq1"])
                S.op("act", lambda e: e.activation(W["ssq"][:, 2:3], W["ssq"][:, 1:2], AF.Sqrt), reads=["ssq1"], writes=["ssq2"])
                S.op("dve", lambda e: e.reciprocal(W["ssq"][:, 3:4], W["ssq"][:, 2:3]), reads=["ssq2"], writes=["ssq3"])
                S.op("dve", lambda e, xt=xt: e.tensor_scalar(W["xn"][:], xt[:], W["ssq"][:, 3:4], None, ALU.mult), reads=[xn_, "ssq3"], writes=["xn"])
                for half in range(2):
                    for j in range(4):
                        kc = half * 4 + j
                        S.op("pe", lambda e, kc=kc, j=j: e.matmul(W["ptr"][:, j * 128:(j + 1) * 128], W["xn"][:, kc * 128:(kc + 1) * 128], ident_bf[:], start=True, stop=True),
                             reads=["xn", "ident_bf"], writes=[W.get("ptrn", "ptr")], inc=(j == 3))
                    for j in range(4):
                        kc = half * 4 + j
                        eng = evac_eng[kc % len(evac_eng)]
                        if eng == "act":
                            S.op("act", lambda e, kc=kc, j=j, tt=tt: e.activation(hT[:, kc, tt * 128:(tt + 1) * 128], W["ptr"][:, j * 128:(j + 1) * 128], AF.Identity,
                                                                                 bias=sh_col[:, kc:kc + 1], scale=gs_col[:, kc:kc + 1]),
                                 reads=[W.get("ptrn", "ptr"), "gs_col", "sh_col"], writes=[(hname, "act")])
                        else:
                            S.op("dve", lambda e, kc=kc, j=j, tt=tt: e.tensor_scalar(hT[:, kc, tt * 128:(tt + 1) * 128], W["ptr"][:, j * 128:(j + 1) * 128],
                                                                                    gs_col[:, kc:kc + 1], sh_col[:, kc:kc + 1], ALU.mult, ALU.add),
                                 reads=[W.get("ptrn", "ptr"), "gs_col", "sh_col"], writes=[(hname, "dve")])
            return [(hname, "act"), (hname, "dve")]

        if WITH_ATTN:
            pass1(nc, S, dr, attn_scr, dbg_out, front, load, load_cast, ident_bf, ident_f)

        pass2(nc, S, dr, out, attn_scr, dbg_out, front, load, load_cast, ident_bf, ident_f, gate_bc)
    return nc


def pass1(nc, S, dr, attn_scr, dbg_out, front, load, load_cast, ident_bf, ident_f):
    with ExitStack() as c1:
        def sb(name, shape, dt=F32):
            return c1.enter_context(nc.sbuf_tensor("a_" + name, list(shape), dt))

        def psb(name, shape=(128, 512), dt=F32):
            return c1.enter_context(nc.psum_tensor("a_" + name, list(shape), dt))

        w1s = sb("w1s", [128, 8, NW1], BF16)
        cw1 = [sb("cw1k", [128, 16, 256], BF16), sb("cw1v", [128, 16, 256], BF16)]
        cw2 = [sb("cw2k", [128, 2, 64], BF16), sb("cw2v", [128, 2, 64], BF16)]
        Kaug = [sb("Kaug0", [128, T], BF16), sb("Kaug1", [128, T], BF16)]
        Vs = sb("Vs", [128, 64, 2, 65], BF16)
        Kw = sb("Kw", [128, 2, 1024], BF16)
        Vw = sb("Vw", [128, 8, 2, 65], BF16)
        kcT = sb("kcT", [128, 2, 512], BF16)
        Vc = sb("Vc", [128, 4, 2, 65], BF16)
        Xs = [sb("Xk", [128, 2, 1056], BF16), sb("Xv", [128, 2, 1056], BF16)]
        posb = sb("posb", [128, 2, 2])
        ones64 = sb("ones64", [128, 64])
        cmask = load(c1, "cmask", [128, 4, 512], BF16, dr["cmask"].rearrange("v p t -> p v t"))
        wlo = load(c1, "wlo", [128, 4, 512], BF16, dr["wlo"].rearrange("v p t -> p v t"))
        ZA = load(c1, "ZA", [128, 4, 128], BF16, dr["ZA"].rearrange("c p j -> p c j"))
        Fadd = load(c1, "Fadd", [128, 256], F32, dr["Fadd"][:, :])
        Finv = load(c1, "Finv", [128, 256], F32, dr["Finv"][:, :])
        Prot = load(c1, "Prot", [128, 128], BF16, dr["Prot"][:, :])
        onesblk = load(c1, "onesblk_f", [128, 128], F32, dr["onesblk_f"][:, :])
        gcol = [load(c1, nm, [128, 2], F32, dr[nm][:, :]) for nm in ("gq", "gks", "gkw")]
        gkc_bc = load(c1, "gkc_bc", [128, 64], F32, dr["gkc_bc"][:, :])
        posc = [load(c1, "posk_col", [128, 16], F32, dr["posk_col"][:, :]), load(c1, "posv_col", [128, 16], F32, dr["posv_col"][:, :])]
        posc_b = [sb("posk_b", [128, 16], BF16), sb("posv_b", [128, 16], BF16)]

        csu = ExitStack()
        stage = [csu.enter_context(nc.sbuf_tensor("a_stage0", [128, 2048], F32)), csu.enter_context(nc.sbuf_tensor("a_stage1", [128, 2048], F32))]
        ppos = csu.enter_context(nc.psum_tensor("a_ppos", [128, 512], F32))
        load_cast(w1s, "w1s", [128, 8, NW1], dr["W1"], stage)
        load_cast(cw1[0], "cw1k", [128, 16, 256], dr["w1k"], stage)
        load_cast(cw1[1], "cw1v", [128, 16, 256], dr["w1v"], stage)
        load_cast(cw2[0], "cw2k", [128, 2, 64], dr["w2k"], stage)
        load_cast(cw2[1], "cw2v", [128, 2, 64], dr["w2v"], stage)
        S.barrier()
        for kv in range(2):
            S.op("dve", lambda e, kv=kv: e.tensor_copy(posc_b[kv][:], posc[kv][:]), writes=["posc_b%d" % kv])
            for hc in range(2):
                for lp in range(16):
                    S.op("pe", lambda e, kv=kv, hc=hc, lp=lp: e.matmul(ppos[:, kv * 2 + hc:kv * 2 + hc + 1], cw1[kv][:, lp, hc * 128:(hc + 1) * 128], posc_b[kv][:, lp:lp + 1], start=(lp == 0), stop=(lp == 15)),
                         reads=["posc_b%d" % kv], writes=["ppos"], inc=(lp == 15))
        S.op("dve", lambda e: e.tensor_copy(posb[:, :, :].rearrange("p a b -> p (a b)"), ppos[:, 0:4]), reads=["ppos"], writes=["posb"])
        S.op("pool", lambda e: e.memset(ones64[:], 1.0), writes=["ones64"])
        S.op("pool", lambda e: e.memset(kcT[:], 0.0), writes=["kcT"])
        S.op("pool", lambda e: e.memset(Vc[:], 0.0), writes=["Vc"])
        S.op("pool", lambda e: e.memset(Vc[:, :, :, 64:65], 1.0), writes=["Vc"])
        S.op("pool", lambda e: e.memset(Vs[:, :, :, 64:65], 1.0), writes=["Vs"])
        S.op("pool", lambda e: e.memset(Vw[:], 0.0), writes=["Vw"])
        S.op("pool", lambda e: e.memset(Vw[:, :, :, 64:65], 1.0), writes=["Vw"])
        S.op("pool", lambda e: e.memset(Kw[:], 0.0), writes=["Kw"])
        for kv in range(2):
            S.op("pool", lambda e, kv=kv: e.memset(Xs[kv][:], 0.0), writes=["X%d" % kv])
        for g in range(2):
            S.dma("sp", lambda e, g=g: e.dma_start(out=Kaug[g][64:128, :], in_=dr["Epat"][:, :]), writes=["Kaug%d" % g])
        S.barrier()
        S.emit()
        csu.close()

        xts = [sb("xt0", [128, 1024]), sb("xt1", [128, 1024])]
        xn_t = sb("xn", [128, 1024], BF16)
        Wf = {"junk": xn_t, "ssq": sb("ssq", [128, 4]), "xn": xn_t, "ptr": None}
        hT = sb("hT", [128, 8, 512], BF16)
        cs_t = sb("cs_t", [128, 512])
        sn_t = sb("sn_t", [128, 512])
        cmm_t = sb("cmm_t", [128, 512], BF16)
        tA = sb("tA", [128, 512])
        tB = sb("tB", [128, 512])
        tC = sb("tC", [128, 512])
        tD = sb("tD", [128, 512])
        gqb = sb("gqb", [128, 512], BF16)
        qrp = sb("qrp", [128, 512], BF16)
        qnp = sb("qnp", [128, 512], BF16)
        Qaug = sb("Qaug", [128, 2, 4, 512], BF16)
        qn = sb("qn", [128, 4, 512], BF16)
        gsb = sb("gsb", [32, 512])
        pTb = [sb("pTb%d" % j, [128, 512], BF16) for j in range(4)]
        zrow = sb("zrow", [128, 2, 512])
        rzbs = [sb("rzb0", [64, 512]), sb("rzb1", [64, 512])]
        tO = sb("tO", [64, 512])
        gbs = [sb("gb%d" % j, [64, 512]) for j in range(3)]
        accH = sb("accH", [64, 4, 512], BF16)
        impg = sb("impg", [128, 4, 128])
        impt = sb("impt", [128, 4, 128])
        rs = sb("rs", [128, 8])
        sc = sb("sc", [128, 128])
        sc2 = sb("sc2", [128, 128])
        m8 = sb("m8", [128, 16])
        selb = sb("selb", [128, 4, 2, 128], BF16)
        selT = sb("selT", [128, 2, 512], BF16)
        hidb = sb("hidb", [128, 2, 2, 32], BF16)
        kcn = sb("kcn", [32, 2, 64], BF16)
        kst = sb("kst", [32, 8])
        pproj = psb("pproj")
        Wf["ptr"] = pproj
        Wf["ptrn"] = "pproj"
        pmisc = psb("pmisc")
        psc = [psb("psc0"), psb("psc1"), psb("psc2")]
        pacc = [psb("pacc0"), psb("pacc1")]
        pimp = psb("pimp")
        cnt = {"sc": 0, "acc": 0, "pt": 0, "pp": 0, "ep": 0, "gb": 0}

        def norm_rope(pproj, ppn, gc, want_qn):
            S.op("act", lambda e: e.activation(tA[:], pproj[:, :], AF.Square), reads=[ppn], writes=["tA"])
            S.op("dve", lambda e: e.tensor_scalar(gqb[:], pproj[:, :], gc[:, 0:1], None, ALU.mult), reads=[ppn, "tA"], writes=["gqb"])
            S.op("pe", lambda e: e.matmul(pmisc[:, :], onesblk[:], tA[:], start=True, stop=True), reads=["tA"], writes=["pmisc"])
            S.op("act", lambda e: e.activation(tB[:], pmisc[:, :], AF.Sqrt, bias=EPS_AP[:, 0:1], scale=1.0 / 64), reads=["pmisc", "eps_ap"], writes=["tB"])
            S.op("dve", lambda e: e.reciprocal(tB[:], tB[:]), reads=["tB"], writes=["tB"])
            S.op("pe", lambda e: e.matmul(pmisc[:, :], Prot[:], gqb[:], start=True, stop=True), reads=["gqb"], writes=["pmisc"])
            S.op("dve", lambda e: e.tensor_tensor(tC[:], gqb[:], cs_t[:], ALU.mult), reads=["gqb", "cs_t"], writes=["tC"])
            S.op("dve", lambda e: e.tensor_tensor(tD[:], pmisc[:, :], sn_t[:], ALU.mult), reads=["pmisc", "sn_t"], writes=["tD"])
            S.op("dve", lambda e: e.tensor_tensor(tC[:], tC[:], tD[:], ALU.add), reads=["tC", "tD"], writes=["tC"])
            S.op("dve", lambda e: e.tensor_tensor(qrp[:], tC[:], tB[:], ALU.mult), reads=["tC", "tB"], writes=["qrp"])
            if want_qn:
                S.op("dve", lambda e: e.tensor_tensor(qnp[:], gqb[:], tB[:], ALU.mult), reads=["gqb", "tB"], writes=["qnp"])

        EPS_AP = sb("eps_ap", [128, 1])
        S.op("pool", lambda e: e.memset(EPS_AP[:], EPS), writes=["eps_ap"])

        pps = [(pproj, "pproj"), (pproj, "pproj")]

        def proj_tile(col0, ncols, hnames):
            pp, ppn = pps[cnt["pp"] % 2]
            cnt["pp"] += 1
            for kc in range(8):
                S.op("pe", lambda e, kc=kc: e.matmul(pp[0:ncols, :], w1s[:, kc, col0:col0 + ncols], hT[:, kc, :], start=(kc == 0), stop=(kc == 7)),
                     reads=hnames, writes=[ppn], inc=(kc == 7))
            return pp, ppn

        def next_sc():
            j = cnt["sc"] % 3
            cnt["sc"] += 1
            return psc[j], "psc%d" % j

        def next_pt():
            j = cnt["pt"] % 4
            cnt["pt"] += 1
            return pTb[j], "pTb%d" % j

        def epilogue(pa, pan, g, hh, br, first, clamp):
            h = 4 * g + hh
            k = 3 * h + br
            if clamp:
                S.op("dve", lambda e: e.tensor_scalar(zrow[64:65, :], pa[64:65, :], 1e-30, None, ALU.max), reads=[pan], writes=["zrow"])
                S.op("dve", lambda e: e.reciprocal(zrow[64:65, :], zrow[64:65, :]), reads=["zrow"], writes=["zrow"])
            else:
                S.op("dve", lambda e: e.reciprocal(zrow[64:65, :], pa[64:65, :]), reads=[pan], writes=["zrow"])
            S.op("pe", lambda e: e.matmul(pmisc[0:64, :], ones64[64:65, 0:64], zrow[64:65, :], start=True, stop=True), reads=["zrow"], writes=["pmisc"])
            S.op("act", lambda e: e.copy(rzb[:], pmisc[0:64, :]), reads=["pmisc"], writes=["rzb"])
            S.op("dve", lambda e: e.tensor_tensor(tO[:], pa[0:64, :], rzb[:], ALU.mult), reads=[pan, "rzb"], writes=["tO"])
            S.op("pe", lambda e, k=k: e.matmul(pmisc[0:64, :], Gsel[:, k, :], gsb[:, :], start=True, stop=True), reads=["gsb", "rzb"], writes=["pmisc"])
            if first:
                S.op("dve", lambda e: e.tensor_tensor(accA[:], pmisc[0:64, :], tO[:], ALU.mult), reads=["pmisc", "tO"], writes=["accA"])
            else:
                S.op("dve", lambda e: e.tensor_tensor(tO2[:], pmisc[0:64, :], tO[:], ALU.mult), reads=["pmisc", "tO"], writes=["tO2"])
                S.op("pool", lambda e: e.tensor_tensor(accA[:], accA[:], tO2[:], ALU.add), reads=["accA", "tO2"], writes=["accA"])

        for i in range(NCH_RUN):
            t0 = i * CH
            hnames = front(i, [xts[tt % 2] for tt in range(4)], ["xt%d" % (tt % 2) for tt in range(4)], hT, "hT", Wf, ["dve"])
            S.dma("sp", lambda e, i=i: e.dma_start(out=cs_t[:], in_=dr["cosT"][i, :, :]), writes=["cs_t"])
            S.dma("sp", lambda e, i=i: e.dma_start(out=sn_t[:], in_=dr["sinS"][i, :, :]), writes=["sn_t"])
            S.dma("sp", lambda e, i=i: e.dma_start(out=cmm_t[:], in_=dr["cmpmask"][i % 4, :, :]), writes=["cmm_t"])
            pp, ppn = proj_tile(512, 128, hnames)
            norm_rope(pp, ppn, gcol[1], False)
            S.op("dve", lambda e, t0=t0: e.tensor_copy(Kaug[0][0:64, t0:t0 + CH], qrp[0:64, :]), reads=["qrp"], writes=["Kaug0"])
            S.op("dve", lambda e, t0=t0: e.tensor_copy(Kaug[1][0:64, t0:t0 + CH], qrp[64:128, :]), reads=["qrp"], writes=["Kaug1"])
            pp, ppn = proj_tile(640, 128, hnames)
            norm_rope(pp, ppn, gcol[2], False)
            w0 = (i % 2) * 512
            S.op("dve", lambda e, w0=w0: e.tensor_copy(Kw[0:64, 0, w0:w0 + CH], qrp[0:64, :]), reads=["qrp"], writes=["Kw"])
            S.op("dve", lambda e, w0=w0: e.tensor_copy(Kw[0:64, 1, w0:w0 + CH], qrp[64:128, :]), reads=["qrp"], writes=["Kw"])
            for kv in range(2):
                X = Xs[kv]
                xn_ = "X%d" % kv
                S.op("dve", lambda e, X=X: e.tensor_copy(X[:, :, 0:512], X[:, :, 512:1024]), reads=[xn_], writes=[xn_])
                pp, ppn = proj_tile(768 + 128 * kv, 128, hnames)
                S.op("act", lambda e, X=X, pp=pp: e.copy(X[0:64, 0, 512:1024], pp[0:64, :]), reads=[ppn], writes=[xn_])
                S.op("act", lambda e, X=X, pp=pp: e.copy(X[64:128, 0, 511:1023], pp[0:64, :]), reads=[ppn], writes=[xn_])
                S.op("act", lambda e, X=X, pp=pp: e.copy(X[0:64, 1, 512:1024], pp[64:128, :]), reads=[ppn], writes=[xn_])
                S.op("act", lambda e, X=X, pp=pp: e.copy(X[64:128, 1, 511:1023], pp[64:128, :]), reads=[ppn], writes=[xn_])
            pp, ppn = proj_tile(1024, 32, hnames)
            S.op("act", lambda e, pp=pp: e.activation(gsb[:, :], pp[0:32, :], AF.Sigmoid), reads=[ppn], writes=["gsb"])
            S.dma("sp", lambda e, i=i: e.dma_start(out=dr["_gscr"][i % 2, :, :], in_=gsb[:, :]), reads=["gsb"], writes=["gscr%d" % (i % 2)])
            for ts in range(4):
                kt = 4 * i + ts
                pp, ppn = pps[cnt["pp"] % 2]
                cnt["pp"] += 1
                for kc in range(8):
                    S.op("pe", lambda e, kc=kc, ts=ts, pp=pp: e.matmul(pp[:, 0:256], hT[:, kc, ts * 128:(ts + 1) * 128], w1s[:, kc, 1056:1312], start=(kc == 0), stop=(kc == 7)),
                         reads=hnames, writes=[ppn], inc=(kc == 7))
                S.op("act", lambda e, kt=kt, pp=pp: e.copy(Vs[:, kt, :, 0:64], pp[:, 0:128].rearrange("p (g d) -> p g d", g=2)), reads=[ppn], writes=["Vs"])
                S.op("act", lambda e, kt=kt, pp=pp: e.copy(Vw[:, kt % 8, :, 0:64], pp[:, 128:256].rearrange("p (g d) -> p g d", g=2)), reads=[ppn], writes=["Vw"])
            for kv in range(2):
                X = Xs[kv]
                xn_ = "X%d" % kv
                for q in ([i - 1, i] if i > 0 else [i]):
                    pos0 = 0 if q == i - 1 else 512
                    for g in range(2):
                        for hc in range(2):
                            c0 = (g * 2 + hc) * 32
                            for lp in range(16):
                                S.op("pe", lambda e, kv=kv, X=X, g=g, hc=hc, lp=lp, pos0=pos0, c0=c0: e.matmul(pmisc[:, c0:c0 + 32], cw1[kv][:, lp, hc * 128:(hc + 1) * 128], X[:, g, pos0 + 2 * lp:pos0 + 2 * lp + 512:16], start=(lp == 0), stop=(lp == 15)),
                                     reads=[xn_], writes=["pmisc"], inc=(lp == 15))
                    for hc in range(2):
                        S.op("act", lambda e, kv=kv, hc=hc: e.activation(hidb[:, :, hc, :], pmisc[:, 0:128].rearrange("p (g h n) -> p g h n", g=2, h=2)[:, :, hc, :], AF.Gelu, bias=posb[:, kv, hc:hc + 1]),
                             reads=["pmisc", "posb"], writes=["hidb"])
                    for g in range(2):
                        for hc in range(2):
                            S.op("pe", lambda e, kv=kv, g=g, hc=hc: e.matmul(pmisc[0:32, 128 + g * 64:128 + (g + 1) * 64], hidb[:, g, hc, :], cw2[kv][:, hc, :], start=(hc == 0), stop=(hc == 1)),
                                 reads=["hidb"], writes=["pmisc"], inc=(hc == 1))
                    cq = q // 4
                    pq = 32 * (q % 4)
                    if kv == 1:
                        S.op("act", lambda e, cq=cq, pq=pq: e.copy(Vc[pq:pq + 32, cq, :, 0:64], pmisc[0:32, 128:256].rearrange("p (g d) -> p g d", g=2)), reads=["pmisc"], writes=["Vc"])
                    else:
                        for g in range(2):
                            S.op("act", lambda e, g=g: e.activation(kcn[:, g, :], pmisc[0:32, 128 + g * 64:128 + (g + 1) * 64], AF.Square, accum_out=kst[:, g:g + 1]), reads=["pmisc"], writes=["kcn", "kst"])
                        S.op("dve", lambda e: e.tensor_scalar(kst[:, 2:4], kst[:, 0:2], 1.0 / 64, EPS, ALU.mult, ALU.add), reads=["kst"], writes=["kst"])
                        S.op("act", lambda e: e.activation(kst[:, 4:6], kst[:, 2:4], AF.Sqrt), reads=["kst"], writes=["kst"])
                        S.op("dve", lambda e: e.reciprocal(kst[:, 6:8], kst[:, 4:6]), reads=["kst"], writes=["kst"])
                        for g in range(2):
                            S.op("dve", lambda e, g=g: e.scalar_tensor_tensor(kcn[:, g, :], pmisc[0:32, 128 + g * 64:128 + (g + 1) * 64], kst[:, 6 + g:7 + g], gkc_bc[0:32, :], ALU.mult, ALU.mult),
                                 reads=["pmisc", "kst"], writes=["kcn"])
                        for g in range(2):
                            S.op("pe", lambda e, g=g: e.matmul(pmisc[0:64, 256 + g * 32:256 + (g + 1) * 32], kcn[:, g, :], ident_bf[0:32, 0:32], start=True, stop=True), reads=["kcn"], writes=["pmisc"], inc=(g == 1))
                        n0 = 32 * q
                        S.op("dve", lambda e, n0=n0: e.tensor_copy(kcT[0:64, :, n0:n0 + 32], pmisc[0:64, 256:320].rearrange("p (g n) -> p g n", g=2)), reads=["pmisc"], writes=["kcT"])
            for g in range(2):
                for j2 in range(2):
                    pp, ppn = proj_tile((2 * g + j2) * 128, 128, hnames)
                    norm_rope(pp, ppn, gcol[0], True)
                    for e2 in range(2):
                        hh = 2 * j2 + e2
                        rows = slice(64 * e2, 64 * e2 + 64)
                        S.op("dve", lambda e, hh=hh, rows=rows: e.tensor_copy(Qaug[0:64, 0, hh, :], qrp[rows, :]), reads=["qrp"], writes=["Qaug"])
                        S.op("dve", lambda e, hh=hh, rows=rows: e.tensor_copy(Qaug[0:64, 1, hh, :], qrp[rows, :]), reads=["qrp"], writes=["Qaug"])
                        S.op("dve", lambda e, hh=hh, rows=rows: e.tensor_copy(qn[0:64, hh, :], qnp[rows, :]), reads=["qnp"], writes=["qn"])
                cmax = i // 4

                def new_acc():
                    j = cnt["acc"] % 2
                    cnt["acc"] += 1
                    return pacc[j], "pacc%d" % j

                def mk_epilogue(pa, pan, g, hh, br, first, clamp, final, t0):
                    h = 4 * g + hh
                    k = 3 * h + br
                    an = "accH%d" % hh
                    ei = cnt["ep"]
                    cnt["ep"] += 1
                    zs = ei % 2
                    zsl = ei % 4
                    rzb, rzn = rzbs[zs], "rzb%d" % zs
                    gj = cnt["gb"] % 3
                    cnt["gb"] += 1
                    gb_, gbn = gbs[gj], "gb%d" % gj
                    isl = (t0 // CH) % 2

                    def s0():
                        S.dma("sp", lambda e: e.dma_start(out=gb_[:], in_=dr["_gscr"][isl, k:k + 1, :].partition_broadcast(64)), reads=["gscr%d" % isl], writes=[gbn])
                        if clamp:
                            S.op("dve", lambda e: e.tensor_scalar(zrow[64:65, zs, :], pa[64:65, :], 1e-30, None, ALU.max), reads=[pan], writes=["zrow%d" % zs])
                            S.op("dve", lambda e: e.reciprocal(zrow[64:65, zs, :], zrow[64:65, zs, :]), reads=["zrow%d" % zs], writes=["zrow%d" % zs])
                        else:
                            S.op("dve", lambda e: e.reciprocal(zrow[64:65, zs, :], pa[64:65, :]), reads=[pan], writes=["zrow%d" % zs])
                        S.dma("sp", lambda e: e.dma_start(out=dr["_zscr"][zsl:zsl + 1, :], in_=zrow[64:65, zs, :]), reads=["zrow%d" % zs], writes=["zscr%d" % zsl])
                        S.dma("sp", lambda e: e.dma_start(out=rzb[:], in_=dr["_zscr"][zsl:zsl + 1, :].partition_broadcast(64)), reads=["zscr%d" % zsl], writes=[rzn])

                    def s1():
                        S.op("dve", lambda e: e.tensor_tensor(tO[:], pa[0:64, :], rzb[:], ALU.mult), reads=[pan, rzn], writes=["tO"])
                        if first:
                            S.op("dve", lambda e: e.tensor_tensor(accH[:, hh, :], tO[:], gb_[:], ALU.mult), reads=["tO", gbn], writes=[an])
                        else:
                            S.op("dve", lambda e: e.tensor_tensor(tO[:], tO[:], gb_[:], ALU.mult), reads=["tO", gbn], writes=["tO"])
                            S.op("pool", lambda e: e.tensor_tensor(accH[:, hh, :], accH[:, hh, :], tO[:], ALU.add), reads=[an, "tO"], writes=[an])
                        if final:
                            S.dma("sp", lambda e: e.dma_start(out=attn_scr[h * 64:(h + 1) * 64, t0:t0 + CH], in_=accH[:, hh, :]), reads=[an], writes=["attn_scr"])
                    return [(0, s0), (2, s1)]

                def mk_imp_post(hh):
                    def f():
                        S.op("dve", lambda e: e.tensor_reduce(rs[:, 0:4], pimp[:, :].rearrange("p (s j) -> p s j", s=4), AX.X, ALU.add), reads=["pimp"], writes=["rs"])
                        S.op("dve", lambda e: e.tensor_scalar(rs[:, 4:8], rs[:, 0:4], 0.5, 1e-30, ALU.mult, ALU.max), reads=["rs"], writes=["rs"])
                        S.op("dve", lambda e: e.reciprocal(rs[:, 4:8], rs[:, 4:8]), reads=["rs"], writes=["rs"])
                        tgt = impg if hh == 0 else impt
                        tgn = "impg" if hh == 0 else "impt"
                        S.op("dve", lambda e: e.tensor_tensor(tgt[:], pimp[:, :].rearrange("p (s j) -> p s j", s=4), rs[:, 4:8].rearrange("p (s o) -> p s o", o=1).to_broadcast([128, 4, 128]), ALU.mult),
                             reads=["pimp", "rs"], writes=[tgn])
                        if hh > 0:
                            S.op("pool", lambda e: e.tensor_tensor(impg[:], impg[:], impt[:], ALU.add), reads=["impg", "impt"], writes=["impg"])
                    return f

                def mk_item(kind, g, hh, idx, npairs, pa, pan, arg, i):
                    st = {}

                    def score():
                        ps_, psn = next_sc()
                        pt, ptn = next_pt()
                        st["pt"], st["ptn"] = pt, ptn
                        if kind == "cmp":
                            c = arg
                            last = (c == npairs - 1)
                            S.op("pe", lambda e: e.matmul(ps_[:, :], kcT[0:64, g, c * 128:(c + 1) * 128], qn[0:64, hh, :], start=True, stop=(not last)), reads=["kcT", "qn"], writes=[psn], inc=(not last))
                            if last:
                                S.op("pe", lambda e: e.matmul(ps_[:, :], ident_bf[:], cmm_t[:], start=False, stop=True), reads=["cmm_t"], writes=[psn])
                        elif kind == "slc":
                            kt = arg
                            H = kt // 32
                            diag = kt >= 4 * i
                            S.op("pe", lambda e: e.matmul(ps_[:, :], Kaug[g][:, kt * 128:(kt + 1) * 128], Qaug[:, H, hh, :], start=True, stop=(not diag)), reads=["Kaug%d" % g, "Qaug"], writes=[psn], inc=(not diag))
                            if diag:
                                S.op("pe", lambda e: e.matmul(ps_[:, :], ident_bf[:], cmask[:, kt - 4 * i, :], start=False, stop=True), writes=[psn])
                        else:
                            kt = arg
                            sl = (kt % 8) * 128
                            mk = cmask[:, kt - 4 * i, :] if kt >= 4 * i else wlo[:, kt - 4 * i + 4, :]
                            S.op("pe", lambda e: e.matmul(ps_[:, :], Kw[0:64, g, sl:sl + 128], Qaug[0:64, 0, hh, :], start=True, stop=False), reads=["Kw", "Qaug"], writes=[psn], inc=False)
                            S.op("pe", lambda e: e.matmul(ps_[:, :], ident_bf[:], mk, start=False, stop=True), writes=[psn])
                        S.op("act", lambda e: e.activation(pt[:], ps_[:, :], AF.Exp, scale=0.125), reads=[psn], writes=[ptn])

                    def pv():
                        pt, ptn = st["pt"], st["ptn"]
                        first_ = (idx == 0)
                        last_ = (idx == npairs - 1)
                        if kind == "cmp":
                            c = arg
                            S.op("pe", lambda e: e.matmul(pa[0:65, :], Vc[:, c, g, :], pt[:], start=first_, stop=last_), reads=[ptn, "Vc"], writes=[pan])
                            for ts in range(4):
                                S.op("pe", lambda e, ts=ts: e.matmul(pimp[:, ts * 128:(ts + 1) * 128], pt[:, ts * 128:(ts + 1) * 128], ZA[:, c, :], start=(first_ and ts == 0), stop=(last_ and ts == 3), skip_group_check=True),
                                     reads=[ptn], writes=["pimp"], inc=(ts == 3))
                        elif kind == "slc":
                            kt = arg
                            S.op("pe", lambda e: e.matmul(pa[0:65, :], Vs[:, kt, g, :], pt[:], start=first_, stop=last_), reads=[ptn, "Vs"], writes=[pan])
                        else:
                            kt = arg
                            S.op("pe", lambda e: e.matmul(pa[0:65, :], Vw[:, kt % 8, g, :], pt[:], start=first_, stop=last_), reads=[ptn, "Vw"], writes=[pan])
                    return {"score": score, "pv": pv, "post": []}

                def run_stream(items, D=2):
                    pending = []
                    N = len(items)
                    for n in range(N + D):
                        if n < N:
                            items[n]["score"]()
                        still = []
                        for (due, fn) in pending:
                            if due <= n:
                                fn()
                            else:
                                still.append((due, fn))
                        pending = still
                        if n - D >= 0:
                            it = items[n - D]
                            it["pv"]()
                            for (dl, fn) in it["post"]:
                                if dl == 0:
                                    fn()
                                else:
                                    pending.append((n + dl, fn))
                    for (due, fn) in pending:
                        fn()

                items = []
                for hh in range(4):
                    pa, pan = new_acc()
                    for c in range(cmax + 1):
                        it = mk_item("cmp", g, hh, c, cmax + 1, pa, pan, c, i)
                        if c == cmax:
                            it["post"] = [(0, mk_imp_post(hh))] + mk_epilogue(pa, pan, g, hh, 0, True, True, False, t0)
                        items.append(it)
                run_stream(items)
                for ts in range(4):
                    tsg = 4 * i + ts
                    off = 126 - 2 * tsg
                    S.op("dve", lambda e, ts=ts, off=off: e.tensor_tensor(sc[:], impg[:, ts, :], Fadd[:, off:off + 128], ALU.add), reads=["impg"], writes=["sc"])
                    S.op("dve", lambda e: e.tensor_scalar(sc[:, 0:1], sc[:, 0:1], 1e4, None, ALU.add), reads=["sc"], writes=["sc"])
                    S.op("dve", lambda e: e.max(out=m8[:, 0:8], in_=sc[:]), reads=["sc"], writes=["m8"])
                    S.op("dve", lambda e: e.match_replace(out=sc2[:], in_to_replace=m8[:, 0:8], in_values=sc[:], imm_value=-3e4), reads=["sc", "m8"], writes=["sc2"])
                    S.op("dve", lambda e: e.max(out=m8[:, 8:16], in_=sc2[:]), reads=["sc2"], writes=["m8"])
                    S.op("dve", lambda e: e.tensor_scalar(sc2[:], sc[:], m8[:, 15:16], BIG, ALU.is_ge, ALU.mult), reads=["sc", "m8"], writes=["sc2"])
                    S.op("dve", lambda e, off=off: e.scalar_tensor_tensor(sc2[:], sc2[:], -BIG, Finv[:, off:off + 128], ALU.add, ALU.add), reads=["sc2"], writes=["sc2"])
                    S.op("dve", lambda e, ts=ts: e.tensor_copy(selb[:, ts, 0, :], sc2[:]), reads=["sc2"], writes=["selb"])
                    S.op("dve", lambda e, ts=ts: e.tensor_copy(selb[:, ts, 1, 0:64], sc2[:, 64:128]), reads=["sc2"], writes=["selb"])
                    S.op("dve", lambda e, ts=ts: e.tensor_copy(selb[:, ts, 1, 64:128], sc2[:, 0:64]), reads=["sc2"], writes=["selb"])
                items = []
                kts = [kt for kt in range(4 * i - 4, 4 * i + 4) if kt >= 0]
                for hh in range(4):
                    pa, pan = new_acc()
                    for idx, kt in enumerate(kts):
                        it = mk_item("win", g, hh, idx, len(kts), pa, pan, kt, i)
                        if idx == len(kts) - 1:
                            it["post"] = mk_epilogue(pa, pan, g, hh, 2, False, False, False, t0)
                        items.append(it)
                run_stream(items)
                for ts in range(4):
                    S.op("pe", lambda e, ts=ts: e.matmul(pmisc[:, 0:128], selb[:, ts, 1, :], ident_bf[:], start=True, stop=True), reads=["selb"], writes=["pmisc"], inc=False)
                    S.op("pe", lambda e, ts=ts: e.matmul(pmisc[:, 128:256], selb[:, ts, 0, :], ident_bf[:], start=True, stop=True), reads=["selb"], writes=["pmisc"])
                    S.op("dve", lambda e, ts=ts: e.tensor_copy(selT[64:128, :, ts * 128:(ts + 1) * 128], pmisc[64:128, 0:256].rearrange("p (h t) -> p h t", h=2)), reads=["pmisc"], writes=["selT"])
                for hh in range(4):
                    S.op("dve", lambda e, hh=hh: e.tensor_copy(Qaug[64:128, :, hh, :], selT[64:128, :, :]), reads=["selT"], writes=["Qaug"])
                items = []
                nkt = 4 * i + 4
                for hh in range(4):
                    pa, pan = new_acc()
                    for kt in range(nkt):
                        it = mk_item("slc", g, hh, kt, nkt, pa, pan, kt, i)
                        if kt == nkt - 1:
                            it["post"] = mk_epilogue(pa, pan, g, hh, 1, False, False, True, t0)
                        items.append(it)
                run_stream(items)
        S.barrier()
        S.emit()


def pass2(nc, S, dr, out, attn_scr, dbg_out, front, load, load_cast, ident_bf, ident_f, gate_bc):
    PI = math.pi
    with ExitStack() as c2:
        def sb(name, shape, dt=F32):
            return c2.enter_context(nc.sbuf_tensor(name, list(shape), dt))

        def psb(name, shape=(128, 512), dt=F32):
            return c2.enter_context(nc.psum_tensor(name, list(shape), dt))

        csu = ExitStack()
        def sbt(name, shape, dt=F32):
            return csu.enter_context(nc.sbuf_tensor(name, list(shape), dt))
        w2 = sb("w2", [128, 8, 1536], BF16)
        wout = sb("wout", [128, 8, 1024], BF16)
        glw = sb("glw", [128, 4, 512], BF16)
        pm_f = load(c2, "pm_f", [128, 128], F32, dr["pm_f"][:, :])
        mrow = load(c2, "mrow", [128, 128], F32, dr["mrow"][:, :])
        are2 = load(c2, "are2", [128, 32], F32, dr["are2"][:, :])
        aim2 = load(c2, "aim2", [128, 32], F32, dr["aim2"][:, :])
        ldt2 = load(c2, "ldt2", [128, 32], F32, dr["ldt2"][:, :])
        cw_l = load(c2, "cw_l", [128, 32, 16], F32, dr["cw_l"][:, :, :])
        dcol = load(c2, "dcol", [128, 4], F32, dr["dcol"][:, :])
        glub = load(c2, "glub", [128, 4], F32, dr["glub_col"][:, :])

        Cm = sb("Cm", [128, 32, 128])
        Sm = sb("Sm", [128, 32, 128])
        rho = sb("rho", [128, 32])
        th = sb("th", [128, 32])
        dt2 = sb("dt2", [128, 32])
        c128 = sb("c128", [128, 32])
        s128 = sb("s128", [128, 32])
        BT = sb("BT", [128, 32, 128], BF16)
        BTs = sb("BTs", [128, 32, 128], BF16)
        Cw = sb("Cw", [128, 32, 16], BF16)
        carry = sb("carry", [128, 32])
        Rlast = sb("Rlast", [128, 32])

        stage = [sbt("stage0", [128, 2048]), sbt("stage1", [128, 2048])]
        _, w2n = load_cast(w2, "w2", [128, 8, 1536], dr["W2"], stage)
        _, woutn = load_cast(wout, "wout", [128, 8, 1024], dr["w_out"], stage)
        _, glwn = load_cast(glw, "glw", [128, 4, 512], dr["glu_w"], stage)
        tA = sbt("tA", [128, 1024])
        tB = sbt("tB", [128, 1024])
        tC = sbt("tC", [128, 1024])
        tD = sbt("tD", [128, 1024])
        tE = sbt("tE", [128, 1024])
        tF = sbt("tF", [128, 1024])
        tG = sbt("tG", [128, 1024])
        tI = sbt("tI", [128, 1024], I32)

        def sin_of(dst_ap, arg_ap, n, shift, names_r, name_w):
            a = tF[:, 0:n]
            S.op("dve", lambda e: e.tensor_scalar(a, arg_ap, 1.0, shift, ALU.mult, ALU.add), reads=names_r, writes=["tF"])
            S.op("dve", lambda e: e.tensor_scalar(tI[:, 0:n], a, 1.0 / (2 * PI), None, ALU.mult), reads=["tF"], writes=["tI"])
            S.op("dve", lambda e: e.tensor_copy(tG[:, 0:n], tI[:, 0:n]), reads=["tI"], writes=["tG"])
            S.op("dve", lambda e: e.scalar_tensor_tensor(tG[:, 0:n], tG[:, 0:n], -2 * PI, a, ALU.mult, ALU.add), reads=["tG", "tF"], writes=["tG"])
            S.op("dve", lambda e: e.tensor_scalar(tG[:, 0:n], tG[:, 0:n], 3.14159, -3.14159, ALU.min, ALU.max), reads=["tG"], writes=["tG"])
            S.op("act", lambda e: e.activation(dst_ap, tG[:, 0:n], AF.Sin), reads=["tG"], writes=[name_w])

        S.op("act", lambda e: e.activation(dt2[:], ldt2[:], AF.Exp), reads=["ldt2"], writes=["dt2"])
        S.op("dve", lambda e: e.tensor_tensor(th[:], aim2[:], dt2[:], ALU.mult), reads=["aim2", "dt2"], writes=["th"])
        S.op("dve", lambda e: e.tensor_tensor(rho[:], are2[:], dt2[:], ALU.mult), reads=["are2", "dt2"], writes=["rho"])
        S.op("act", lambda e: e.activation(rho[:], rho[:], AF.Exp), reads=["rho"], writes=["rho"])
        S.op("dve", lambda e: e.tensor_scalar(tA[:, 0:32], th[:], 128.0, None, ALU.mult), reads=["th"], writes=["tA"])
        sin_of(s128[:], tA[:, 0:32], 32, 0.0, ["tA"], "s128")
        sin_of(c128[:], tA[:, 0:32], 32, PI / 2, ["tA"], "c128")
        for gq in range(4):
            for gi in range(8):
                g = gq * 8 + gi
                S.op("dve", lambda e, g=g, gi=gi: e.tensor_scalar(tA[:, gi * 128:(gi + 1) * 128], mrow[:], th[:, g:g + 1], None, ALU.mult), reads=["mrow", "th"], writes=["tA"])
            sin_of(Sm[:, gq * 8:(gq + 1) * 8, :].rearrange("p g m -> p (g m)"), tA[:, 0:1024], 1024, 0.0, ["tA"], "Sm")
            sin_of(Cm[:, gq * 8:(gq + 1) * 8, :].rearrange("p g m -> p (g m)"), tA[:, 0:1024], 1024, PI / 2, ["tA"], "Cm")
        S.op("dve", lambda e: e.tensor_copy(Cw[0:64, :, :], cw_l[0:64, :, :]), reads=["cw_l"], writes=["Cw0"])
        S.op("dve", lambda e: e.tensor_scalar(Cw[64:128, :, :], cw_l[64:128, :, :], -1.0, None, ALU.mult), reads=["cw_l"], writes=["Cw1"])
        for gq in range(4):
            gsl = slice(gq * 8, (gq + 1) * 8)
            n = 512
            a_re_t, a_im_t, ldt_t, bre_t, bim_t = tA[:, 0:n], tA[:, n:2 * n], tB[:, 0:n], tB[:, n:2 * n], tC[:, 0:n]
            for (dst, nm, rn) in ((a_re_t, "b_are", "tA"), (a_im_t, "b_aim", "tA"), (ldt_t, "b_ldt", "tB"), (bre_t, "b_bre", "tB"), (bim_t, "b_bim", "tC")):
                S.dma("sp", lambda e, dst=dst, nm=nm, gsl=gsl: e.dma_start(out=dst.rearrange("p (g m) -> p g m", g=8), in_=dr[nm][:, gsl, :]), writes=[rn])
            dtt, rl, tht, ee = tC[:, n:2 * n], tD[:, 0:n], tD[:, n:2 * n], tE[:, 0:n]
            sn, cs = tE[:, n:2 * n], tC[:, n:2 * n]
            S.op("act", lambda e: e.activation(dtt, ldt_t, AF.Exp), reads=["tB"], writes=["tC"])
            S.op("dve", lambda e: e.tensor_tensor(rl, a_re_t, dtt, ALU.mult), reads=["tA", "tC"], writes=["tD"])
            S.op("dve", lambda e: e.tensor_tensor(tht, a_im_t, dtt, ALU.mult), reads=["tA", "tC"], writes=["tD"])
            S.op("act", lambda e: e.activation(ee, rl, AF.Exp), reads=["tD"], writes=["tE"])
            sin_of(sn, tht, n, 0.0, ["tD"], "tE")
            sin_of(cs, tht, n, PI / 2, ["tD"], "tC")
            lbr1, lbi = rl, tht
            S.op("dve", lambda e: e.tensor_tensor(lbi, ee, sn, ALU.mult), reads=["tE"], writes=["tD"])
            S.op("dve", lambda e: e.tensor_tensor(lbr1, ee, cs, ALU.mult), reads=["tE", "tC"], writes=["tD"])
            S.op("dve", lambda e: e.tensor_scalar(lbr1, lbr1, -1.0, None, ALU.add), reads=["tD"], writes=["tD"])
            den = ee
            S.op("dve", lambda e: e.tensor_tensor(den, a_re_t, a_re_t, ALU.mult), reads=["tA", "tE"], writes=["tE"])
            S.op("dve", lambda e: e.tensor_tensor(sn, a_im_t, a_im_t, ALU.mult), reads=["tA", "tE"], writes=["tE"])
            S.op("dve", lambda e: e.tensor_tensor(den, den, sn, ALU.add), reads=["tE"], writes=["tE"])
            S.op("dve", lambda e: e.reciprocal(den, den), reads=["tE"], writes=["tE"])
            fr, fi, tmp = sn, cs, ldt_t
            S.op("dve", lambda e: e.tensor_tensor(fr, lbr1, a_re_t, ALU.mult), reads=["tD", "tA"], writes=["tE"])
            S.op("dve", lambda e: e.tensor_tensor(tmp, lbi, a_im_t, ALU.mult), reads=["tD", "tA"], writes=["tB"])
            S.op("dve", lambda e: e.tensor_tensor(fr, fr, tmp, ALU.add), reads=["tE", "tB"], writes=["tE"])
            S.op("dve", lambda e: e.tensor_tensor(fr, fr, den, ALU.mult), reads=["tE"], writes=["tE"])
            S.op("dve", lambda e: e.tensor_tensor(fi, lbi, a_re_t, ALU.mult), reads=["tD", "tA"], writes=["tC"])
            S.op("dve", lambda e: e.tensor_tensor(tmp, lbr1, a_im_t, ALU.mult), reads=["tD", "tA"], writes=["tB"])
            S.op("dve", lambda e: e.tensor_tensor(fi, fi, tmp, ALU.subtract), reads=["tC", "tB"], writes=["tC"])
            S.op("dve", lambda e: e.tensor_tensor(fi, fi, den, ALU.mult), reads=["tC", "tE"], writes=["tC"])
            br_, bi_ = lbr1, lbi
            S.op("dve", lambda e: e.tensor_tensor(br_, fr, bre_t, ALU.mult), reads=["tE", "tB"], writes=["tD"])
            S.op("dve", lambda e: e.tensor_tensor(tmp, fi, bim_t, ALU.mult), reads=["tC"], writes=["tB"])
            S.op("dve", lambda e: e.tensor_tensor(br_, br_, tmp, ALU.subtract), reads=["tD", "tB"], writes=["tD"])
            S.op("dve", lambda e: e.tensor_tensor(bi_, fr, bim_t, ALU.mult), reads=["tE", "tC"], writes=["tD"])
            S.op("dve", lambda e: e.tensor_tensor(tmp, fi, bre_t, ALU.mult), reads=["tC", "tB"], writes=["tB"])
            S.op("dve", lambda e: e.tensor_tensor(bi_, bi_, tmp, ALU.add), reads=["tD", "tB"], writes=["tD"])
            v3 = lambda ap: ap.rearrange("p (g m) -> p g m", g=8)
            S.op("dve", lambda e, gsl=gsl: e.tensor_copy(BT[:, gsl, 0:64], v3(br_)), reads=["tD"], writes=["BT"])
            S.op("dve", lambda e, gsl=gsl: e.tensor_copy(BT[:, gsl, 64:128], v3(bi_)), reads=["tD"], writes=["BT"])
            S.op("dve", lambda e, gsl=gsl: e.tensor_copy(BTs[:, gsl, 0:64], v3(bi_)), reads=["tD"], writes=["BTs"])
            S.op("dve", lambda e, gsl=gsl: e.tensor_scalar(BTs[:, gsl, 64:128], v3(br_), -1.0, None, ALU.mult), reads=["tD"], writes=["BTs"])
        S.op("dve", lambda e: e.memset(carry[:], 0.0), writes=["carry"])
        S.barrier()
        S.emit()
        csu.close()

        xts = [sb("xt%d" % i, [128, 1024]) for i in range(2)]
        xrs = [sb("xr%d" % i, [128, 1024]) for i in range(1)]
        xn_t = sb("xn", [128, 1024], BF16)
        Wf = {"junk": xn_t, "ssq": sb("ssq", [128, 4]), "xn": xn_t,
              "ptr": None}
        hTs = [sb("hT%d" % j, [128, 8, 512], BF16) for j in range(2)]
        uT_bfs = [sb("uT_bf%d" % j, [128, 4, 512], BF16) for j in range(2)]
        uT_fs = uT_bfs
        szss = [sb("szs%d" % j, [128, 4, 512], BF16) for j in range(2)]
        szas = [sb("sza%d" % j, [128, 4, 512], BF16) for j in range(2)]
        bufA = [sb("bufA%d" % j, [128, 256]) for j in range(2)]
        bufB = [sb("bufB%d" % j, [128, 256]) for j in range(2)]
        bB16 = [sb("bB16_%d" % j, [128, 256], BF16) for j in range(2)]
        bC16 = [sb("bC16_%d" % j, [128, 256], BF16) for j in range(2)]
        bufC = [None, None]
        Rts = [sb("Rt%d" % j, [128, 2, 128]) for j in range(2)]
        Rtbs = [sb("Rtb%d" % j, [128, 256], BF16) for j in range(2)]
        pm_b = sb("pm_b", [128, 128], BF16)
        S.op("dve", lambda e: e.tensor_copy(pm_b[:], pm_f[:]), reads=["pm_f"], writes=["pm_b"])
        ysb = sb("ysb", [128, 512])
        y2 = sb("y2", [128, 4, 128])
        gy_bs = [sb("gy_b%d" % j, [128, 4, 512], BF16) for j in range(2)]
        sig = sb("sig", [128, 512])
        m1 = sb("m1", [128, 512])
        mixs = [sb("mix%d" % j, [128, 8, 512], BF16) for j in range(2)]
        res = sb("res", [128, 1024])
        ct1 = sb("ct1", [128, 32])
        ct2 = sb("ct2", [128, 32])
        pproj = psb("pproj")
        Wf["ptr"] = pproj
        Wf["ptrn"] = "pproj"
        pVW = [psb("pVW0"), psb("pVW1")]
        pRs = [psb("pR0"), psb("pR1")]
        pY = psb("pY")
        pT = psb("pT")
        pO = [psb("pO0"), pproj]
        pOn = ["pO0", "pproj"]

        if not WITH_ATTN:
            for j_ in range(2):
                S.op("pool", lambda e, j_=j_: e.memset(mixs[j_][:, 0:4, :], 0.0), writes=["mix_attn%d" % j_])

        def make_units(i):
            cp = i % 2
            hT, uT_bf, uT_f, szs, sza = hTs[cp], uT_bfs[cp], uT_fs[cp], szss[cp], szas[cp]
            hn = "hT%d" % cp
            xs_ = [xts[tt % 2] for tt in range(4)]
            xn_ = ["xt%d" % (tt % 2) for tt in range(4)]
            hnames = [(hn, "act"), (hn, "dve")]
            units = []
            for tt in range(4):
                units.append(lambda tt=tt: front(i, xs_, xn_, hT, hn, Wf, ["act"], tts=(tt,)))

            def proj_unit(ct):
                for kc in range(8):
                    S.op("pe", lambda e, kc=kc: e.matmul(pproj[:, :], w2[:, kc, ct * 128:(ct + 1) * 128], hT[:, kc, :], start=(kc == 0), stop=(kc == 7)),
                         reads=hnames, writes=["pproj"], inc=(kc == 7))
                if ct < 4:
                    S.op("act", lambda e: e.copy(uT_bf[:, ct, :], pproj[:, :]), reads=["pproj"], writes=["uT_bf%d" % cp])
                elif ct < 8:
                    S.op("act", lambda e: e.activation(szs[:, ct - 4, :], pproj[:, :], AF.Silu), reads=["pproj"], writes=["szs%d" % cp])
                else:
                    S.op("act", lambda e: e.activation(sza[:, ct - 8, :], pproj[:, :], AF.Silu), reads=["pproj"], writes=["sza%d" % cp])
            for ct in range(12):
                units.append(lambda ct=ct: proj_unit(ct))
            return units

        for u_ in make_units(0):
            u_()
        pending_tail = []
        for i in range(NCH_RUN):
            t0 = i * CH
            cp = i % 2
            hT, uT_bf, uT_f, szs, sza = hTs[cp], uT_bfs[cp], uT_fs[cp], szss[cp], szas[cp]
            uTbn, uTfn, szsn, szan = "uT_bf%d" % cp, "uT_bf%d" % cp, "szs%d" % cp, "sza%d" % cp
            gy_b = gy_bs[cp]
            gyn = "gy_b%d" % cp
            units = pending_tail + (make_units(i + 1) if i + 1 < NCH_RUN else [])
            pending_tail = []
            for f in range(4):
                fl = f * 128

                def stageA1(gb, fl=fl, uT_bf=uT_bf):
                    par = gb % 2
                    po = par * 256
                    g0 = gb * 2
                    An, Bn = "bA%d" % par, "bB%d" % par
                    pVn, pWn = "pV%d" % par, "pW%d" % par
                    bA, bB = bufA[par], bufB[par]
                    for gi in range(2):
                        g = g0 + gi
                        ctg = g // 8
                        base = 64 * ((g % 8) // 4)
                        S.op("pe", lambda e, g=g, gi=gi, ctg=ctg, base=base: e.matmul(pVW[par][:, gi * 128:(gi + 1) * 128], BT[base:base + 64, g, :], uT_bf[base:base + 64, ctg, fl:fl + 128], start=True, stop=True),
                             reads=[uTbn], writes=[pVn], inc=(gi == 1))
                    for gi in range(2):
                        g = g0 + gi
                        ctg = g // 8
                        base = 64 * ((g % 8) // 4)
                        S.op("pe", lambda e, g=g, gi=gi, ctg=ctg, base=base: e.matmul(pVW[par][:, 256 + gi * 128:256 + (gi + 1) * 128], BTs[base:base + 64, g, :], uT_bf[base:base + 64, ctg, fl:fl + 128], start=True, stop=True),
                             reads=[uTbn], writes=[pWn], inc=(gi == 1))
                    cmv = Cm[:, g0:g0 + 2, :].rearrange("p g m -> p (g m)")
                    smv = Sm[:, g0:g0 + 2, :].rearrange("p g m -> p (g m)")
                    S.op("dve", lambda e: e.tensor_tensor(bA[:], pVW[par][:, 0:256], cmv, ALU.mult), reads=[pVn, pWn], writes=[An])
                    S.op("dve", lambda e: e.tensor_tensor(bB[:], pVW[par][:, 256:512], smv, ALU.mult), reads=[pVn, pWn], writes=[Bn])
                    S.op("pool", lambda e: e.tensor_tensor(bA[:], bA[:], bB[:], ALU.add), reads=[An, Bn], writes=[An])

                def stageA2(gb):
                    par = gb % 2
                    g0 = gb * 2
                    An, Rn, Rbn = "bA%d" % par, "Rt%d" % par, "Rtb%d" % par
                    bA, Rt_, Rtb_ = bufA[par], Rts[par], Rtbs[par]
                    for gi in range(2):
                        g = g0 + gi
                        S.op("dve", lambda e, g=g, gi=gi: e.tensor_tensor_scan(Rt_[:, gi, :], rho[:, g:g + 1].to_broadcast([128, 128]), bA[:, gi * 128:(gi + 1) * 128], carry[:, g:g + 1], ALU.mult, ALU.add),
                             reads=[An, "carry"], writes=[Rn])
                    S.op("act", lambda e: e.copy(Rlast[:, g0:g0 + 2], Rt_[:, :, 127]), reads=[Rn], writes=["Rlast"])
                    rtv = Rt_[:, :, :].rearrange("p g m -> p (g m)")
                    S.op("act", lambda e: e.copy(Rtb_[:], rtv), reads=[Rn], writes=[Rbn])

                def stageB(gb):
                    par = gb % 2
                    po = par * 256
                    g0 = gb * 2
                    Rn, Rbn, pRn = "Rt%d" % par, "Rtb%d" % par, "pR%d" % par
                    Rt_, Rtb_ = Rts[par], Rtbs[par]
                    cmv = Cm[:, g0:g0 + 2, :].rearrange("p g m -> p (g m)")
                    smv = Sm[:, g0:g0 + 2, :].rearrange("p g m -> p (g m)")
                    rtv = Rt_[:, :, :].rearrange("p g m -> p (g m)")
                    S.op("pe", lambda e: e.matmul(pRs[par][:, 0:256], pm_b[:], Rtb_[:], start=True, stop=True), reads=[Rbn], writes=[pRn])
                    S.op("pool", lambda e: e.tensor_tensor(bB16[par][:], rtv, cmv, ALU.mult), reads=[Rn], writes=["bB16_%d" % par])
                    S.op("dve", lambda e: e.tensor_tensor(bC16[par][:], pRs[par][:, 0:256], smv, ALU.mult), reads=[pRn], writes=["bC16_%d" % par])
                    for gi in range(2):
                        g = g0 + gi
                        S.op("pe", lambda e, g=g, gi=gi: e.matmul(pY[:, g * 16:(g + 1) * 16], bB16[par][:, gi * 128:(gi + 1) * 128], Cw[:, g, :], start=True, stop=False),
                             reads=["bB16_%d" % par], writes=["pY"], inc=False)
                        S.op("pe", lambda e, g=g, gi=gi: e.matmul(pY[:, g * 16:(g + 1) * 16], bC16[par][:, gi * 128:(gi + 1) * 128], Cw[:, g, :], start=False, stop=True),
                             reads=["bC16_%d" % par], writes=["pY"], inc=(gi == 1))

                NB = 16
                stageA1(0)
                stageA2(0)
                for gb in range(NB):
                    if gb + 1 < NB:
                        stageA1(gb + 1)
                    stageB(gb)
                    if gb + 1 < NB:
                        stageA2(gb + 1)
                    if gb % 2 == 1 and units:
                        units.pop(0)()
                S.op("pe", lambda e: e.matmul(pRs[0][:, 0:32], pm_f[:], Rlast[:], start=True, stop=True), reads=["pm_f", "Rlast"], writes=["pR0"])
                S.op("dve", lambda e: e.tensor_tensor(ct1[:], Rlast[:], c128[:], ALU.mult), reads=["Rlast", "c128"], writes=["ct1"])
                S.op("dve", lambda e: e.tensor_tensor(ct2[:], pRs[0][:, 0:32], s128[:], ALU.mult), reads=["pR0", "s128"], writes=["ct2"])
                S.op("dve", lambda e: e.tensor_tensor(carry[:], ct1[:], ct2[:], ALU.add), reads=["ct1", "ct2"], writes=["carry"])
                S.op("act", lambda e: e.copy(ysb[:], pY[:, :]), reads=["pY"], writes=["ysb"])
                for ct in range(4):
                    S.op("pe", lambda e, ct=ct: e.transpose(pT[:, ct * 128:(ct + 1) * 128], ysb[:, ct * 128:(ct + 1) * 128], ident_f[:]), reads=["ysb", "ident_f"], writes=["pT"], inc=(ct == 3))
                for ct in range(4):
                    S.op("dve", lambda e, ct=ct, fl=fl, uT_f=uT_f: e.scalar_tensor_tensor(y2[:, ct, :], uT_f[:, ct, fl:fl + 128], dcol[:, ct:ct + 1], pT[:, ct * 128:(ct + 1) * 128], ALU.mult, ALU.add),
                         reads=[uTfn, "dcol", "pT"], writes=["y2"])
                S.op("act", lambda e, fl=fl, gy_b=gy_b: e.activation(gy_b[:, :, fl:fl + 128], y2[:, :, :], AF.Gelu), reads=["y2"], writes=[gyn])
            while units:
                units.pop(0)()

            def make_tail(i, cp, t0):
                gy_b, mix, szs, sza = gy_bs[cp], mixs[cp], szss[cp], szas[cp]
                gyn, szsn, szan = "gy_b%d" % cp, "szs%d" % cp, "sza%d" % cp
                man, msn = "mix_attn%d" % cp, "mix_ssm%d" % cp
                tl = []

                def attn_unit():
                    if WITH_ATTN:
                        av = attn_scr.rearrange("(kc p) t -> p kc t", p=128)
                        S.dma("sp", lambda e: e.dma_start(out=mix[:, 0:4, :], in_=av[:, :, t0:t0 + CH]), reads=["attn_scr"], writes=[man])
                        S.op("pool", lambda e: e.tensor_tensor(mix[:, 0:4, :], mix[:, 0:4, :], sza[:, :, :], ALU.mult), reads=[man, szan], writes=[man])
                tl.append(attn_unit)

                def glu_unit(co):
                    for ct in range(4):
                        S.op("pe", lambda e, ct=ct: e.matmul(pT[:, :], glw[:, ct, co * 128:(co + 1) * 128], gy_b[:, ct, :], start=(ct == 0), stop=(ct == 3)),
                             reads=glwn + [gyn], writes=["pT"], inc=(ct == 3))
                    S.op("act", lambda e: e.activation(sig[:], pT[:, :], AF.Sigmoid, bias=glub[:, co:co + 1]), reads=["pT", "glub"], writes=["sig"])
                    S.op("dve", lambda e: e.tensor_tensor(m1[:], gy_b[:, co, :], sig[:], ALU.mult), reads=[gyn, "sig"], writes=["m1"])
                    S.op("pool", lambda e: e.tensor_tensor(mix[:, 4 + co, :], m1[:], szs[:, co, :], ALU.mult), reads=["m1", szsn], writes=[msn])
                for co in range(4):
                    tl.append(lambda co=co: glu_unit(co))

                def out_unit(tt):
                    gt = 4 * i + tt
                    for hf in range(2):
                        for kc in range(8):
                            S.op("pe", lambda e, hf=hf, kc=kc: e.matmul(pO[hf][:, :], mix[:, kc, tt * 128:(tt + 1) * 128], wout[:, kc, hf * 512:(hf + 1) * 512], start=(kc == 0), stop=(kc == 7)),
                                 reads=woutn + [man, msn], writes=[pOn[hf]], inc=(kc == 7))
                        S.op("dve", lambda e, hf=hf: e.tensor_tensor(res[:, hf * 512:(hf + 1) * 512], pO[hf][:, :], gate_bc[:, hf * 512:(hf + 1) * 512], ALU.mult),
                             reads=[pOn[hf], "gate_bc"], writes=["res%d" % hf])
                    xr = xrs[0]
                    S.dma("sp", lambda e: e.dma_start(out=xr[:], in_=dr["x"][gt * 128:(gt + 1) * 128, :]), writes=["xr0"])
                    S.op("dve", lambda e: e.tensor_tensor(res[:], res[:], xr[:], ALU.add), reads=["res0", "res1", "xr0"], writes=["res0", "res1"])
                    S.dma("sp", lambda e: e.dma_start(out=out[gt * 128:(gt + 1) * 128, :], in_=res[:]), reads=["res0", "res1"], writes=["out"])
                for tt in range(4):
                    tl.append(lambda tt=tt: out_unit(tt))
                return tl

            pending_tail = make_tail(i, cp, t0)
            if i + 1 >= NCH_RUN:
                for u_ in pending_tail:
                    u_()
                pending_tail = []
        S.barrier()
        S.emit()


_CACHE = {}


def kernel(**inputs):
    consts = make_consts()
    params = make_params(inputs)
    specs = input_specs(consts, params)
    nc = build(specs)
    x = np.asarray(inputs["x"], np.float32)
    c = np.asarray(inputs["c"], np.float32)
    in_maps = []
    for b in range(8):
        m = {"x": np.ascontiguousarray(x[b]), "c_col": np.ascontiguousarray(c[b].reshape(8, 128).T)}
        m.update(consts)
        m.update(params)
        in_maps.append(m)
    res = run_bass_kernel_spmd(nc, in_maps, core_ids=list(range(8)))
    return np.stack([np.asarray(r["out"], np.float32) for r in res.results], 0)
```

```python
import math
import numpy as np
import ml_dtypes
from contextlib import ExitStack
import concourse.bass as bass
import concourse.mybir as mybir
from concourse.bass_utils import run_bass_kernel_spmd

F32 = mybir.dt.float32
BF16 = mybir.dt.bfloat16
I32 = mybir.dt.int32
AF = mybir.ActivationFunctionType
ALU = mybir.AluOpType
AX = mybir.AxisListType

T = 8192
D = 1024
CH = 512
NCH = T // CH
EPS = 1e-6
BIG = 30000.0
NW1 = 1312
WITH_ATTN = True
WITH_SSM = True
NCH_RUN = NCH
STAGE = 99

ENGS = ["pe", "act", "dve", "pool", "sp"]


class Sched:
    def __init__(self, nc, ctx, n_dma_sems=12):
        self.nc = nc
        self.prog = {e: [] for e in ENGS}
        self.sem = {e: ctx.enter_context(nc.semaphore("s_" + e)) for e in ENGS}
        self.cnt = {e: 0 for e in ENGS}
        self.seen = {e: {} for e in ENGS}
        self.last_w = {}
        self.readers = {}
        self.dsem, self.dcnt, self.dnext = {}, {}, {}
        for q in ["sp", "act"]:
            self.dsem[q] = [ctx.enter_context(nc.semaphore("d_%s%d" % (q, i))) for i in range(n_dma_sems)]
            self.dcnt[q] = [0] * n_dma_sems
            self.dnext[q] = 0
        self.semobj = {}
        for e in ENGS:
            self.semobj[("e", e)] = self.sem[e]
        for q in self.dsem:
            for i, s in enumerate(self.dsem[q]):
                self.semobj[("d", q, i)] = s

    def _waits_for(self, eng, toks):
        need = {}
        for (k, v) in toks:
            if k == ("e", eng) and (v > self.cnt[eng] or eng == "pe"):
                continue
            if self.seen[eng].get(k, 0) < v:
                need[k] = max(need.get(k, 0), v)
        for k, v in need.items():
            self.seen[eng][k] = v
        return list(need.items())

    def _deps(self, reads, writes):
        toks = []
        for r in reads:
            t = self.last_w.get(r)
            if t is not None:
                toks.append(t)
        for w in writes:
            t = self.last_w.get(w)
            if t is not None:
                toks.append(t)
            toks.extend(self.readers.get(w, []))
        return toks

    def _commit(self, tok, reads, writes):
        for r in reads:
            self.readers.setdefault(r, []).append(tok)
        for w in writes:
            self.last_w[w] = tok
            self.readers[w] = []

    def op(self, eng, fn, reads=(), writes=(), inc=True):
        toks = self._deps(reads, writes)
        waits = self._waits_for(eng, toks)
        if inc:
            self.cnt[eng] += 1
            tok = (("e", eng), self.cnt[eng])
        else:
            tok = (("e", eng), self.cnt[eng] + 1)
        self.prog[eng].append((waits, fn, ("e", eng) if inc else None, 1))
        self._commit(tok, reads, writes)
        return tok

    def dma(self, q, fn, reads=(), writes=()):
        toks = self._deps(reads, writes)
        i = self.dnext[q]
        self.dnext[q] = (i + 1) % len(self.dsem[q])
        key = ("d", q, i)
        if self.dcnt[q][i] > 0:
            toks.append((key, 16 * self.dcnt[q][i]))
        waits = self._waits_for(q, toks)
        self.dcnt[q][i] += 1
        tok = (key, 16 * self.dcnt[q][i])
        self.prog[q].append((waits, fn, key, 16))
        self._commit(tok, reads, writes)
        return tok

    def final_wait(self, eng, toks):
        waits = self._waits_for(eng, toks)
        self.prog[eng].append((waits, None, None, 0))

    def barrier(self):
        toks = [(("e", e), self.cnt[e]) for e in ENGS if self.cnt[e] > 0]
        for q in self.dsem:
            for i in range(len(self.dsem[q])):
                if self.dcnt[q][i] > 0:
                    toks.append((("d", q, i), 16 * self.dcnt[q][i]))
        for e in ENGS:
            self.final_wait(e, toks)

    def emit(self):
        nc = self.nc
        prog = self.prog
        semobj = self.semobj

        def run(e_obj, lst):
            for waits, fn, key, amt in lst:
                for k, v in waits:
                    e_obj.wait_ge(semobj[k], v)
                if fn is None:
                    continue
                ins = fn(e_obj)
                if key is not None:
                    ins.then_inc(semobj[key], amt)

        with nc.Block() as block:
            @block.tensor
            def _(e):
                run(e, prog["pe"])

            @block.scalar
            def _(e):
                run(e, prog["act"])

            @block.vector
            def _(e):
                run(e, prog["dve"])

            @block.gpsimd
            def _(e):
                run(e, prog["pool"])

            @block.sync
            def _(e):
                run(e, prog["sp"])
        self.prog = {e: [] for e in ENGS}


def _bf(a):
    return np.ascontiguousarray(a).astype(ml_dtypes.bfloat16)


def make_consts():
    c = {}
    c["ident_bf"] = _bf(np.eye(128, dtype=np.float32))
    c["ident_f"] = np.eye(128, dtype=np.float32)
    ob = np.zeros((128, 128), np.float32)
    ob[:64, :64] = 1.0
    ob[64:, 64:] = 1.0
    c["onesblk_f"] = ob
    pm = np.zeros((128, 128), np.float32)
    for j in range(64):
        pm[64 + j, j] = -1.0
        pm[j, 64 + j] = 1.0
    c["pm_f"] = pm
    c["mrow"] = np.tile(np.arange(128, dtype=np.float32)[None, :], (128, 1))
    inv = 1.0 / (10000.0 ** (np.arange(0, 64, 2, dtype=np.float32) / 64.0))
    t = np.arange(T, dtype=np.float32)
    ang = t[None, :] * inv[:, None].astype(np.float32)
    cos32 = np.cos(ang).astype(np.float32)
    sin32 = np.sin(ang).astype(np.float32)
    cos64 = np.concatenate([cos32, cos32], 0)
    sin64 = np.concatenate([-sin32, sin32], 0)
    cosT = np.concatenate([cos64, cos64], 0)
    sinS = np.concatenate([sin64, sin64], 0)
    c["cosT"] = np.ascontiguousarray(cosT.reshape(128, NCH, CH).transpose(1, 0, 2))
    c["sinS"] = np.ascontiguousarray(sinS.reshape(128, NCH, CH).transpose(1, 0, 2))
    kl = np.arange(128)[:, None]
    tl = np.arange(CH)[None, :]
    cm = np.zeros((4, 128, CH), np.float32)
    wl = np.zeros((4, 128, CH), np.float32)
    pmk = np.zeros((4, 128, CH), np.float32)
    for v in range(4):
        cm[v] = np.where(128 * v + kl > tl, -BIG, 0.0)
        wl[v] = np.where(128 * v + kl <= tl, -BIG, 0.0)
        pmk[v] = np.where(16 * kl + 31 > 512 * v + tl, -BIG, 0.0)
    c["cmask"] = _bf(cm)
    c["wlo"] = _bf(wl)
    c["cmpmask"] = _bf(pmk)
    A = np.zeros((512, 128), np.float32)
    for j in range(128):
        for m in range(4):
            for n in range(2):
                idx = 4 * j + m - n
                if 0 <= idx < 511:
                    A[idx, j] += 1.0
    c["ZA"] = _bf(A.reshape(4, 128, 128))
    r = np.arange(128)[:, None]
    rel = np.arange(256)[None, :] - 126
    cur = r // 64
    fadd = np.zeros((128, 256), np.float32)
    fadd = np.where(rel > cur, -1e4, fadd)
    fadd = np.where((rel == cur) | (rel == cur - 1), 1e4, fadd)
    c["Fadd"] = fadd.astype(np.float32)
    c["Finv"] = np.where(rel > cur, -BIG, 0.0).astype(np.float32)
    key = np.arange(T)[None, :]
    jj = np.arange(64)[:, None]
    c["Epat"] = _bf(((key // 64) % 64 == jj).astype(np.float32))
    gs = np.zeros((32, 24, 64), np.float32)
    for k in range(24):
        gs[k, k, :] = 1.0
    c["Gsel"] = gs
    c["ones_row"] = np.ones((1, 128), np.float32)
    pr = np.zeros((128, 128), np.float32)
    for e in range(2):
        for d in range(64):
            pr[e * 64 + (d + 32) % 64, e * 64 + d] = 1.0
    c["Prot"] = _bf(pr)
    return c


def make_params(inp):
    p = {}
    f = lambda a: np.ascontiguousarray(np.asarray(a, dtype=np.float32))
    l = 0
    w_in = f(inp["w_in"][l])
    q = w_in[:, 0:512]
    kcr = w_in[:, 512:640]
    vcr = w_in[:, 640:768]
    ksl = w_in[:, 768:896]
    vsl = w_in[:, 896:1024]
    kwn = w_in[:, 1024:1152]
    vwn = w_in[:, 1152:1280]
    z_a = w_in[:, 1280:1792]
    g_br = w_in[:, 1792:1816]
    u = w_in[:, 1816:2328]
    z_s = w_in[:, 2328:2840]

    gpad = np.concatenate([g_br, np.zeros((1024, 8), np.float32)], 1)
    W1 = np.concatenate([q, ksl, kwn, kcr, vcr, gpad, vsl, vwn], 1)
    assert W1.shape[1] == NW1
    p["W1"] = f(W1)
    p["W2"] = f(np.concatenate([u, z_s, z_a], 1))
    p["w_ada"] = f(inp["w_ada"][l])
    p["bada_col"] = f(inp["b_ada"][l].reshape(24, 128).T)
    p["bada_grow"] = f(inp["b_ada"][l][2048:3072].reshape(1, 1024))
    p["ng_col"] = f(inp["norm_g"][l].reshape(8, 128).T)

    def gcols(g):
        g = np.asarray(g, np.float32)
        return f(np.stack([np.tile(g, 2), np.tile(g[(np.arange(64) + 32) % 64], 2)], 1))

    p["gq"] = gcols(inp["q_norm_g"][l])
    p["gks"] = gcols(inp["k_slc_norm_g"][l])
    p["gkw"] = gcols(inp["k_win_norm_g"][l])
    p["gkc_bc"] = f(np.tile(np.asarray(inp["k_cmp_norm_g"][l], np.float32)[None, :], (128, 1)))
    p["posk_col"] = f(np.asarray(inp["cmp_pos_k"][l]).reshape(16, 128).T)
    p["posv_col"] = f(np.asarray(inp["cmp_pos_v"][l]).reshape(16, 128).T)
    p["w1k"] = f(inp["cmp_w1_k"][l])
    p["w1v"] = f(inp["cmp_w1_v"][l])
    p["w2k"] = f(inp["cmp_w2_k"][l])
    p["w2v"] = f(inp["cmp_w2_v"][l])
    a_re = np.asarray(inp["ssm_a_re"][l], np.float32)
    a_im = np.asarray(inp["ssm_a_im"][l], np.float32)
    ldt = np.asarray(inp["ssm_log_dt"][l], np.float32)
    p["are2"] = f(np.concatenate([a_re.T, a_re.T], 0))
    p["aim2"] = f(np.concatenate([a_im.T, a_im.T], 0))
    p["ldt2"] = f(np.tile(ldt[None, :], (128, 1)))
    p["b_are"] = f(np.tile(a_re[None, :, :], (128, 1, 1)))
    p["b_aim"] = f(np.tile(a_im[None, :, :], (128, 1, 1)))
    p["b_ldt"] = f(np.tile(ldt[None, :, None], (128, 1, 64)))
    b_re = np.asarray(inp["ssm_b_re"][l], np.float32)
    b_im = np.asarray(inp["ssm_b_im"][l], np.float32)
    bre_l = np.zeros((128, 32, 64), np.float32)
    bim_l = np.zeros((128, 32, 64), np.float32)
    for g in range(32):
        k0 = 16 * (g % 8)
        bre_l[k0:k0 + 16, g, :] = b_re[g].T
        bim_l[k0:k0 + 16, g, :] = b_im[g].T
    p["b_bre"] = bre_l
    p["b_bim"] = bim_l
    c_re = np.asarray(inp["ssm_c_re"][l], np.float32)
    c_im = np.asarray(inp["ssm_c_im"][l], np.float32)
    p["cw_l"] = f(np.concatenate([c_re.transpose(2, 0, 1), c_im.transpose(2, 0, 1)], 0))
    p["dcol"] = f(np.asarray(inp["ssm_d"][l]).reshape(4, 128).T)
    p["glub_col"] = f(np.asarray(inp["glu_b"][l]).reshape(4, 128).T)
    p["glu_w"] = f(inp["glu_w"][l])
    p["w_out"] = f(inp["w_out"][l])
    return p


IN_SPECS = None


def input_specs(consts, params):
    specs = {"x": ((T, D), F32), "c_col": ((128, 8), F32)}
    for d in (consts, params):
        for k, v in d.items():
            specs[k] = (tuple(v.shape), BF16 if v.dtype == ml_dtypes.bfloat16 else F32)
    return specs


def build(specs, dbg=None):
    nc = bass.Bass("TRN2", target_bir_lowering=False)
    dr = {}
    for name, (shape, dt) in specs.items():
        dr[name] = nc.dram_tensor(name, list(shape), dt, kind="ExternalInput").ap()
    out = nc.dram_tensor("out", [T, D], F32, kind="ExternalOutput").ap()
    attn_scr = nc.dram_tensor("attn_scr", [512, T], BF16, kind="Internal").ap()
    dr["_zscr"] = nc.dram_tensor("zscr", [4, 512], F32, kind="Internal").ap()
    dr["_gscr"] = nc.dram_tensor("gscr", [2, 32, 512], F32, kind="Internal").ap()
    dbg_out = {}
    if dbg:
        for name, (shape, dt) in dbg.items():
            dbg_out[name] = nc.dram_tensor(name, list(shape), dt, kind="ExternalOutput").ap()

    with ExitStack() as ctx0:
        S = Sched(nc, ctx0)
        uid = [0]

        def U(prefix):
            uid[0] += 1
            return "%s_%d" % (prefix, uid[0])

        def load(ctx, name, shape, dt, src_ap, q="sp"):
            t = ctx.enter_context(nc.sbuf_tensor("sb_" + name, list(shape), dt))
            S.dma(q, lambda e: e.dma_start(out=t[:], in_=src_ap), writes=[name])
            return t

        def load_cast(t, name, shape3, src_w, stage, col0=0):
            kcs, ncols = shape3[1], shape3[2]
            srcv = src_w.rearrange("(kc p) n -> p kc n", p=128)
            per = max(1, 2048 // kcs)
            c = 0
            pi = 0
            while c < ncols:
                w = min(per, ncols - c)
                st = stage[pi % 2]
                rn = "stage%d" % (pi % 2)
                S.dma("sp", lambda e, st=st, c=c, w=w: e.dma_start(out=st[:, 0:kcs * w].rearrange("p (k n) -> p k n", k=kcs), in_=srcv[:, :, col0 + c:col0 + c + w]), writes=[rn])
                eng = ["dve", "pool", "act"][pi % 3]
                if eng == "act":
                    S.op(eng, lambda e, st=st, c=c, w=w: e.copy(t[:, :, c:c + w], st[:, 0:kcs * w].rearrange("p (k n) -> p k n", k=kcs)), reads=[rn], writes=[name])
                else:
                    S.op(eng, lambda e, st=st, c=c, w=w: e.tensor_copy(t[:, :, c:c + w], st[:, 0:kcs * w].rearrange("p (k n) -> p k n", k=kcs)), reads=[rn], writes=[name + "_%d" % pi])
                c += w
                pi += 1
            return t, [name] + [name + "_%d" % j for j in range(pi)]

        ident_bf = load(ctx0, "ident_bf", [128, 128], BF16, dr["ident_bf"][:, :])
        ident_f = load(ctx0, "ident_f", [128, 128], F32, dr["ident_f"][:, :])
        c_col = load(ctx0, "c_col", [128, 8], F32, dr["c_col"][:, :])
        bada_col = load(ctx0, "bada_col", [128, 24], F32, dr["bada_col"][:, :])
        ng_col = load(ctx0, "ng_col", [128, 8], F32, dr["ng_col"][:, :])
        bada_grow = load(ctx0, "bada_grow", [1, 1024], F32, dr["bada_grow"][:, :])
        ones_row = load(ctx0, "ones_row", [1, 128], F32, dr["ones_row"][:, :])
        gs_col = ctx0.enter_context(nc.sbuf_tensor("gs_col", [128, 8], F32))
        sh_col = ctx0.enter_context(nc.sbuf_tensor("sh_col", [128, 8], F32))
        gate_bc = ctx0.enter_context(nc.sbuf_tensor("gate_bc", [128, 1024], F32))

        with ExitStack() as c0:
            sc_col = c0.enter_context(nc.sbuf_tensor("sc_col", [128, 8], F32))
            mod_col = c0.enter_context(nc.sbuf_tensor("mod_col", [128, 24], F32))
            grow = c0.enter_context(nc.sbuf_tensor("grow", [1, 1024], F32))
            wst = [c0.enter_context(nc.sbuf_tensor("wst%d" % i, [128, 8, 128], F32)) for i in range(2)]
            pmod = c0.enter_context(nc.psum_tensor("pmod", [128, 512], F32))
            prow = c0.enter_context(nc.psum_tensor("prow", [128, 512], F32))
            pbc = c0.enter_context(nc.psum_tensor("pbc", [128, 512], F32))
            S.op("act", lambda e: e.activation(sc_col[:], c_col[:], AF.Silu), reads=["c_col"], writes=["sc_col"])
            wv = dr["w_ada"].rearrange("(kc p) n -> p kc n", p=128)
            for jc in range(24):
                st = wst[jc % 2]
                rn = "wst%d" % (jc % 2)
                S.dma("sp", lambda e, st=st, jc=jc: e.dma_start(out=st[:], in_=wv[:, :, jc * 128:(jc + 1) * 128]), writes=[rn])
                for kc in range(8):
                    S.op("pe", lambda e, st=st, jc=jc, kc=kc: e.matmul(pmod[:, jc:jc + 1], st[:, kc, :], sc_col[:, kc:kc + 1], start=(kc == 0), stop=(kc == 7)),
                         reads=[rn, "sc_col"], writes=["pmod"], inc=(kc == 7))
                if jc >= 16:
                    j0 = (jc - 16) * 128
                    for kc in range(8):
                        S.op("pe", lambda e, st=st, j0=j0, kc=kc: e.matmul(prow[0:1, (j0 % 512):(j0 % 512) + 128], sc_col[:, kc:kc + 1], st[:, kc, :], start=(kc == 0), stop=(kc == 7)),
                             reads=[rn, "sc_col"], writes=["prow"], inc=(kc == 7))
                    if jc in (19, 23):
                        h0 = 0 if jc == 19 else 512
                        S.op("dve", lambda e, h0=h0: e.tensor_tensor(grow[0:1, h0:h0 + 512], prow[0:1, 0:512], bada_grow[0:1, h0:h0 + 512], ALU.add),
                             reads=["prow", "bada_grow"], writes=["grow"])
            S.op("dve", lambda e: e.tensor_tensor(mod_col[:], pmod[:, 0:24], bada_col[:], ALU.add), reads=["pmod", "bada_col"], writes=["mod_col"])
            S.op("dve", lambda e: e.scalar_tensor_tensor(gs_col[:], mod_col[:, 8:16], 1.0, ng_col[:], ALU.add, ALU.mult), reads=["mod_col", "ng_col"], writes=["gs_col"])
            S.op("dve", lambda e: e.tensor_copy(sh_col[:], mod_col[:, 0:8]), reads=["mod_col"], writes=["sh_col"])
            for h0 in (0, 512):
                S.op("pe", lambda e, h0=h0: e.matmul(pbc[:, 0:512], ones_row[0:1, :], grow[0:1, h0:h0 + 512], start=True, stop=True), reads=["ones_row", "grow"], writes=["pbc"])
                S.op("dve", lambda e, h0=h0: e.tensor_copy(gate_bc[:, h0:h0 + 512], pbc[:, 0:512]), reads=["pbc"], writes=["gate_bc"])
            S.barrier()
            S.emit()

        def front(i, xt_tiles, xt_names, hT, hname, W, evac_eng, tts=(0, 1, 2, 3)):
            for tt in tts:
                gt = 4 * i + tt
                xt = xt_tiles[tt]
                xn_ = xt_names[tt]
                S.dma("sp", lambda e, xt=xt, gt=gt: e.dma_start(out=xt[:], in_=dr["x"][gt * 128:(gt + 1) * 128, :]), writes=[xn_])
                S.op("act", lambda e, xt=xt: e.activation(W["junk"][:], xt[:], AF.Square, accum_out=W["ssq"][:, 0:1]), reads=[xn_], writes=["xn", "ssq"])
                S.op("dve", lambda e: e.tensor_scalar(W["ssq"][:, 1:2], W["ssq"][:, 0:1], 1.0 / D, EPS, ALU.mult, ALU.add), reads=["ssq"], writes=["ssq1"])
                S.op("act", lambda e: e.activation(W["ssq"][:, 2:3], W["ssq"][:, 1:2], AF.Ln), reads=["ssq1"], writes=["ssq2"])
                S.op("act", lambda e: e.activation(W["ssq"][:, 3:4], W["ssq"][:, 2:3], AF.Exp, scale=-0.5), reads=["ssq2"], writes=["ssq3"])
                S.op("dve", lambda e, xt=xt: e.tensor_scalar(W["xn"][:], xt[:], W["ssq"][:, 3:4], None, ALU.mult), reads=[xn_, "ssq3"], writes=["xn"])
                for half in range(2):
                    for j in range(4):
                        kc = half * 4 + j
                        S.op("pe", lambda e, kc=kc, j=j: e.matmul(W["ptr"][:, j * 128:(j + 1) * 128], W["xn"][:, kc * 128:(kc + 1) * 128], ident_bf[:], start=True, stop=True),
                             reads=["xn", "ident_bf"], writes=[W.get("ptrn", "ptr")], inc=(j == 3))
                    for j in range(4):
                        kc = half * 4 + j
                        eng = evac_eng[kc % len(evac_eng)]
                        if eng == "act":
                            S.op("act", lambda e, kc=kc, j=j, tt=tt: e.activation(hT[:, kc, tt * 128:(tt + 1) * 128], W["ptr"][:, j * 128:(j + 1) * 128], AF.Identity,
                                                                                 bias=sh_col[:, kc:kc + 1], scale=gs_col[:, kc:kc + 1]),
                                 reads=[W.get("ptrn", "ptr"), "gs_col", "sh_col"], writes=[(hname, "act")])
                        else:
                            S.op("dve", lambda e, kc=kc, j=j, tt=tt: e.tensor_scalar(hT[:, kc, tt * 128:(tt + 1) * 128], W["ptr"][:, j * 128:(j + 1) * 128],
                                                                                    gs_col[:, kc:kc + 1], sh_col[:, kc:kc + 1], ALU.mult, ALU.add),
                                 reads=[W.get("ptrn", "ptr"), "gs_col", "sh_col"], writes=[(hname, "dve")])
            return [(hname, "act"), (hname, "dve")]

        if WITH_ATTN:
            pass1(nc, S, dr, attn_scr, dbg_out, front, load, load_cast, ident_bf, ident_f)

        pass2(nc, S, dr, out, attn_scr, dbg_out, front, load, load_cast, ident_bf, ident_f, gate_bc)
    return nc


def pass1(nc, S, dr, attn_scr, dbg_out, front, load, load_cast, ident_bf, ident_f):
    with ExitStack() as c1:
        def sb(name, shape, dt=F32):
            return c1.enter_context(nc.sbuf_tensor("a_" + name, list(shape), dt))

        def psb(name, shape=(128, 512), dt=F32):
            return c1.enter_context(nc.psum_tensor("a_" + name, list(shape), dt))

        w1s = sb("w1s", [128, 8, NW1], BF16)
        cw1 = [sb("cw1k", [128, 16, 256], BF16), sb("cw1v", [128, 16, 256], BF16)]
        cw2 = [sb("cw2k", [128, 2, 64], BF16), sb("cw2v", [128, 2, 64], BF16)]
        Kaug = [sb("Kaug0", [128, T], BF16), sb("Kaug1", [128, T], BF16)]
        Vs = sb("Vs", [128, 64, 2, 65], BF16)
        Kw = sb("Kw", [128, 2, 1024], BF16)
        Vw = sb("Vw", [128, 8, 2, 65], BF16)
        kcT = sb("kcT", [128, 2, 512], BF16)
        Vc = sb("Vc", [128, 4, 2, 65], BF16)
        Xs = [sb("Xk", [128, 2, 1056], BF16), sb("Xv", [128, 2, 1056], BF16)]
        posb = sb("posb", [128, 2, 2])
        ones64 = sb("ones64", [128, 64])
        cmask = load(c1, "cmask", [128, 4, 512], BF16, dr["cmask"].rearrange("v p t -> p v t"))
        wlo = load(c1, "wlo", [128, 4, 512], BF16, dr["wlo"].rearrange("v p t -> p v t"))
        ZA = load(c1, "ZA", [128, 4, 128], BF16, dr["ZA"].rearrange("c p j -> p c j"))
        Fadd = load(c1, "Fadd", [128, 256], F32, dr["Fadd"][:, :])
        Finv = load(c1, "Finv", [128, 256], F32, dr["Finv"][:, :])
        Prot = load(c1, "Prot", [128, 128], BF16, dr["Prot"][:, :])
        onesblk = load(c1, "onesblk_f", [128, 128], F32, dr["onesblk_f"][:, :])
        gcol = [load(c1, nm, [128, 2], F32, dr[nm][:, :]) for nm in ("gq", "gks", "gkw")]
        gkc_bc = load(c1, "gkc_bc", [128, 64], F32, dr["gkc_bc"][:, :])
        posc = [load(c1, "posk_col", [128, 16], F32, dr["posk_col"][:, :]), load(c1, "posv_col", [128, 16], F32, dr["posv_col"][:, :])]
        posc_b = [sb("posk_b", [128, 16], BF16), sb("posv_b", [128, 16], BF16)]

        csu = ExitStack()
        stage = [csu.enter_context(nc.sbuf_tensor("a_stage0", [128, 2048], F32)), csu.enter_context(nc.sbuf_tensor("a_stage1", [128, 2048], F32))]
        ppos = csu.enter_context(nc.psum_tensor("a_ppos", [128, 512], F32))
        load_cast(w1s, "w1s", [128, 8, NW1], dr["W1"], stage)
        load_cast(cw1[0], "cw1k", [128, 16, 256], dr["w1k"], stage)
        load_cast(cw1[1], "cw1v", [128, 16, 256], dr["w1v"], stage)
        load_cast(cw2[0], "cw2k", [128, 2, 64], dr["w2k"], stage)
        load_cast(cw2[1], "cw2v", [128, 2, 64], dr["w2v"], stage)
        S.barrier()
        for kv in range(2):
            S.op("dve", lambda e, kv=kv: e.tensor_copy(posc_b[kv][:], posc[kv][:]), writes=["posc_b%d" % kv])
            for hc in range(2):
                for lp in range(16):
                    S.op("pe", lambda e, kv=kv, hc=hc, lp=lp: e.matmul(ppos[:, kv * 2 + hc:kv * 2 + hc + 1], cw1[kv][:, lp, hc * 128:(hc + 1) * 128], posc_b[kv][:, lp:lp + 1], start=(lp == 0), stop=(lp == 15)),
                         reads=["posc_b%d" % kv], writes=["ppos"], inc=(lp == 15))
        S.op("dve", lambda e: e.tensor_copy(posb[:, :, :].rearrange("p a b -> p (a b)"), ppos[:, 0:4]), reads=["ppos"], writes=["posb"])
        S.op("pool", lambda e: e.memset(ones64[:], 1.0), writes=["ones64"])
        S.op("pool", lambda e: e.memset(kcT[:], 0.0), writes=["kcT"])
        S.op("pool", lambda e: e.memset(Vc[:], 0.0), writes=["Vc"])
        S.op("pool", lambda e: e.memset(Vc[:, :, :, 64:65], 1.0), writes=["Vc"])
        S.op("pool", lambda e: e.memset(Vs[:, :, :, 64:65], 1.0), writes=["Vs"])
        S.op("pool", lambda e: e.memset(Vw[:], 0.0), writes=["Vw"])
        S.op("pool", lambda e: e.memset(Vw[:, :, :, 64:65], 1.0), writes=["Vw"])
        S.op("pool", lambda e: e.memset(Kw[:], 0.0), writes=["Kw"])
        for kv in range(2):
            S.op("pool", lambda e, kv=kv: e.memset(Xs[kv][:], 0.0), writes=["X%d" % kv])
        for g in range(2):
            S.dma("sp", lambda e, g=g: e.dma_start(out=Kaug[g][64:128, :], in_=dr["Epat"][:, :]), writes=["Kaug%d" % g])
        S.barrier()
        S.emit()
        csu.close()

        xts = [sb("xt0", [128, 1024]), sb("xt1", [128, 1024])]
        xn_t = sb("xn", [128, 1024], BF16)
        Wf = {"junk": xn_t, "ssq": sb("ssq", [128, 4]), "xn": xn_t, "ptr": None}
        hT = sb("hT", [128, 8, 512], BF16)
        cs_t = sb("cs_t", [128, 512])
        sn_t = sb("sn_t", [128, 512])
        cmm_t = sb("cmm_t", [128, 512], BF16)
        tA = sb("tA", [128, 512], BF16)
        onesblk_b = sb("onesblk_b", [128, 128], BF16)
        S.op("dve", lambda e: e.tensor_copy(onesblk_b[:], onesblk[:]), writes=["onesblk_b"])
        tB = sb("tB", [128, 512])
        tC = sb("tC", [128, 512])
        tD = sb("tD", [128, 512])
        gqb = sb("gqb", [128, 512], BF16)
        qrp = sb("qrp", [128, 512], BF16)
        qnp = sb("qnp", [128, 512], BF16)
        Qaug = sb("Qaug", [128, 2, 4, 512], BF16)
        qn = sb("qn", [128, 4, 512], BF16)
        gsb = sb("gsb", [32, 512])
        pTb = [sb("pTb%d" % j, [128, 512], BF16) for j in range(4)]
        zrow = sb("zrow", [128, 2, 512])
        rzbs = [sb("rzb0", [64, 512]), sb("rzb1", [64, 512])]
        tO = sb("tO", [64, 512])
        gbs = [sb("gb%d" % j, [64, 512]) for j in range(3)]
        accH = sb("accH", [64, 4, 512], BF16)
        impg = sb("impg", [128, 4, 128])
        impt = sb("impt", [128, 4, 128])
        rs = sb("rs", [128, 8])
        sc = sb("sc", [128, 128])
        sc2 = sb("sc2", [128, 128])
        m8 = sb("m8", [128, 16])
        selb = sb("selb", [128, 4, 2, 128], BF16)
        selT = sb("selT", [128, 2, 512], BF16)
        hidb = sb("hidb", [128, 2, 2, 32], BF16)
        kcn = sb("kcn", [32, 2, 64], BF16)
        kst = sb("kst", [32, 8])
        pproj = psb("pproj")
        Wf["ptr"] = pproj
        Wf["ptrn"] = "pproj"
        pmisc = psb("pmisc")
        psc = [psb("psc0"), psb("psc1"), psb("psc2")]
        pacc = [psb("pacc0"), psb("pacc1")]
        pimp = psb("pimp")
        cnt = {"sc": 0, "acc": 0, "pt": 0, "pp": 0, "ep": 0, "gb": 0}

        def norm_rope(pproj, ppn, gc, want_qn):
            S.op("act", lambda e: e.activation(tA[:], pproj[:, :], AF.Square), reads=[ppn], writes=["tA"])
            S.op("dve", lambda e: e.tensor_scalar(gqb[:], pproj[:, :], gc[:, 0:1], None, ALU.mult), reads=[ppn, "tA"], writes=["gqb"])
            S.op("pe", lambda e: e.matmul(pmisc[:, :], onesblk_b[:], tA[:], start=True, stop=True), reads=["tA", "onesblk_b"], writes=["pmisc"])
            S.op("act", lambda e: e.activation(tB[:], pmisc[:, :], AF.Ln, bias=EPS_AP[:, 0:1], scale=1.0 / 64), reads=["pmisc", "eps_ap"], writes=["tB"])
            S.op("act", lambda e: e.activation(tB[:], tB[:], AF.Exp, scale=-0.5), reads=["tB"], writes=["tB"])
            S.op("pe", lambda e: e.matmul(pmisc[:, :], Prot[:], gqb[:], start=True, stop=True), reads=["gqb"], writes=["pmisc"])
            S.op("dve", lambda e: e.tensor_tensor(tC[:], gqb[:], cs_t[:], ALU.mult), reads=["gqb", "cs_t"], writes=["tC"])
            S.op("dve", lambda e: e.tensor_tensor(tD[:], pmisc[:, :], sn_t[:], ALU.mult), reads=["pmisc", "sn_t"], writes=["tD"])
            S.op("dve", lambda e: e.tensor_tensor(tC[:], tC[:], tD[:], ALU.add), reads=["tC", "tD"], writes=["tC"])
            S.op("dve", lambda e: e.tensor_tensor(qrp[:], tC[:], tB[:], ALU.mult), reads=["tC", "tB"], writes=["qrp"])
            if want_qn:
                S.op("dve", lambda e: e.tensor_tensor(qnp[:], gqb[:], tB[:], ALU.mult), reads=["gqb", "tB"], writes=["qnp"])

        EPS_AP = sb("eps_ap", [128, 1])
        S.op("pool", lambda e: e.memset(EPS_AP[:], EPS), writes=["eps_ap"])

        pps = [(pproj, "pproj"), (pproj, "pproj")]

        def proj_tile(col0, ncols, hnames):
            pp, ppn = pps[cnt["pp"] % 2]
            cnt["pp"] += 1
            for kc in range(8):
                S.op("pe", lambda e, kc=kc: e.matmul(pp[0:ncols, :], w1s[:, kc, col0:col0 + ncols], hT[:, kc, :], start=(kc == 0), stop=(kc == 7)),
                     reads=hnames, writes=[ppn], inc=(kc == 7))
            return pp, ppn

        def next_sc():
            j = cnt["sc"] % 3
            cnt["sc"] += 1
            return psc[j], "psc%d" % j

        def next_pt():
            j = cnt["pt"] % 4
            cnt["pt"] += 1
            return pTb[j], "pTb%d" % j

        def epilogue(pa, pan, g, hh, br, first, clamp):
            h = 4 * g + hh
            k = 3 * h + br
            if clamp:
                S.op("dve", lambda e: e.tensor_scalar(zrow[64:65, :], pa[64:65, :], 1e-30, None, ALU.max), reads=[pan], writes=["zrow"])
                S.op("dve", lambda e: e.reciprocal(zrow[64:65, :], zrow[64:65, :]), reads=["zrow"], writes=["zrow"])
            else:
                S.op("dve", lambda e: e.reciprocal(zrow[64:65, :], pa[64:65, :]), reads=[pan], writes=["zrow"])
            S.op("pe", lambda e: e.matmul(pmisc[0:64, :], ones64[64:65, 0:64], zrow[64:65, :], start=True, stop=True), reads=["zrow"], writes=["pmisc"])
            S.op("act", lambda e: e.copy(rzb[:], pmisc[0:64, :]), reads=["pmisc"], writes=["rzb"])
            S.op("dve", lambda e: e.tensor_tensor(tO[:], pa[0:64, :], rzb[:], ALU.mult), reads=[pan, "rzb"], writes=["tO"])
            S.op("pe", lambda e, k=k: e.matmul(pmisc[0:64, :], Gsel[:, k, :], gsb[:, :], start=True, stop=True), reads=["gsb", "rzb"], writes=["pmisc"])
            if first:
                S.op("dve", lambda e: e.tensor_tensor(accA[:], pmisc[0:64, :], tO[:], ALU.mult), reads=["pmisc", "tO"], writes=["accA"])
            else:
                S.op("dve", lambda e: e.tensor_tensor(tO2[:], pmisc[0:64, :], tO[:], ALU.mult), reads=["pmisc", "tO"], writes=["tO2"])
                S.op("pool", lambda e: e.tensor_tensor(accA[:], accA[:], tO2[:], ALU.add), reads=["accA", "tO2"], writes=["accA"])

        for i in range(NCH_RUN):
            t0 = i * CH
            hnames = front(i, [xts[tt % 2] for tt in range(4)], ["xt%d" % (tt % 2) for tt in range(4)], hT, "hT", Wf, ["dve"])
            S.dma("sp", lambda e, i=i: e.dma_start(out=cs_t[:], in_=dr["cosT"][i, :, :]), writes=["cs_t"])
            S.dma("sp", lambda e, i=i: e.dma_start(out=sn_t[:], in_=dr["sinS"][i, :, :]), writes=["sn_t"])
            S.dma("sp", lambda e, i=i: e.dma_start(out=cmm_t[:], in_=dr["cmpmask"][i % 4, :, :]), writes=["cmm_t"])
            pp, ppn = proj_tile(512, 128, hnames)
            norm_rope(pp, ppn, gcol[1], False)
            S.op("dve", lambda e, t0=t0: e.tensor_copy(Kaug[0][0:64, t0:t0 + CH], qrp[0:64, :]), reads=["qrp"], writes=["Kaug0"])
            S.op("dve", lambda e, t0=t0: e.tensor_copy(Kaug[1][0:64, t0:t0 + CH], qrp[64:128, :]), reads=["qrp"], writes=["Kaug1"])
            pp, ppn = proj_tile(640, 128, hnames)
            norm_rope(pp, ppn, gcol[2], False)
            w0 = (i % 2) * 512
            S.op("dve", lambda e, w0=w0: e.tensor_copy(Kw[0:64, 0, w0:w0 + CH], qrp[0:64, :]), reads=["qrp"], writes=["Kw"])
            S.op("dve", lambda e, w0=w0: e.tensor_copy(Kw[0:64, 1, w0:w0 + CH], qrp[64:128, :]), reads=["qrp"], writes=["Kw"])
            for kv in range(2):
                X = Xs[kv]
                xn_ = "X%d" % kv
                S.op("dve", lambda e, X=X: e.tensor_copy(X[:, :, 0:512], X[:, :, 512:1024]), reads=[xn_], writes=[xn_])
                pp, ppn = proj_tile(768 + 128 * kv, 128, hnames)
                S.op("act", lambda e, X=X, pp=pp: e.copy(X[0:64, 0, 512:1024], pp[0:64, :]), reads=[ppn], writes=[xn_])
                S.op("act", lambda e, X=X, pp=pp: e.copy(X[64:128, 0, 511:1023], pp[0:64, :]), reads=[ppn], writes=[xn_])
                S.op("act", lambda e, X=X, pp=pp: e.copy(X[0:64, 1, 512:1024], pp[64:128, :]), reads=[ppn], writes=[xn_])
                S.op("act", lambda e, X=X, pp=pp: e.copy(X[64:128, 1, 511:1023], pp[64:128, :]), reads=[ppn], writes=[xn_])
            pp, ppn = proj_tile(1024, 32, hnames)
            S.op("act", lambda e, pp=pp: e.activation(gsb[:, :], pp[0:32, :], AF.Sigmoid), reads=[ppn], writes=["gsb"])
            S.dma("sp", lambda e, i=i: e.dma_start(out=dr["_gscr"][i % 2, :, :], in_=gsb[:, :]), reads=["gsb"], writes=["gscr%d" % (i % 2)])
            for ts in range(4):
                kt = 4 * i + ts
                pp, ppn = pps[cnt["pp"] % 2]
                cnt["pp"] += 1
                for kc in range(8):
                    S.op("pe", lambda e, kc=kc, ts=ts, pp=pp: e.matmul(pp[:, 0:256], hT[:, kc, ts * 128:(ts + 1) * 128], w1s[:, kc, 1056:1312], start=(kc == 0), stop=(kc == 7)),
                         reads=hnames, writes=[ppn], inc=(kc == 7))
                S.op("act", lambda e, kt=kt, pp=pp: e.copy(Vs[:, kt, :, 0:64], pp[:, 0:128].rearrange("p (g d) -> p g d", g=2)), reads=[ppn], writes=["Vs"])
                S.op("act", lambda e, kt=kt, pp=pp: e.copy(Vw[:, kt % 8, :, 0:64], pp[:, 128:256].rearrange("p (g d) -> p g d", g=2)), reads=[ppn], writes=["Vw"])
            for kv in range(2):
                X = Xs[kv]
                xn_ = "X%d" % kv
                for q in ([i - 1, i] if i > 0 else [i]):
                    pos0 = 0 if q == i - 1 else 512
                    for g in range(2):
                        for hc in range(2):
                            c0 = (g * 2 + hc) * 32
                            for lp in range(16):
                                S.op("pe", lambda e, kv=kv, X=X, g=g, hc=hc, lp=lp, pos0=pos0, c0=c0: e.matmul(pmisc[:, c0:c0 + 32], cw1[kv][:, lp, hc * 128:(hc + 1) * 128], X[:, g, pos0 + 2 * lp:pos0 + 2 * lp + 512:16], start=(lp == 0), stop=(lp == 15)),
                                     reads=[xn_], writes=["pmisc"], inc=(lp == 15))
                    for hc in range(2):
                        S.op("act", lambda e, kv=kv, hc=hc: e.activation(hidb[:, :, hc, :], pmisc[:, 0:128].rearrange("p (g h n) -> p g h n", g=2, h=2)[:, :, hc, :], AF.Gelu, bias=posb[:, kv, hc:hc + 1]),
                             reads=["pmisc", "posb"], writes=["hidb"])
                    for g in range(2):
                        for hc in range(2):
                            S.op("pe", lambda e, kv=kv, g=g, hc=hc: e.matmul(pmisc[0:32, 128 + g * 64:128 + (g + 1) * 64], hidb[:, g, hc, :], cw2[kv][:, hc, :], start=(hc == 0), stop=(hc == 1)),
                                 reads=["hidb"], writes=["pmisc"], inc=(hc == 1))
                    cq = q // 4
                    pq = 32 * (q % 4)
                    if kv == 1:
                        S.op("act", lambda e, cq=cq, pq=pq: e.copy(Vc[pq:pq + 32, cq, :, 0:64], pmisc[0:32, 128:256].rearrange("p (g d) -> p g d", g=2)), reads=["pmisc"], writes=["Vc"])
                    else:
                        for g in range(2):
                            S.op("act", lambda e, g=g: e.activation(kcn[:, g, :], pmisc[0:32, 128 + g * 64:128 + (g + 1) * 64], AF.Square, accum_out=kst[:, g:g + 1]), reads=["pmisc"], writes=["kcn", "kst"])
                        S.op("dve", lambda e: e.tensor_scalar(kst[:, 2:4], kst[:, 0:2], 1.0 / 64, EPS, ALU.mult, ALU.add), reads=["kst"], writes=["kst"])
                        S.op("act", lambda e: e.activation(kst[:, 4:6], kst[:, 2:4], AF.Ln), reads=["kst"], writes=["kst"])
                        S.op("act", lambda e: e.activation(kst[:, 6:8], kst[:, 4:6], AF.Exp, scale=-0.5), reads=["kst"], writes=["kst"])
                        for g in range(2):
                            S.op("dve", lambda e, g=g: e.scalar_tensor_tensor(kcn[:, g, :], pmisc[0:32, 128 + g * 64:128 + (g + 1) * 64], kst[:, 6 + g:7 + g], gkc_bc[0:32, :], ALU.mult, ALU.mult),
                                 reads=["pmisc", "kst"], writes=["kcn"])
                        for g in range(2):
                            S.op("pe", lambda e, g=g: e.matmul(pmisc[0:64, 256 + g * 32:256 + (g + 1) * 32], kcn[:, g, :], ident_bf[0:32, 0:32], start=True, stop=True), reads=["kcn"], writes=["pmisc"], inc=(g == 1))
                        n0 = 32 * q
                        S.op("dve", lambda e, n0=n0: e.tensor_copy(kcT[0:64, :, n0:n0 + 32], pmisc[0:64, 256:320].rearrange("p (g n) -> p g n", g=2)), reads=["pmisc"], writes=["kcT"])
            for g in range(2):
                for j2 in range(2):
                    pp, ppn = proj_tile((2 * g + j2) * 128, 128, hnames)
                    norm_rope(pp, ppn, gcol[0], True)
                    for e2 in range(2):
                        hh = 2 * j2 + e2
                        rows = slice(64 * e2, 64 * e2 + 64)
                        S.op("dve", lambda e, hh=hh, rows=rows: e.tensor_copy(Qaug[0:64, 0, hh, :], qrp[rows, :]), reads=["qrp"], writes=["Qaug"])
                        S.op("dve", lambda e, hh=hh, rows=rows: e.tensor_copy(Qaug[0:64, 1, hh, :], qrp[rows, :]), reads=["qrp"], writes=["Qaug"])
                        S.op("dve", lambda e, hh=hh, rows=rows: e.tensor_copy(qn[0:64, hh, :], qnp[rows, :]), reads=["qnp"], writes=["qn"])
                cmax = i // 4

                def new_acc():
                    j = cnt["acc"] % 2
                    cnt["acc"] += 1
                    return pacc[j], "pacc%d" % j

                def mk_epilogue(pa, pan, g, hh, br, first, clamp, final, t0):
                    h = 4 * g + hh
                    k = 3 * h + br
                    an = "accH%d" % hh
                    ei = cnt["ep"]
                    cnt["ep"] += 1
                    zs = ei % 2
                    zsl = ei % 4
                    rzb, rzn = rzbs[zs], "rzb%d" % zs
                    gj = cnt["gb"] % 3
                    cnt["gb"] += 1
                    gb_, gbn = gbs[gj], "gb%d" % gj
                    isl = (t0 // CH) % 2

                    def s0():
                        S.dma("sp", lambda e: e.dma_start(out=gb_[:], in_=dr["_gscr"][isl, k:k + 1, :].partition_broadcast(64)), reads=["gscr%d" % isl], writes=[gbn])
                        if clamp:
                            S.op("dve", lambda e: e.tensor_scalar(zrow[64:65, zs, :], pa[64:65, :], 1e-30, None, ALU.max), reads=[pan], writes=["zrow%d" % zs])
                            S.op("dve", lambda e: e.reciprocal(zrow[64:65, zs, :], zrow[64:65, zs, :]), reads=["zrow%d" % zs], writes=["zrow%d" % zs])
                        else:
                            S.op("dve", lambda e: e.reciprocal(zrow[64:65, zs, :], pa[64:65, :]), reads=[pan], writes=["zrow%d" % zs])
                        S.dma("sp", lambda e: e.dma_start(out=dr["_zscr"][zsl:zsl + 1, :], in_=zrow[64:65, zs, :]), reads=["zrow%d" % zs], writes=["zscr%d" % zsl])
                        S.dma("sp", lambda e: e.dma_start(out=rzb[:], in_=dr["_zscr"][zsl:zsl + 1, :].partition_broadcast(64)), reads=["zscr%d" % zsl], writes=[rzn])

                    def s1():
                        S.op("dve", lambda e: e.tensor_tensor(tO[:], pa[0:64, :], rzb[:], ALU.mult), reads=[pan, rzn], writes=["tO"])
                        if first:
                            S.op("dve", lambda e: e.tensor_tensor(accH[:, hh, :], tO[:], gb_[:], ALU.mult), reads=["tO", gbn], writes=[an])
                        else:
                            S.op("dve", lambda e: e.tensor_tensor(tO[:], tO[:], gb_[:], ALU.mult), reads=["tO", gbn], writes=["tO"])
                            S.op("pool", lambda e: e.tensor_tensor(accH[:, hh, :], accH[:, hh, :], tO[:], ALU.add), reads=[an, "tO"], writes=[an])
                        if final:
                            S.dma("sp", lambda e: e.dma_start(out=attn_scr[h * 64:(h + 1) * 64, t0:t0 + CH], in_=accH[:, hh, :]), reads=[an], writes=["attn_scr"])
                    return [(0, s0), (2, s1)]

                def mk_imp_post(hh):
                    def f():
                        S.op("dve", lambda e: e.tensor_reduce(rs[:, 0:4], pimp[:, :].rearrange("p (s j) -> p s j", s=4), AX.X, ALU.add), reads=["pimp"], writes=["rs"])
                        S.op("dve", lambda e: e.tensor_scalar(rs[:, 4:8], rs[:, 0:4], 0.5, 1e-30, ALU.mult, ALU.max), reads=["rs"], writes=["rs"])
                        S.op("dve", lambda e: e.reciprocal(rs[:, 4:8], rs[:, 4:8]), reads=["rs"], writes=["rs"])
                        tgt = impg if hh == 0 else impt
                        tgn = "impg" if hh == 0 else "impt"
                        S.op("dve", lambda e: e.tensor_tensor(tgt[:], pimp[:, :].rearrange("p (s j) -> p s j", s=4), rs[:, 4:8].rearrange("p (s o) -> p s o", o=1).to_broadcast([128, 4, 128]), ALU.mult),
                             reads=["pimp", "rs"], writes=[tgn])
                        if hh > 0:
                            S.op("pool", lambda e: e.tensor_tensor(impg[:], impg[:], impt[:], ALU.add), reads=["impg", "impt"], writes=["impg"])
                    return f

                def mk_item(kind, g, hh, idx, npairs, pa, pan, arg, i):
                    st = {}

                    def score():
                        ps_, psn = next_sc()
                        pt, ptn = next_pt()
                        st["pt"], st["ptn"] = pt, ptn
                        if kind == "cmp":
                            c = arg
                            last = (c == npairs - 1)
                            S.op("pe", lambda e: e.matmul(ps_[:, :], kcT[0:64, g, c * 128:(c + 1) * 128], qn[0:64, hh, :], start=True, stop=(not last)), reads=["kcT", "qn"], writes=[psn], inc=(not last))
                            if last:
                                S.op("pe", lambda e: e.matmul(ps_[:, :], ident_bf[:], cmm_t[:], start=False, stop=True), reads=["cmm_t"], writes=[psn])
                        elif kind == "slc":
                            kt = arg
                            H = kt // 32
                            diag = kt >= 4 * i
                            S.op("pe", lambda e: e.matmul(ps_[:, :], Kaug[g][:, kt * 128:(kt + 1) * 128], Qaug[:, H, hh, :], start=True, stop=(not diag)), reads=["Kaug%d" % g, "Qaug"], writes=[psn], inc=(not diag))
                            if diag:
                                S.op("pe", lambda e: e.matmul(ps_[:, :], ident_bf[:], cmask[:, kt - 4 * i, :], start=False, stop=True), writes=[psn])
                        else:
                            kt = arg
                            sl = (kt % 8) * 128
                            mk = cmask[:, kt - 4 * i, :] if kt >= 4 * i else wlo[:, kt - 4 * i + 4, :]
                            S.op("pe", lambda e: e.matmul(ps_[:, :], Kw[0:64, g, sl:sl + 128], Qaug[0:64, 0, hh, :], start=True, stop=False), reads=["Kw", "Qaug"], writes=[psn], inc=False)
                            S.op("pe", lambda e: e.matmul(ps_[:, :], ident_bf[:], mk, start=False, stop=True), writes=[psn])
                        S.op("act", lambda e: e.activation(pt[:], ps_[:, :], AF.Exp, scale=0.125), reads=[psn], writes=[ptn])

                    def pv():
                        pt, ptn = st["pt"], st["ptn"]
                        first_ = (idx == 0)
                        last_ = (idx == npairs - 1)
                        if kind == "cmp":
                            c = arg
                            S.op("pe", lambda e: e.matmul(pa[0:65, :], Vc[:, c, g, :], pt[:], start=first_, stop=last_), reads=[ptn, "Vc"], writes=[pan])
                            for ts in range(4):
                                S.op("pe", lambda e, ts=ts: e.matmul(pimp[:, ts * 128:(ts + 1) * 128], pt[:, ts * 128:(ts + 1) * 128], ZA[:, c, :], start=(first_ and ts == 0), stop=(last_ and ts == 3), skip_group_check=True),
                                     reads=[ptn], writes=["pimp"], inc=(ts == 3))
                        elif kind == "slc":
                            kt = arg
                            S.op("pe", lambda e: e.matmul(pa[0:65, :], Vs[:, kt, g, :], pt[:], start=first_, stop=last_), reads=[ptn, "Vs"], writes=[pan])
                        else:
                            kt = arg
                            S.op("pe", lambda e: e.matmul(pa[0:65, :], Vw[:, kt % 8, g, :], pt[:], start=first_, stop=last_), reads=[ptn, "Vw"], writes=[pan])
                    return {"score": score, "pv": pv, "post": []}

                def run_stream(items, D=2):
                    pending = []
                    N = len(items)
                    for n in range(N + D):
                        if n < N:
                            items[n]["score"]()
                        still = []
                        for (due, fn) in pending:
                            if due <= n:
                                fn()
                            else:
                                still.append((due, fn))
                        pending = still
                        if n - D >= 0:
                            it = items[n - D]
                            it["pv"]()
                            for (dl, fn) in it["post"]:
                                if dl == 0:
                                    fn()
                                else:
                                    pending.append((n + dl, fn))
                    for (due, fn) in pending:
                        fn()

                items = []
                for hh in range(4):
                    pa, pan = new_acc()
                    for c in range(cmax + 1):
                        it = mk_item("cmp", g, hh, c, cmax + 1, pa, pan, c, i)
                        if c == cmax:
                            it["post"] = [(0, mk_imp_post(hh))] + mk_epilogue(pa, pan, g, hh, 0, True, True, False, t0)
                        items.append(it)
                run_stream(items)
                for ts in range(4):
                    tsg = 4 * i + ts
                    off = 126 - 2 * tsg
                    S.op("dve", lambda e, ts=ts, off=off: e.tensor_tensor(sc[:], impg[:, ts, :], Fadd[:, off:off + 128], ALU.add), reads=["impg"], writes=["sc"])
                    S.op("dve", lambda e: e.tensor_scalar(sc[:, 0:1], sc[:, 0:1], 1e4, None, ALU.add), reads=["sc"], writes=["sc"])
                    S.op("dve", lambda e: e.max(out=m8[:, 0:8], in_=sc[:]), reads=["sc"], writes=["m8"])
                    S.op("dve", lambda e: e.match_replace(out=sc2[:], in_to_replace=m8[:, 0:8], in_values=sc[:], imm_value=-3e4), reads=["sc", "m8"], writes=["sc2"])
                    S.op("dve", lambda e: e.max(out=m8[:, 8:16], in_=sc2[:]), reads=["sc2"], writes=["m8"])
                    S.op("dve", lambda e: e.tensor_scalar(sc2[:], sc[:], m8[:, 15:16], BIG, ALU.is_ge, ALU.mult), reads=["sc", "m8"], writes=["sc2"])
                    S.op("dve", lambda e, off=off: e.scalar_tensor_tensor(sc2[:], sc2[:], -BIG, Finv[:, off:off + 128], ALU.add, ALU.add), reads=["sc2"], writes=["sc2"])
                    S.op("dve", lambda e, ts=ts: e.tensor_copy(selb[:, ts, 0, :], sc2[:]), reads=["sc2"], writes=["selb"])
                    S.op("dve", lambda e, ts=ts: e.tensor_copy(selb[:, ts, 1, 0:64], sc2[:, 64:128]), reads=["sc2"], writes=["selb"])
                    S.op("dve", lambda e, ts=ts: e.tensor_copy(selb[:, ts, 1, 64:128], sc2[:, 0:64]), reads=["sc2"], writes=["selb"])
                items = []
                kts = [kt for kt in range(4 * i - 4, 4 * i + 4) if kt >= 0]
                for hh in range(4):
                    pa, pan = new_acc()
                    for idx, kt in enumerate(kts):
                        it = mk_item("win", g, hh, idx, len(kts), pa, pan, kt, i)
                        if idx == len(kts) - 1:
                            it["post"] = mk_epilogue(pa, pan, g, hh, 2, False, False, False, t0)
                        items.append(it)
                run_stream(items)
                for ts in range(4):
                    S.op("pe", lambda e, ts=ts: e.matmul(pmisc[:, 0:128], selb[:, ts, 1, :], ident_bf[:], start=True, stop=True), reads=["selb"], writes=["pmisc"], inc=False)
                    S.op("pe", lambda e, ts=ts: e.matmul(pmisc[:, 128:256], selb[:, ts, 0, :], ident_bf[:], start=True, stop=True), reads=["selb"], writes=["pmisc"])
                    S.op("dve", lambda e, ts=ts: e.tensor_copy(selT[64:128, :, ts * 128:(ts + 1) * 128], pmisc[64:128, 0:256].rearrange("p (h t) -> p h t", h=2)), reads=["pmisc"], writes=["selT"])
                for hh in range(4):
                    S.op("dve", lambda e, hh=hh: e.tensor_copy(Qaug[64:128, :, hh, :], selT[64:128, :, :]), reads=["selT"], writes=["Qaug"])
                items = []
                nkt = 4 * i + 4
                for hh in range(4):
                    pa, pan = new_acc()
                    for kt in range(nkt):
                        it = mk_item("slc", g, hh, kt, nkt, pa, pan, kt, i)
                        if kt == nkt - 1:
                            it["post"] = mk_epilogue(pa, pan, g, hh, 1, False, False, True, t0)
                        items.append(it)
                run_stream(items)
        S.barrier()
        S.emit()


def pass2(nc, S, dr, out, attn_scr, dbg_out, front, load, load_cast, ident_bf, ident_f, gate_bc):
    PI = math.pi
    with ExitStack() as c2:
        def sb(name, shape, dt=F32):
            return c2.enter_context(nc.sbuf_tensor(name, list(shape), dt))

        def psb(name, shape=(128, 512), dt=F32):
            return c2.enter_context(nc.psum_tensor(name, list(shape), dt))

        csu = ExitStack()
        def sbt(name, shape, dt=F32):
            return csu.enter_context(nc.sbuf_tensor(name, list(shape), dt))
        w2 = sb("w2", [128, 8, 1536], BF16)
        wout = sb("wout", [128, 8, 1024], BF16)
        glw = sb("glw", [128, 4, 512], BF16)
        pm_f = load(c2, "pm_f", [128, 128], F32, dr["pm_f"][:, :])
        mrow = load(c2, "mrow", [128, 128], F32, dr["mrow"][:, :])
        are2 = load(c2, "are2", [128, 32], F32, dr["are2"][:, :])
        aim2 = load(c2, "aim2", [128, 32], F32, dr["aim2"][:, :])
        ldt2 = load(c2, "ldt2", [128, 32], F32, dr["ldt2"][:, :])
        cw_l = load(c2, "cw_l", [128, 32, 16], F32, dr["cw_l"][:, :, :])
        dcol = load(c2, "dcol", [128, 4], F32, dr["dcol"][:, :])
        glub = load(c2, "glub", [128, 4], F32, dr["glub_col"][:, :])

        Cm = sb("Cm", [128, 32, 128])
        Sm = sb("Sm", [128, 32, 128])
        rho = sb("rho", [128, 32])
        th = sb("th", [128, 32])
        dt2 = sb("dt2", [128, 32])
        c128 = sb("c128", [128, 32])
        s128 = sb("s128", [128, 32])
        BT = sb("BT", [128, 32, 128], BF16)
        BTs = sb("BTs", [128, 32, 128], BF16)
        Cw = sb("Cw", [128, 32, 16], BF16)
        carry = sb("carry", [128, 32])
        Rlast = sb("Rlast", [128, 32])

        stage = [sbt("stage0", [128, 2048]), sbt("stage1", [128, 2048])]
        _, w2n = load_cast(w2, "w2", [128, 8, 1536], dr["W2"], stage)
        _, woutn = load_cast(wout, "wout", [128, 8, 1024], dr["w_out"], stage)
        _, glwn = load_cast(glw, "glw", [128, 4, 512], dr["glu_w"], stage)
        tA = sbt("tA", [128, 1024])
        tB = sbt("tB", [128, 1024])
        tC = sbt("tC", [128, 1024])
        tD = sbt("tD", [128, 1024])
        tE = sbt("tE", [128, 1024])
        tF = sbt("tF", [128, 1024])
        tG = sbt("tG", [128, 1024])
        tI = sbt("tI", [128, 1024], I32)

        def sin_of(dst_ap, arg_ap, n, shift, names_r, name_w):
            a = tF[:, 0:n]
            S.op("dve", lambda e: e.tensor_scalar(a, arg_ap, 1.0, shift, ALU.mult, ALU.add), reads=names_r, writes=["tF"])
            S.op("dve", lambda e: e.tensor_scalar(tI[:, 0:n], a, 1.0 / (2 * PI), None, ALU.mult), reads=["tF"], writes=["tI"])
            S.op("dve", lambda e: e.tensor_copy(tG[:, 0:n], tI[:, 0:n]), reads=["tI"], writes=["tG"])
            S.op("dve", lambda e: e.scalar_tensor_tensor(tG[:, 0:n], tG[:, 0:n], -2 * PI, a, ALU.mult, ALU.add), reads=["tG", "tF"], writes=["tG"])
            S.op("dve", lambda e: e.tensor_scalar(tG[:, 0:n], tG[:, 0:n], 3.14159, -3.14159, ALU.min, ALU.max), reads=["tG"], writes=["tG"])
            S.op("act", lambda e: e.activation(dst_ap, tG[:, 0:n], AF.Sin), reads=["tG"], writes=[name_w])

        S.op("act", lambda e: e.activation(dt2[:], ldt2[:], AF.Exp), reads=["ldt2"], writes=["dt2"])
        S.op("dve", lambda e: e.tensor_tensor(th[:], aim2[:], dt2[:], ALU.mult), reads=["aim2", "dt2"], writes=["th"])
        S.op("dve", lambda e: e.tensor_tensor(rho[:], are2[:], dt2[:], ALU.mult), reads=["are2", "dt2"], writes=["rho"])
        S.op("act", lambda e: e.activation(rho[:], rho[:], AF.Exp), reads=["rho"], writes=["rho"])
        S.op("dve", lambda e: e.tensor_scalar(tA[:, 0:32], th[:], 128.0, None, ALU.mult), reads=["th"], writes=["tA"])
        sin_of(s128[:], tA[:, 0:32], 32, 0.0, ["tA"], "s128")
        sin_of(c128[:], tA[:, 0:32], 32, PI / 2, ["tA"], "c128")
        for gq in range(4):
            for gi in range(8):
                g = gq * 8 + gi
                S.op("dve", lambda e, g=g, gi=gi: e.tensor_scalar(tA[:, gi * 128:(gi + 1) * 128], mrow[:], th[:, g:g + 1], None, ALU.mult), reads=["mrow", "th"], writes=["tA"])
            sin_of(Sm[:, gq * 8:(gq + 1) * 8, :].rearrange("p g m -> p (g m)"), tA[:, 0:1024], 1024, 0.0, ["tA"], "Sm")
            sin_of(Cm[:, gq * 8:(gq + 1) * 8, :].rearrange("p g m -> p (g m)"), tA[:, 0:1024], 1024, PI / 2, ["tA"], "Cm")
        S.op("dve", lambda e: e.tensor_copy(Cw[0:64, :, :], cw_l[0:64, :, :]), reads=["cw_l"], writes=["Cw0"])
        S.op("dve", lambda e: e.tensor_scalar(Cw[64:128, :, :], cw_l[64:128, :, :], -1.0, None, ALU.mult), reads=["cw_l"], writes=["Cw1"])
        for gq in range(4):
            gsl = slice(gq * 8, (gq + 1) * 8)
            n = 512
            a_re_t, a_im_t, ldt_t, bre_t, bim_t = tA[:, 0:n], tA[:, n:2 * n], tB[:, 0:n], tB[:, n:2 * n], tC[:, 0:n]
            for (dst, nm, rn) in ((a_re_t, "b_are", "tA"), (a_im_t, "b_aim", "tA"), (ldt_t, "b_ldt", "tB"), (bre_t, "b_bre", "tB"), (bim_t, "b_bim", "tC")):
                S.dma("sp", lambda e, dst=dst, nm=nm, gsl=gsl: e.dma_start(out=dst.rearrange("p (g m) -> p g m", g=8), in_=dr[nm][:, gsl, :]), writes=[rn])
            dtt, rl, tht, ee = tC[:, n:2 * n], tD[:, 0:n], tD[:, n:2 * n], tE[:, 0:n]
            sn, cs = tE[:, n:2 * n], tC[:, n:2 * n]
            S.op("act", lambda e: e.activation(dtt, ldt_t, AF.Exp), reads=["tB"], writes=["tC"])
            S.op("dve", lambda e: e.tensor_tensor(rl, a_re_t, dtt, ALU.mult), reads=["tA", "tC"], writes=["tD"])
            S.op("dve", lambda e: e.tensor_tensor(tht, a_im_t, dtt, ALU.mult), reads=["tA", "tC"], writes=["tD"])
            S.op("act", lambda e: e.activation(ee, rl, AF.Exp), reads=["tD"], writes=["tE"])
            sin_of(sn, tht, n, 0.0, ["tD"], "tE")
            sin_of(cs, tht, n, PI / 2, ["tD"], "tC")
            lbr1, lbi = rl, tht
            S.op("dve", lambda e: e.tensor_tensor(lbi, ee, sn, ALU.mult), reads=["tE"], writes=["tD"])
            S.op("dve", lambda e: e.tensor_tensor(lbr1, ee, cs, ALU.mult), reads=["tE", "tC"], writes=["tD"])
            S.op("dve", lambda e: e.tensor_scalar(lbr1, lbr1, -1.0, None, ALU.add), reads=["tD"], writes=["tD"])
            den = ee
            S.op("dve", lambda e: e.tensor_tensor(den, a_re_t, a_re_t, ALU.mult), reads=["tA", "tE"], writes=["tE"])
            S.op("dve", lambda e: e.tensor_tensor(sn, a_im_t, a_im_t, ALU.mult), reads=["tA", "tE"], writes=["tE"])
            S.op("dve", lambda e: e.tensor_tensor(den, den, sn, ALU.add), reads=["tE"], writes=["tE"])
            S.op("dve", lambda e: e.reciprocal(den, den), reads=["tE"], writes=["tE"])
            fr, fi, tmp = sn, cs, ldt_t
            S.op("dve", lambda e: e.tensor_tensor(fr, lbr1, a_re_t, ALU.mult), reads=["tD", "tA"], writes=["tE"])
            S.op("dve", lambda e: e.tensor_tensor(tmp, lbi, a_im_t, ALU.mult), reads=["tD", "tA"], writes=["tB"])
            S.op("dve", lambda e: e.tensor_tensor(fr, fr, tmp, ALU.add), reads=["tE", "tB"], writes=["tE"])
            S.op("dve", lambda e: e.tensor_tensor(fr, fr, den, ALU.mult), reads=["tE"], writes=["tE"])
            S.op("dve", lambda e: e.tensor_tensor(fi, lbi, a_re_t, ALU.mult), reads=["tD", "tA"], writes=["tC"])
            S.op("dve", lambda e: e.tensor_tensor(tmp, lbr1, a_im_t, ALU.mult), reads=["tD", "tA"], writes=["tB"])
            S.op("dve", lambda e: e.tensor_tensor(fi, fi, tmp, ALU.subtract), reads=["tC", "tB"], writes=["tC"])
            S.op("dve", lambda e: e.tensor_tensor(fi, fi, den, ALU.mult), reads=["tC", "tE"], writes=["tC"])
            br_, bi_ = lbr1, lbi
            S.op("dve", lambda e: e.tensor_tensor(br_, fr, bre_t, ALU.mult), reads=["tE", "tB"], writes=["tD"])
            S.op("dve", lambda e: e.tensor_tensor(tmp, fi, bim_t, ALU.mult), reads=["tC"], writes=["tB"])
            S.op("dve", lambda e: e.tensor_tensor(br_, br_, tmp, ALU.subtract), reads=["tD", "tB"], writes=["tD"])
            S.op("dve", lambda e: e.tensor_tensor(bi_, fr, bim_t, ALU.mult), reads=["tE", "tC"], writes=["tD"])
            S.op("dve", lambda e: e.tensor_tensor(tmp, fi, bre_t, ALU.mult), reads=["tC", "tB"], writes=["tB"])
            S.op("dve", lambda e: e.tensor_tensor(bi_, bi_, tmp, ALU.add), reads=["tD", "tB"], writes=["tD"])
            v3 = lambda ap: ap.rearrange("p (g m) -> p g m", g=8)
            S.op("dve", lambda e, gsl=gsl: e.tensor_copy(BT[:, gsl, 0:64], v3(br_)), reads=["tD"], writes=["BT"])
            S.op("dve", lambda e, gsl=gsl: e.tensor_copy(BT[:, gsl, 64:128], v3(bi_)), reads=["tD"], writes=["BT"])
            S.op("dve", lambda e, gsl=gsl: e.tensor_copy(BTs[:, gsl, 0:64], v3(bi_)), reads=["tD"], writes=["BTs"])
            S.op("dve", lambda e, gsl=gsl: e.tensor_scalar(BTs[:, gsl, 64:128], v3(br_), -1.0, None, ALU.mult), reads=["tD"], writes=["BTs"])
        S.op("dve", lambda e: e.memset(carry[:], 0.0), writes=["carry"])
        S.barrier()
        S.emit()
        csu.close()

        xts = [sb("xt%d" % i, [128, 1024]) for i in range(2)]
        xrs = [sb("xr%d" % i, [128, 1024]) for i in range(1)]
        xn_t = sb("xn", [128, 1024], BF16)
        Wf = {"junk": xn_t, "ssq": sb("ssq", [128, 4]), "xn": xn_t,
              "ptr": None}
        hTs = [sb("hT%d" % j, [128, 8, 512], BF16) for j in range(2)]
        uT_bfs = [sb("uT_bf%d" % j, [128, 4, 512], BF16) for j in range(2)]
        uT_fs = uT_bfs
        szss = [sb("szs%d" % j, [128, 4, 512], BF16) for j in range(2)]
        szas = [sb("sza%d" % j, [128, 4, 512], BF16) for j in range(2)]
        bufA = [sb("bufA%d" % j, [128, 256]) for j in range(2)]
        bufB = [sb("bufB%d" % j, [128, 256]) for j in range(2)]
        bB16 = [sb("bB16_%d" % j, [128, 256], BF16) for j in range(2)]
        bC16 = [sb("bC16_%d" % j, [128, 256], BF16) for j in range(2)]
        bufC = [None, None]
        Rts = [sb("Rt%d" % j, [128, 2, 128]) for j in range(2)]
        Rtbs = [sb("Rtb%d" % j, [128, 256], BF16) for j in range(2)]
        pm_b = sb("pm_b", [128, 128], BF16)
        S.op("dve", lambda e: e.tensor_copy(pm_b[:], pm_f[:]), reads=["pm_f"], writes=["pm_b"])
        ysb = sb("ysb", [128, 512])
        y2 = sb("y2", [128, 4, 128])
        gy_bs = [sb("gy_b%d" % j, [128, 4, 512], BF16) for j in range(2)]
        sig = sb("sig", [128, 512])
        m1 = sb("m1", [128, 512])
        mixs = [sb("mix%d" % j, [128, 8, 512], BF16) for j in range(2)]
        res = sb("res", [128, 1024])
        ct1 = sb("ct1", [128, 32])
        ct2 = sb("ct2", [128, 32])
        pproj = psb("pproj")
        Wf["ptr"] = pproj
        Wf["ptrn"] = "pproj"
        pVW = [psb("pVW0"), psb("pVW1")]
        pRs = [psb("pR0"), psb("pR1")]
        pY = psb("pY")
        pT = psb("pT")
        pO = [psb("pO0"), pproj]
        pOn = ["pO0", "pproj"]

        if not WITH_ATTN:
            for j_ in range(2):
                S.op("pool", lambda e, j_=j_: e.memset(mixs[j_][:, 0:4, :], 0.0), writes=["mix_attn%d" % j_])

        def make_units(i):
            cp = i % 2
            hT, uT_bf, uT_f, szs, sza = hTs[cp], uT_bfs[cp], uT_fs[cp], szss[cp], szas[cp]
            hn = "hT%d" % cp
            xs_ = [xts[tt % 2] for tt in range(4)]
            xn_ = ["xt%d" % (tt % 2) for tt in range(4)]
            hnames = [(hn, "act"), (hn, "dve")]
            units = []
            for tt in range(4):
                units.append(lambda tt=tt: front(i, xs_, xn_, hT, hn, Wf, ["act"], tts=(tt,)))

            def proj_unit(ct):
                for kc in range(8):
                    S.op("pe", lambda e, kc=kc: e.matmul(pproj[:, :], w2[:, kc, ct * 128:(ct + 1) * 128], hT[:, kc, :], start=(kc == 0), stop=(kc == 7)),
                         reads=hnames, writes=["pproj"], inc=(kc == 7))
                if ct < 4:
                    S.op("act", lambda e: e.copy(uT_bf[:, ct, :], pproj[:, :]), reads=["pproj"], writes=["uT_bf%d" % cp])
                elif ct < 8:
                    S.op("act", lambda e: e.activation(szs[:, ct - 4, :], pproj[:, :], AF.Silu), reads=["pproj"], writes=["szs%d" % cp])
                else:
                    S.op("act", lambda e: e.activation(sza[:, ct - 8, :], pproj[:, :], AF.Silu), reads=["pproj"], writes=["sza%d" % cp])
            for ct in range(12):
                units.append(lambda ct=ct: proj_unit(ct))
            return units

        for u_ in make_units(0):
            u_()
        pending_tail = []
        for i in range(NCH_RUN):
            t0 = i * CH
            cp = i % 2
            hT, uT_bf, uT_f, szs, sza = hTs[cp], uT_bfs[cp], uT_fs[cp], szss[cp], szas[cp]
            uTbn, uTfn, szsn, szan = "uT_bf%d" % cp, "uT_bf%d" % cp, "szs%d" % cp, "sza%d" % cp
            gy_b = gy_bs[cp]
            gyn = "gy_b%d" % cp
            units = pending_tail + (make_units(i + 1) if i + 1 < NCH_RUN else [])
            pending_tail = []
            for f in range(4):
                fl = f * 128

                def stageA1(gb, fl=fl, uT_bf=uT_bf):
                    par = gb % 2
                    po = par * 256
                    g0 = gb * 2
                    An, Bn = "bA%d" % par, "bB%d" % par
                    pVn, pWn = "pV%d" % par, "pW%d" % par
                    bA, bB = bufA[par], bufB[par]
                    for gi in range(2):
                        g = g0 + gi
                        ctg = g // 8
                        base = 64 * ((g % 8) // 4)
                        S.op("pe", lambda e, g=g, gi=gi, ctg=ctg, base=base: e.matmul(pVW[par][:, gi * 128:(gi + 1) * 128], BT[base:base + 64, g, :], uT_bf[base:base + 64, ctg, fl:fl + 128], start=True, stop=True),
                             reads=[uTbn], writes=[pVn], inc=(gi == 1))
                    for gi in range(2):
                        g = g0 + gi
                        ctg = g // 8
                        base = 64 * ((g % 8) // 4)
                        S.op("pe", lambda e, g=g, gi=gi, ctg=ctg, base=base: e.matmul(pVW[par][:, 256 + gi * 128:256 + (gi + 1) * 128], BTs[base:base + 64, g, :], uT_bf[base:base + 64, ctg, fl:fl + 128], start=True, stop=True),
                             reads=[uTbn], writes=[pWn], inc=(gi == 1))
                    cmv = Cm[:, g0:g0 + 2, :].rearrange("p g m -> p (g m)")
                    smv = Sm[:, g0:g0 + 2, :].rearrange("p g m -> p (g m)")
                    S.op("dve", lambda e: e.tensor_tensor(bA[:], pVW[par][:, 0:256], cmv, ALU.mult), reads=[pVn, pWn], writes=[An])
                    S.op("dve", lambda e: e.tensor_tensor(bB[:], pVW[par][:, 256:512], smv, ALU.mult), reads=[pVn, pWn], writes=[Bn])
                    S.op("pool", lambda e: e.tensor_tensor(bA[:], bA[:], bB[:], ALU.add), reads=[An, Bn], writes=[An])

                def stageA2(gb):
                    par = gb % 2
                    g0 = gb * 2
                    An, Rn, Rbn = "bA%d" % par, "Rt%d" % par, "Rtb%d" % par
                    bA, Rt_, Rtb_ = bufA[par], Rts[par], Rtbs[par]
                    for gi in range(2):
                        g = g0 + gi
                        S.op("dve", lambda e, g=g, gi=gi: e.tensor_tensor_scan(Rt_[:, gi, :], rho[:, g:g + 1].to_broadcast([128, 128]), bA[:, gi * 128:(gi + 1) * 128], carry[:, g:g + 1], ALU.mult, ALU.add),
                             reads=[An, "carry"], writes=[Rn])
                    S.op("act", lambda e: e.copy(Rlast[:, g0:g0 + 2], Rt_[:, :, 127]), reads=[Rn], writes=["Rlast"])
                    rtv = Rt_[:, :, :].rearrange("p g m -> p (g m)")
                    S.op("act", lambda e: e.copy(Rtb_[:], rtv), reads=[Rn], writes=[Rbn])

                def stageB(gb):
                    par = gb % 2
                    po = par * 256
                    g0 = gb * 2
                    Rn, Rbn, pRn = "Rt%d" % par, "Rtb%d" % par, "pR%d" % par
                    Rt_, Rtb_ = Rts[par], Rtbs[par]
                    cmv = Cm[:, g0:g0 + 2, :].rearrange("p g m -> p (g m)")
                    smv = Sm[:, g0:g0 + 2, :].rearrange("p g m -> p (g m)")
                    rtv = Rt_[:, :, :].rearrange("p g m -> p (g m)")
                    S.op("pe", lambda e: e.matmul(pRs[par][:, 0:256], pm_b[:], Rtb_[:], start=True, stop=True), reads=[Rbn], writes=[pRn])
                    S.op("pool", lambda e: e.tensor_tensor(bB16[par][:], rtv, cmv, ALU.mult), reads=[Rn], writes=["bB16_%d" % par])
                    S.op("dve", lambda e: e.tensor_tensor(bC16[par][:], pRs[par][:, 0:256], smv, ALU.mult), reads=[pRn], writes=["bC16_%d" % par])
                    for gi in range(2):
                        g = g0 + gi
                        S.op("pe", lambda e, g=g, gi=gi: e.matmul(pY[:, g * 16:(g + 1) * 16], bB16[par][:, gi * 128:(gi + 1) * 128], Cw[:, g, :], start=True, stop=False),
                             reads=["bB16_%d" % par], writes=["pY"], inc=False)
                        S.op("pe", lambda e, g=g, gi=gi: e.matmul(pY[:, g * 16:(g + 1) * 16], bC16[par][:, gi * 128:(gi + 1) * 128], Cw[:, g, :], start=False, stop=True),
                             reads=["bC16_%d" % par], writes=["pY"], inc=(gi == 1))

                NB = 16
                stageA1(0)
                stageA2(0)
                for gb in range(NB):
                    if gb + 1 < NB:
                        stageA1(gb + 1)
                    stageB(gb)
                    if gb + 1 < NB:
                        stageA2(gb + 1)
                    if gb % 2 == 1 and units:
                        units.pop(0)()
                S.op("pe", lambda e: e.matmul(pRs[0][:, 0:32], pm_f[:], Rlast[:], start=True, stop=True), reads=["pm_f", "Rlast"], writes=["pR0"])
                S.op("dve", lambda e: e.tensor_tensor(ct1[:], Rlast[:], c128[:], ALU.mult), reads=["Rlast", "c128"], writes=["ct1"])
                S.op("dve", lambda e: e.tensor_tensor(ct2[:], pRs[0][:, 0:32], s128[:], ALU.mult), reads=["pR0", "s128"], writes=["ct2"])
                S.op("dve", lambda e: e.tensor_tensor(carry[:], ct1[:], ct2[:], ALU.add), reads=["ct1", "ct2"], writes=["carry"])
                S.op("act", lambda e: e.copy(ysb[:], pY[:, :]), reads=["pY"], writes=["ysb"])
                for ct in range(4):
                    S.op("pe", lambda e, ct=ct: e.transpose(pT[:, ct * 128:(ct + 1) * 128], ysb[:, ct * 128:(ct + 1) * 128], ident_f[:]), reads=["ysb", "ident_f"], writes=["pT"], inc=(ct == 3))
                for ct in range(4):
                    S.op("dve", lambda e, ct=ct, fl=fl, uT_f=uT_f: e.scalar_tensor_tensor(y2[:, ct, :], uT_f[:, ct, fl:fl + 128], dcol[:, ct:ct + 1], pT[:, ct * 128:(ct + 1) * 128], ALU.mult, ALU.add),
                         reads=[uTfn, "dcol", "pT"], writes=["y2"])
                S.op("act", lambda e, fl=fl, gy_b=gy_b: e.activation(gy_b[:, :, fl:fl + 128], y2[:, :, :], AF.Gelu), reads=["y2"], writes=[gyn])
            while units:
                units.pop(0)()

            def make_tail(i, cp, t0):
                gy_b, mix, szs, sza = gy_bs[cp], mixs[cp], szss[cp], szas[cp]
                gyn, szsn, szan = "gy_b%d" % cp, "szs%d" % cp, "sza%d" % cp
                man, msn = "mix_attn%d" % cp, "mix_ssm%d" % cp
                tl = []

                def attn_unit():
                    if WITH_ATTN:
                        av = attn_scr.rearrange("(kc p) t -> p kc t", p=128)
                        S.dma("sp", lambda e: e.dma_start(out=mix[:, 0:4, :], in_=av[:, :, t0:t0 + CH]), reads=["attn_scr"], writes=[man])
                        S.op("pool", lambda e: e.tensor_tensor(mix[:, 0:4, :], mix[:, 0:4, :], sza[:, :, :], ALU.mult), reads=[man, szan], writes=[man])
                tl.append(attn_unit)

                def glu_unit(co):
                    for ct in range(4):
                        S.op("pe", lambda e, ct=ct: e.matmul(pT[:, :], glw[:, ct, co * 128:(co + 1) * 128], gy_b[:, ct, :], start=(ct == 0), stop=(ct == 3)),
                             reads=glwn + [gyn], writes=["pT"], inc=(ct == 3))
                    S.op("act", lambda e: e.activation(sig[:], pT[:, :], AF.Sigmoid, bias=glub[:, co:co + 1]), reads=["pT", "glub"], writes=["sig"])
                    S.op("dve", lambda e: e.tensor_tensor(m1[:], gy_b[:, co, :], sig[:], ALU.mult), reads=[gyn, "sig"], writes=["m1"])
                    S.op("pool", lambda e: e.tensor_tensor(mix[:, 4 + co, :], m1[:], szs[:, co, :], ALU.mult), reads=["m1", szsn], writes=[msn])
                for co in range(4):
                    tl.append(lambda co=co: glu_unit(co))

                def out_unit(tt):
                    gt = 4 * i + tt
                    for hf in range(2):
                        for kc in range(8):
                            S.op("pe", lambda e, hf=hf, kc=kc: e.matmul(pO[hf][:, :], mix[:, kc, tt * 128:(tt + 1) * 128], wout[:, kc, hf * 512:(hf + 1) * 512], start=(kc == 0), stop=(kc == 7)),
                                 reads=woutn + [man, msn], writes=[pOn[hf]], inc=(kc == 7))
                        S.op("dve", lambda e, hf=hf: e.tensor_tensor(res[:, hf * 512:(hf + 1) * 512], pO[hf][:, :], gate_bc[:, hf * 512:(hf + 1) * 512], ALU.mult),
                             reads=[pOn[hf], "gate_bc"], writes=["res%d" % hf])
                    xr = xrs[0]
                    S.dma("sp", lambda e: e.dma_start(out=xr[:], in_=dr["x"][gt * 128:(gt + 1) * 128, :]), writes=["xr0"])
                    S.op("dve", lambda e: e.tensor_tensor(res[:], res[:], xr[:], ALU.add), reads=["res0", "res1", "xr0"], writes=["res0", "res1"])
                    S.dma("sp", lambda e: e.dma_start(out=out[gt * 128:(gt + 1) * 128, :], in_=res[:]), reads=["res0", "res1"], writes=["out"])
                for tt in range(4):
                    tl.append(lambda tt=tt: out_unit(tt))
                return tl

            pending_tail = make_tail(i, cp, t0)
            if i + 1 >= NCH_RUN:
                for u_ in pending_tail:
                    u_()
                pending_tail = []
        S.barrier()
        S.emit()


_CACHE = {}


def kernel(**inputs):
    consts = make_consts()
    params = make_params(inputs)
    specs = input_specs(consts, params)
    nc = build(specs)
    x = np.asarray(inputs["x"], np.float32)
    c = np.asarray(inputs["c"], np.float32)
    in_maps = []
    for b in range(8):
        m = {"x": np.ascontiguousarray(x[b]), "c_col": np.ascontiguousarray(c[b].reshape(8, 128).T)}
        m.update(consts)
        m.update(params)
        in_maps.append(m)
    res = run_bass_kernel_spmd(nc, in_maps, core_ids=list(range(8)))
    return np.stack([np.asarray(r["out"], np.float32) for r in res.results], 0)
```

```python
import math
import numpy as np
import ml_dtypes
from contextlib import ExitStack
import concourse.bass as bass
import concourse.mybir as mybir
from concourse.bass_utils import run_bass_kernel_spmd

F32 = mybir.dt.float32
BF16 = mybir.dt.bfloat16
I32 = mybir.dt.int32
AF = mybir.ActivationFunctionType
ALU = mybir.AluOpType
AX = mybir.AxisListType

T = 8192
D = 1024
CH = 512
NCH = T // CH
EPS = 1e-6
BIG = 30000.0
NW1 = 1312
WITH_ATTN = True
WITH_SSM = True
NCH_RUN = NCH
STAGE = 99

ENGS = ["pe", "act", "dve", "pool", "sp"]


class Sched:
    def __init__(self, nc, ctx, n_dma_sems=12):
        self.nc = nc
        self.prog = {e: [] for e in ENGS}
        self.sem = {e: ctx.enter_context(nc.semaphore("s_" + e)) for e in ENGS}
        self.cnt = {e: 0 for e in ENGS}
        self.seen = {e: {} for e in ENGS}
        self.last_w = {}
        self.readers = {}
        self.dsem, self.dcnt, self.dnext = {}, {}, {}
        for q in ["sp", "act"]:
            self.dsem[q] = [ctx.enter_context(nc.semaphore("d_%s%d" % (q, i))) for i in range(n_dma_sems)]
            self.dcnt[q] = [0] * n_dma_sems
            self.dnext[q] = 0
        self.semobj = {}
        for e in ENGS:
            self.semobj[("e", e)] = self.sem[e]
        for q in self.dsem:
            for i, s in enumerate(self.dsem[q]):
                self.semobj[("d", q, i)] = s

    def _waits_for(self, eng, toks):
        need = {}
        for (k, v) in toks:
            if k == ("e", eng) and (v > self.cnt[eng] or eng == "pe"):
                continue
            if self.seen[eng].get(k, 0) < v:
                need[k] = max(need.get(k, 0), v)
        for k, v in need.items():
            self.seen[eng][k] = v
        return list(need.items())

    def _deps(self, reads, writes):
        toks = []
        for r in reads:
            t = self.last_w.get(r)
            if t is not None:
                toks.append(t)
        for w in writes:
            t = self.last_w.get(w)
            if t is not None:
                toks.append(t)
            toks.extend(self.readers.get(w, []))
        return toks

    def _commit(self, tok, reads, writes):
        for r in reads:
            self.readers.setdefault(r, []).append(tok)
        for w in writes:
            self.last_w[w] = tok
            self.readers[w] = []

    def op(self, eng, fn, reads=(), writes=(), inc=True):
        toks = self._deps(reads, writes)
        waits = self._waits_for(eng, toks)
        if inc:
            self.cnt[eng] += 1
            tok = (("e", eng), self.cnt[eng])
        else:
            tok = (("e", eng), self.cnt[eng] + 1)
        self.prog[eng].append((waits, fn, ("e", eng) if inc else None, 1))
        self._commit(tok, reads, writes)
        return tok

    def dma(self, q, fn, reads=(), writes=()):
        toks = self._deps(reads, writes)
        i = self.dnext[q]
        self.dnext[q] = (i + 1) % len(self.dsem[q])
        key = ("d", q, i)
        if self.dcnt[q][i] > 0:
            toks.append((key, 16 * self.dcnt[q][i]))
        waits = self._waits_for(q, toks)
        self.dcnt[q][i] += 1
        tok = (key, 16 * self.dcnt[q][i])
        self.prog[q].append((waits, fn, key, 16))
        self._commit(tok, reads, writes)
        return tok

    def final_wait(self, eng, toks):
        waits = self._waits_for(eng, toks)
        self.prog[eng].append((waits, None, None, 0))

    def barrier(self):
        toks = [(("e", e), self.cnt[e]) for e in ENGS if self.cnt[e] > 0]
        for q in self.dsem:
            for i in range(len(self.dsem[q])):
                if self.dcnt[q][i] > 0:
                    toks.append((("d", q, i), 16 * self.dcnt[q][i]))
        for e in ENGS:
            self.final_wait(e, toks)

    def emit(self):
        nc = self.nc
        prog = self.prog
        semobj = self.semobj

        def run(e_obj, lst):
            for waits, fn, key, amt in lst:
                for k, v in waits:
                    e_obj.wait_ge(semobj[k], v)
                if fn is None:
                    continue
                ins = fn(e_obj)
                if key is not None:
                    ins.then_inc(semobj[key], amt)

        with nc.Block() as block:
            @block.tensor
            def _(e):
                run(e, prog["pe"])

            @block.scalar
            def _(e):
                run(e, prog["act"])

            @block.vector
            def _(e):
                run(e, prog["dve"])

            @block.gpsimd
            def _(e):
                run(e, prog["pool"])

            @block.sync
            def _(e):
                run(e, prog["sp"])
        self.prog = {e: [] for e in ENGS}


def _bf(a):
    return np.ascontiguousarray(a).astype(ml_dtypes.bfloat16)


def make_consts():
    c = {}
    c["ident_bf"] = _bf(np.eye(128, dtype=np.float32))
    c["ident_f"] = np.eye(128, dtype=np.float32)
    ob = np.zeros((128, 128), np.float32)
    ob[:64, :64] = 1.0
    ob[64:, 64:] = 1.0
    c["onesblk_f"] = ob
    pm = np.zeros((128, 128), np.float32)
    for j in range(64):
        pm[64 + j, j] = -1.0
        pm[j, 64 + j] = 1.0
    c["pm_f"] = pm
    c["mrow"] = np.tile(np.arange(128, dtype=np.float32)[None, :], (128, 1))
    inv = 1.0 / (10000.0 ** (np.arange(0, 64, 2, dtype=np.float32) / 64.0))
    t = np.arange(T, dtype=np.float32)
    ang = t[None, :] * inv[:, None].astype(np.float32)
    cos32 = np.cos(ang).astype(np.float32)
    sin32 = np.sin(ang).astype(np.float32)
    cos64 = np.concatenate([cos32, cos32], 0)
    sin64 = np.concatenate([-sin32, sin32], 0)
    cosT = np.concatenate([cos64, cos64], 0)
    sinS = np.concatenate([sin64, sin64], 0)
    c["cosT"] = np.ascontiguousarray(cosT.reshape(128, NCH, CH).transpose(1, 0, 2))
    c["sinS"] = np.ascontiguousarray(sinS.reshape(128, NCH, CH).transpose(1, 0, 2))
    kl = np.arange(128)[:, None]
    tl = np.arange(CH)[None, :]
    cm = np.zeros((4, 128, CH), np.float32)
    wl = np.zeros((4, 128, CH), np.float32)
    pmk = np.zeros((4, 128, CH), np.float32)
    for v in range(4):
        cm[v] = np.where(128 * v + kl > tl, -BIG, 0.0)
        wl[v] = np.where(128 * v + kl <= tl, -BIG, 0.0)
        pmk[v] = np.where(16 * kl + 31 > 512 * v + tl, -BIG, 0.0)
    c["cmask"] = _bf(cm)
    c["wlo"] = _bf(wl)
    c["cmpmask"] = _bf(pmk)
    A = np.zeros((512, 128), np.float32)
    for j in range(128):
        for m in range(4):
            for n in range(2):
                idx = 4 * j + m - n
                if 0 <= idx < 511:
                    A[idx, j] += 1.0
    c["ZA"] = _bf(A.reshape(4, 128, 128))
    r = np.arange(128)[:, None]
    rel = np.arange(256)[None, :] - 126
    cur = r // 64
    fadd = np.zeros((128, 256), np.float32)
    fadd = np.where(rel > cur, -1e4, fadd)
    fadd = np.where((rel == cur) | (rel == cur - 1), 1e4, fadd)
    c["Fadd"] = fadd.astype(np.float32)
    c["Finv"] = np.where(rel > cur, -BIG, 0.0).astype(np.float32)
    key = np.arange(T)[None, :]
    jj = np.arange(64)[:, None]
    c["Epat"] = _bf(((key // 64) % 64 == jj).astype(np.float32))
    gs = np.zeros((32, 24, 64), np.float32)
    for k in range(24):
        gs[k, k, :] = 1.0
    c["Gsel"] = gs
    c["ones_row"] = np.ones((1, 128), np.float32)
    pr = np.zeros((128, 128), np.float32)
    for e in range(2):
        for d in range(64):
            pr[e * 64 + (d + 32) % 64, e * 64 + d] = 1.0
    c["Prot"] = _bf(pr)
    return c


def make_params(inp):
    p = {}
    f = lambda a: np.ascontiguousarray(np.asarray(a, dtype=np.float32))
    l = 0
    w_in = f(inp["w_in"][l])
    q = w_in[:, 0:512]
    kcr = w_in[:, 512:640]
    vcr = w_in[:, 640:768]
    ksl = w_in[:, 768:896]
    vsl = w_in[:, 896:1024]
    kwn = w_in[:, 1024:1152]
    vwn = w_in[:, 1152:1280]
    z_a = w_in[:, 1280:1792]
    g_br = w_in[:, 1792:1816]
    u = w_in[:, 1816:2328]
    z_s = w_in[:, 2328:2840]

    gpad = np.concatenate([g_br, np.zeros((1024, 8), np.float32)], 1)
    W1 = np.concatenate([q, ksl, kwn, kcr, vcr, gpad, vsl, vwn], 1)
    assert W1.shape[1] == NW1
    p["W1"] = f(W1)
    p["W2"] = f(np.concatenate([u, z_s, z_a], 1))
    p["w_ada"] = f(inp["w_ada"][l])
    p["bada_col"] = f(inp["b_ada"][l].reshape(24, 128).T)
    p["bada_grow"] = f(inp["b_ada"][l][2048:3072].reshape(1, 1024))
    p["ng_col"] = f(inp["norm_g"][l].reshape(8, 128).T)

    def gcols(g):
        g = np.asarray(g, np.float32)
        return f(np.stack([np.tile(g, 2), np.tile(g[(np.arange(64) + 32) % 64], 2)], 1))

    p["gq"] = gcols(inp["q_norm_g"][l])
    p["gks"] = gcols(inp["k_slc_norm_g"][l])
    p["gkw"] = gcols(inp["k_win_norm_g"][l])
    p["gkc_bc"] = f(np.tile(np.asarray(inp["k_cmp_norm_g"][l], np.float32)[None, :], (128, 1)))
    p["posk_col"] = f(np.asarray(inp["cmp_pos_k"][l]).reshape(16, 128).T)
    p["posv_col"] = f(np.asarray(inp["cmp_pos_v"][l]).reshape(16, 128).T)
    p["w1k"] = f(inp["cmp_w1_k"][l])
    p["w1v"] = f(inp["cmp_w1_v"][l])
    p["w2k"] = f(inp["cmp_w2_k"][l])
    p["w2v"] = f(inp["cmp_w2_v"][l])
    a_re = np.asarray(inp["ssm_a_re"][l], np.float32)
    a_im = np.asarray(inp["ssm_a_im"][l], np.float32)
    ldt = np.asarray(inp["ssm_log_dt"][l], np.float32)
    p["are2"] = f(np.concatenate([a_re.T, a_re.T], 0))
    p["aim2"] = f(np.concatenate([a_im.T, a_im.T], 0))
    p["ldt2"] = f(np.tile(ldt[None, :], (128, 1)))
    p["b_are"] = f(np.tile(a_re[None, :, :], (128, 1, 1)))
    p["b_aim"] = f(np.tile(a_im[None, :, :], (128, 1, 1)))
    p["b_ldt"] = f(np.tile(ldt[None, :, None], (128, 1, 64)))
    b_re = np.asarray(inp["ssm_b_re"][l], np.float32)
    b_im = np.asarray(inp["ssm_b_im"][l], np.float32)
    bre_l = np.zeros((128, 32, 64), np.float32)
    bim_l = np.zeros((128, 32, 64), np.float32)
    for g in range(32):
        k0 = 16 * (g % 8)
        bre_l[k0:k0 + 16, g, :] = b_re[g].T
        bim_l[k0:k0 + 16, g, :] = b_im[g].T
    p["b_bre"] = bre_l
    p["b_bim"] = bim_l
    c_re = np.asarray(inp["ssm_c_re"][l], np.float32)
    c_im = np.asarray(inp["ssm_c_im"][l], np.float32)
    p["cw_l"] = f(np.concatenate([c_re.transpose(2, 0, 1), c_im.transpose(2, 0, 1)], 0))
    p["dcol"] = f(np.asarray(inp["ssm_d"][l]).reshape(4, 128).T)
    p["glub_col"] = f(np.asarray(inp["glu_b"][l]).reshape(4, 128).T)
    p["glu_w"] = f(inp["glu_w"][l])
    p["w_out"] = f(inp["w_out"][l])
    return p


IN_SPECS = None


def input_specs(consts, params):
    specs = {"x": ((T, D), F32), "c_col": ((128, 8), F32)}
    for d in (consts, params):
        for k, v in d.items():
            specs[k] = (tuple(v.shape), BF16 if v.dtype == ml_dtypes.bfloat16 else F32)
    return specs


def build(specs, dbg=None):
    nc = bass.Bass("TRN2", target_bir_lowering=False)
    dr = {}
    for name, (shape, dt) in specs.items():
        dr[name] = nc.dram_tensor(name, list(shape), dt, kind="ExternalInput").ap()
    out = nc.dram_tensor("out", [T, D], F32, kind="ExternalOutput").ap()
    attn_scr = nc.dram_tensor("attn_scr", [512, T], BF16, kind="Internal").ap()
    dr["_zscr"] = nc.dram_tensor("zscr", [4, 512], F32, kind="Internal").ap()
    dr["_gscr"] = nc.dram_tensor("gscr", [2, 32, 512], F32, kind="Internal").ap()
    dbg_out = {}
    if dbg:
        for name, (shape, dt) in dbg.items():
            dbg_out[name] = nc.dram_tensor(name, list(shape), dt, kind="ExternalOutput").ap()

    with ExitStack() as ctx0:
        S = Sched(nc, ctx0)
        uid = [0]

        def U(prefix):
            uid[0] += 1
            return "%s_%d" % (prefix, uid[0])

        def load(ctx, name, shape, dt, src_ap, q="sp"):
            t = ctx.enter_context(nc.sbuf_tensor("sb_" + name, list(shape), dt))
            S.dma(q, lambda e: e.dma_start(out=t[:], in_=src_ap), writes=[name])
            return t

        def load_cast(t, name, shape3, src_w, stage, col0=0):
            kcs, ncols = shape3[1], shape3[2]
            srcv = src_w.rearrange("(kc p) n -> p kc n", p=128)
            per = max(1, 2048 // kcs)
            c = 0
            pi = 0
            while c < ncols:
                w = min(per, ncols - c)
                st = stage[pi % 2]
                rn = "stage%d" % (pi % 2)
                S.dma("sp", lambda e, st=st, c=c, w=w: e.dma_start(out=st[:, 0:kcs * w].rearrange("p (k n) -> p k n", k=kcs), in_=srcv[:, :, col0 + c:col0 + c + w]), writes=[rn])
                eng = ["dve", "pool", "act"][pi % 3]
                if eng == "act":
                    S.op(eng, lambda e, st=st, c=c, w=w: e.copy(t[:, :, c:c + w], st[:, 0:kcs * w].rearrange("p (k n) -> p k n", k=kcs)), reads=[rn], writes=[name])
                else:
                    S.op(eng, lambda e, st=st, c=c, w=w: e.tensor_copy(t[:, :, c:c + w], st[:, 0:kcs * w].rearrange("p (k n) -> p k n", k=kcs)), reads=[rn], writes=[name + "_%d" % pi])
                c += w
                pi += 1
            return t, [name] + [name + "_%d" % j for j in range(pi)]

        ident_bf = load(ctx0, "ident_bf", [128, 128], BF16, dr["ident_bf"][:, :])
        ident_f = load(ctx0, "ident_f", [128, 128], F32, dr["ident_f"][:, :])
        c_col = load(ctx0, "c_col", [128, 8], F32, dr["c_col"][:, :])
        bada_col = load(ctx0, "bada_col", [128, 24], F32, dr["bada_col"][:, :])
        ng_col = load(ctx0, "ng_col", [128, 8], F32, dr["ng_col"][:, :])
        bada_grow = load(ctx0, "bada_grow", [1, 1024], F32, dr["bada_grow"][:, :])
        ones_row = load(ctx0, "ones_row", [1, 128], F32, dr["ones_row"][:, :])
        gs_col = ctx0.enter_context(nc.sbuf_tensor("gs_col", [128, 8], F32))
        sh_col = ctx0.enter_context(nc.sbuf_tensor("sh_col", [128, 8], F32))
        gate_bc = ctx0.enter_context(nc.sbuf_tensor("gate_bc", [128, 1024], F32))

        with ExitStack() as c0:
            sc_col = c0.enter_context(nc.sbuf_tensor("sc_col", [128, 8], F32))
            mod_col = c0.enter_context(nc.sbuf_tensor("mod_col", [128, 24], F32))
            grow = c0.enter_context(nc.sbuf_tensor("grow", [1, 1024], F32))
            wst = [c0.enter_context(nc.sbuf_tensor("wst%d" % i, [128, 8, 128], F32)) for i in range(2)]
            pmod = c0.enter_context(nc.psum_tensor("pmod", [128, 512], F32))
            prow = c0.enter_context(nc.psum_tensor("prow", [128, 512], F32))
            pbc = c0.enter_context(nc.psum_tensor("pbc", [128, 512], F32))
            S.op("act", lambda e: e.activation(sc_col[:], c_col[:], AF.Silu), reads=["c_col"], writes=["sc_col"])
            wv = dr["w_ada"].rearrange("(kc p) n -> p kc n", p=128)
            for jc in range(24):
                st = wst[jc % 2]
                rn = "wst%d" % (jc % 2)
                S.dma("sp", lambda e, st=st, jc=jc: e.dma_start(out=st[:], in_=wv[:, :, jc * 128:(jc + 1) * 128]), writes=[rn])
                for kc in range(8):
                    S.op("pe", lambda e, st=st, jc=jc, kc=kc: e.matmul(pmod[:, jc:jc + 1], st[:, kc, :], sc_col[:, kc:kc + 1], start=(kc == 0), stop=(kc == 7)),
                         reads=[rn, "sc_col"], writes=["pmod"], inc=(kc == 7))
                if jc >= 16:
                    j0 = (jc - 16) * 128
                    for kc in range(8):
                        S.op("pe", lambda e, st=st, j0=j0, kc=kc: e.matmul(prow[0:1, (j0 % 512):(j0 % 512) + 128], sc_col[:, kc:kc + 1], st[:, kc, :], start=(kc == 0), stop=(kc == 7)),
                             reads=[rn, "sc_col"], writes=["prow"], inc=(kc == 7))
                    if jc in (19, 23):
                        h0 = 0 if jc == 19 else 512
                        S.op("dve", lambda e, h0=h0: e.tensor_tensor(grow[0:1, h0:h0 + 512], prow[0:1, 0:512], bada_grow[0:1, h0:h0 + 512], ALU.add),
                             reads=["prow", "bada_grow"], writes=["grow"])
            S.op("dve", lambda e: e.tensor_tensor(mod_col[:], pmod[:, 0:24], bada_col[:], ALU.add), reads=["pmod", "bada_col"], writes=["mod_col"])
            S.op("dve", lambda e: e.scalar_tensor_tensor(gs_col[:], mod_col[:, 8:16], 1.0, ng_col[:], ALU.add, ALU.mult), reads=["mod_col", "ng_col"], writes=["gs_col"])
            S.op("dve", lambda e: e.tensor_copy(sh_col[:], mod_col[:, 0:8]), reads=["mod_col"], writes=["sh_col"])
            for h0 in (0, 512):
                S.op("pe", lambda e, h0=h0: e.matmul(pbc[:, 0:512], ones_row[0:1, :], grow[0:1, h0:h0 + 512], start=True, stop=True), reads=["ones_row", "grow"], writes=["pbc"])
                S.op("dve", lambda e, h0=h0: e.tensor_copy(gate_bc[:, h0:h0 + 512], pbc[:, 0:512]), reads=["pbc"], writes=["gate_bc"])
            S.barrier()
            S.emit()

        def front(i, xt_tiles, xt_names, hT, hname, W, evac_eng, tts=(0, 1, 2, 3)):
            for tt in tts:
                gt = 4 * i + tt
                xt = xt_tiles[tt]
                xn_ = xt_names[tt]
                S.dma("sp", lambda e, xt=xt, gt=gt: e.dma_start(out=xt[:], in_=dr["x"][gt * 128:(gt + 1) * 128, :]), writes=[xn_])
                S.op("act", lambda e, xt=xt: e.activation(W["junk"][:], xt[:], AF.Square, accum_out=W["ssq"][:, 0:1]), reads=[xn_], writes=["xn", "ssq"])
                S.op("dve", lambda e: e.tensor_scalar(W["ssq"][:, 1:2], W["ssq"][:, 0:1], 1.0 / D, EPS, ALU.mult, ALU.add), reads=["ssq"], writes=["ssq1"])
                S.op("act", lambda e: e.activation(W["ssq"][:, 2:3], W["ssq"][:, 1:2], AF.Ln), reads=["ssq1"], writes=["ssq2"])
                S.op("act", lambda e: e.activation(W["ssq"][:, 3:4], W["ssq"][:, 2:3], AF.Exp, scale=-0.5), reads=["ssq2"], writes=["ssq3"])
                S.op("dve", lambda e, xt=xt: e.tensor_scalar(W["xn"][:], xt[:], W["ssq"][:, 3:4], None, ALU.mult), reads=[xn_, "ssq3"], writes=["xn"])
                for half in range(2):
                    for j in range(4):
                        kc = half * 4 + j
                        S.op("pe", lambda e, kc=kc, j=j: e.matmul(W["ptr"][:, j * 128:(j + 1) * 128], W["xn"][:, kc * 128:(kc + 1) * 128], ident_bf[:], start=True, stop=True),
                             reads=["xn", "ident_bf"], writes=[W.get("ptrn", "ptr")], inc=(j == 3))
                    for j in range(4):
                        kc = half * 4 + j
                        eng = evac_eng[kc % len(evac_eng)]
                        if eng == "act":
                            S.op("act", lambda e, kc=kc, j=j, tt=tt: e.activation(hT[:, kc, tt * 128:(tt + 1) * 128], W["ptr"][:, j * 128:(j + 1) * 128], AF.Identity,
                                                                                 bias=sh_col[:, kc:kc + 1], scale=gs_col[:, kc:kc + 1]),
                                 reads=[W.get("ptrn", "ptr"), "gs_col", "sh_col"], writes=[(hname, "act")])
                        else:
                            S.op("dve", lambda e, kc=kc, j=j, tt=tt: e.tensor_scalar(hT[:, kc, tt * 128:(tt + 1) * 128], W["ptr"][:, j * 128:(j + 1) * 128],
                                                                                    gs_col[:, kc:kc + 1], sh_col[:, kc:kc + 1], ALU.mult, ALU.add),
                                 reads=[W.get("ptrn", "ptr"), "gs_col", "sh_col"], writes=[(hname, "dve")])
            return [(hname, "act"), (hname, "dve")]

        if WITH_ATTN:
            pass1(nc, S, dr, attn_scr, dbg_out, front, load, load_cast, ident_bf, ident_f)

        pass2(nc, S, dr, out, attn_scr, dbg_out, front, load, load_cast, ident_bf, ident_f, gate_bc)
    return nc


def pass1(nc, S, dr, attn_scr, dbg_out, front, load, load_cast, ident_bf, ident_f):
    with ExitStack() as c1:
        def sb(name, shape, dt=F32):
            return c1.enter_context(nc.sbuf_tensor("a_" + name, list(shape), dt))

        def psb(name, shape=(128, 512), dt=F32):
            return c1.enter_context(nc.psum_tensor("a_" + name, list(shape), dt))

        w1s = sb("w1s", [128, 8, NW1], BF16)
        cw1 = [sb("cw1k", [128, 16, 256], BF16), sb("cw1v", [128, 16, 256], BF16)]
        cw2 = [sb("cw2k", [128, 2, 64], BF16), sb("cw2v", [128, 2, 64], BF16)]
        Kaug = [sb("Kaug0", [128, T], BF16), sb("Kaug1", [128, T], BF16)]
        Vs = sb("Vs", [128, 64, 2, 65], BF16)
        Kw = sb("Kw", [128, 2, 1024], BF16)
        Vw = sb("Vw", [128, 8, 2, 65], BF16)
        kcT = sb("kcT", [128, 2, 512], BF16)
        Vc = sb("Vc", [128, 4, 2, 65], BF16)
        Xs = [sb("Xk", [128, 2, 1056], BF16), sb("Xv", [128, 2, 1056], BF16)]
        posb = sb("posb", [128, 2, 2])
        ones64 = sb("ones64", [128, 64])
        cmask = load(c1, "cmask", [128, 4, 512], BF16, dr["cmask"].rearrange("v p t -> p v t"))
        wlo = load(c1, "wlo", [128, 4, 512], BF16, dr["wlo"].rearrange("v p t -> p v t"))
        ZA = load(c1, "ZA", [128, 4, 128], BF16, dr["ZA"].rearrange("c p j -> p c j"))
        Fadd = load(c1, "Fadd", [128, 256], F32, dr["Fadd"][:, :])
        Finv = load(c1, "Finv", [128, 256], F32, dr["Finv"][:, :])
        Prot = load(c1, "Prot", [128, 128], BF16, dr["Prot"][:, :])
        onesblk = load(c1, "onesblk_f", [128, 128], F32, dr["onesblk_f"][:, :])
        gcol = [load(c1, nm, [128, 2], F32, dr[nm][:, :]) for nm in ("gq", "gks", "gkw")]
        gkc_bc = load(c1, "gkc_bc", [128, 64], F32, dr["gkc_bc"][:, :])
        posc = [load(c1, "posk_col", [128, 16], F32, dr["posk_col"][:, :]), load(c1, "posv_col", [128, 16], F32, dr["posv_col"][:, :])]
        posc_b = [sb("posk_b", [128, 16], BF16), sb("posv_b", [128, 16], BF16)]

        csu = ExitStack()
        stage = [csu.enter_context(nc.sbuf_tensor("a_stage0", [128, 2048], F32)), csu.enter_context(nc.sbuf_tensor("a_stage1", [128, 2048], F32))]
        ppos = csu.enter_context(nc.psum_tensor("a_ppos", [128, 512], F32))
        load_cast(w1s, "w1s", [128, 8, NW1], dr["W1"], stage)
        load_cast(cw1[0], "cw1k", [128, 16, 256], dr["w1k"], stage)
        load_cast(cw1[1], "cw1v", [128, 16, 256], dr["w1v"], stage)
        load_cast(cw2[0], "cw2k", [128, 2, 64], dr["w2k"], stage)
        load_cast(cw2[1], "cw2v", [128, 2, 64], dr["w2v"], stage)
        S.barrier()
        for kv in range(2):
            S.op("dve", lambda e, kv=kv: e.tensor_copy(posc_b[kv][:], posc[kv][:]), writes=["posc_b%d" % kv])
            for hc in range(2):
                for lp in range(16):
                    S.op("pe", lambda e, kv=kv, hc=hc, lp=lp: e.matmul(ppos[:, kv * 2 + hc:kv * 2 + hc + 1], cw1[kv][:, lp, hc * 128:(hc + 1) * 128], posc_b[kv][:, lp:lp + 1], start=(lp == 0), stop=(lp == 15)),
                         reads=["posc_b%d" % kv], writes=["ppos"], inc=(lp == 15))
        S.op("dve", lambda e: e.tensor_copy(posb[:, :, :].rearrange("p a b -> p (a b)"), ppos[:, 0:4]), reads=["ppos"], writes=["posb"])
        S.op("pool", lambda e: e.memset(ones64[:], 1.0), writes=["ones64"])
        S.op("pool", lambda e: e.memset(kcT[:], 0.0), writes=["kcT"])
        S.op("pool", lambda e: e.memset(Vc[:], 0.0), writes=["Vc"])
        S.op("pool", lambda e: e.memset(Vc[:, :, :, 64:65], 1.0), writes=["Vc"])
        S.op("pool", lambda e: e.memset(Vs[:, :, :, 64:65], 1.0), writes=["Vs"])
        S.op("pool", lambda e: e.memset(Vw[:], 0.0), writes=["Vw"])
        S.op("pool", lambda e: e.memset(Vw[:, :, :, 64:65], 1.0), writes=["Vw"])
        S.op("pool", lambda e: e.memset(Kw[:], 0.0), writes=["Kw"])
        for kv in range(2):
            S.op("pool", lambda e, kv=kv: e.memset(Xs[kv][:], 0.0), writes=["X%d" % kv])
        for g in range(2):
            S.dma("sp", lambda e, g=g: e.dma_start(out=Kaug[g][64:128, :], in_=dr["Epat"][:, :]), writes=["Kaug%d" % g])
        S.barrier()
        S.emit()
        csu.close()

        xts = [sb("xt0", [128, 1024]), sb("xt1", [128, 1024])]
        xn_t = sb("xn", [128, 1024], BF16)
        Wf = {"junk": xn_t, "ssq": sb("ssq", [128, 4]), "xn": xn_t, "ptr": None}
        hT = sb("hT", [128, 8, 512], BF16)
        cs_t = sb("cs_t", [128, 512])
        sn_t = sb("sn_t", [128, 512])
        cmm_t = sb("cmm_t", [128, 512], BF16)
        tA = sb("tA", [128, 512], BF16)
        onesblk_b = sb("onesblk_b", [128, 128], BF16)
        S.op("dve", lambda e: e.tensor_copy(onesblk_b[:], onesblk[:]), writes=["onesblk_b"])
        tB = sb("tB", [128, 512])
        tC = sb("tC", [128, 512])
        tD = sb("tD", [128, 512])
        gqb = sb("gqb", [128, 512], BF16)
        qrp = sb("qrp", [128, 512], BF16)
        qnp = sb("qnp", [128, 512], BF16)
        Qaug = sb("Qaug", [128, 2, 4, 512], BF16)
        qn = sb("qn", [128, 4, 512], BF16)
        gsb = sb("gsb", [32, 512])
        pTb = [sb("pTb%d" % j, [128, 512], BF16) for j in range(4)]
        zrow = sb("zrow", [128, 2, 512])
        rzbs = [sb("rzb0", [64, 512]), sb("rzb1", [64, 512])]
        tO = sb("tO", [64, 512])
        gbs = [sb("gb%d" % j, [64, 512]) for j in range(3)]
        accH = sb("accH", [64, 4, 512], BF16)
        impg = sb("impg", [128, 4, 128])
        impt = sb("impt", [128, 4, 128])
        rs = sb("rs", [128, 8])
        sc = sb("sc", [128, 128])
        sc2 = sb("sc2", [128, 128])
        m8 = sb("m8", [128, 16])
        selb = sb("selb", [128, 4, 2, 128], BF16)
        selT = sb("selT", [128, 2, 512], BF16)
        hidb = sb("hidb", [128, 2, 2, 32], BF16)
        kcn = sb("kcn", [32, 2, 64], BF16)
        kst = sb("kst", [32, 8])
        pproj = psb("pproj")
        Wf["ptr"] = pproj
        Wf["ptrn"] = "pproj"
        pmisc = psb("pmisc")
        psc = [psb("psc0"), psb("psc1"), psb("psc2")]
        pacc = [psb("pacc0"), psb("pacc1")]
        pimp = psb("pimp")
        cnt = {"sc": 0, "acc": 0, "pt": 0, "pp": 0, "ep": 0, "gb": 0}

        def norm_rope(pproj, ppn, gc, want_qn):
            S.op("act", lambda e: e.activation(tA[:], pproj[:, :], AF.Square), reads=[ppn], writes=["tA"])
            S.op("dve", lambda e: e.tensor_scalar(gqb[:], pproj[:, :], gc[:, 0:1], None, ALU.mult), reads=[ppn, "tA"], writes=["gqb"])
            S.op("pe", lambda e: e.matmul(pmisc[:, :], onesblk_b[:], tA[:], start=True, stop=True), reads=["tA", "onesblk_b"], writes=["pmisc"])
            S.op("act", lambda e: e.activation(tB[:], pmisc[:, :], AF.Ln, bias=EPS_AP[:, 0:1], scale=1.0 / 64), reads=["pmisc", "eps_ap"], writes=["tB"])
            S.op("act", lambda e: e.activation(tB[:], tB[:], AF.Exp, scale=-0.5), reads=["tB"], writes=["tB"])
            S.op("pe", lambda e: e.matmul(pmisc[:, :], Prot[:], gqb[:], start=True, stop=True), reads=["gqb"], writes=["pmisc"])
            S.op("dve", lambda e: e.tensor_tensor(tC[:], gqb[:], cs_t[:], ALU.mult), reads=["gqb", "cs_t"], writes=["tC"])
            S.op("dve", lambda e: e.tensor_tensor(tD[:], pmisc[:, :], sn_t[:], ALU.mult), reads=["pmisc", "sn_t"], writes=["tD"])
            S.op("dve", lambda e: e.tensor_tensor(tC[:], tC[:], tD[:], ALU.add), reads=["tC", "tD"], writes=["tC"])
            S.op("dve", lambda e: e.tensor_tensor(qrp[:], tC[:], tB[:], ALU.mult), reads=["tC", "tB"], writes=["qrp"])
            if want_qn:
                S.op("dve", lambda e: e.tensor_tensor(qnp[:], gqb[:], tB[:], ALU.mult), reads=["gqb", "tB"], writes=["qnp"])

        EPS_AP = sb("eps_ap", [128, 1])
        S.op("pool", lambda e: e.memset(EPS_AP[:], EPS), writes=["eps_ap"])

        pps = [(pproj, "pproj"), (pproj, "pproj")]

        def proj_tile(col0, ncols, hnames):
            pp, ppn = pps[cnt["pp"] % 2]
            cnt["pp"] += 1
            for kc in range(8):
                S.op("pe", lambda e, kc=kc: e.matmul(pp[0:ncols, :], w1s[:, kc, col0:col0 + ncols], hT[:, kc, :], start=(kc == 0), stop=(kc == 7)),
                     reads=hnames, writes=[ppn], inc=(kc == 7))
            return pp, ppn

        def next_sc():
            j = cnt["sc"] % 3
            cnt["sc"] += 1
            return psc[j], "psc%d" % j

        def next_pt():
            j = cnt["pt"] % 4
            cnt["pt"] += 1
            return pTb[j], "pTb%d" % j

        def epilogue(pa, pan, g, hh, br, first, clamp):
            h = 4 * g + hh
            k = 3 * h + br
            if clamp:
                S.op("dve", lambda e: e.tensor_scalar(zrow[64:65, :], pa[64:65, :], 1e-30, None, ALU.max), reads=[pan], writes=["zrow"])
                S.op("dve", lambda e: e.reciprocal(zrow[64:65, :], zrow[64:65, :]), reads=["zrow"], writes=["zrow"])
            else:
                S.op("dve", lambda e: e.reciprocal(zrow[64:65, :], pa[64:65, :]), reads=[pan], writes=["zrow"])
            S.op("pe", lambda e: e.matmul(pmisc[0:64, :], ones64[64:65, 0:64], zrow[64:65, :], start=True, stop=True), reads=["zrow"], writes=["pmisc"])
            S.op("act", lambda e: e.copy(rzb[:], pmisc[0:64, :]), reads=["pmisc"], writes=["rzb"])
            S.op("dve", lambda e: e.tensor_tensor(tO[:], pa[0:64, :], rzb[:], ALU.mult), reads=[pan, "rzb"], writes=["tO"])
            S.op("pe", lambda e, k=k: e.matmul(pmisc[0:64, :], Gsel[:, k, :], gsb[:, :], start=True, stop=True), reads=["gsb", "rzb"], writes=["pmisc"])
            if first:
                S.op("dve", lambda e: e.tensor_tensor(accA[:], pmisc[0:64, :], tO[:], ALU.mult), reads=["pmisc", "tO"], writes=["accA"])
            else:
                S.op("dve", lambda e: e.tensor_tensor(tO2[:], pmisc[0:64, :], tO[:], ALU.mult), reads=["pmisc", "tO"], writes=["tO2"])
                S.op("pool", lambda e: e.tensor_tensor(accA[:], accA[:], tO2[:], ALU.add), reads=["accA", "tO2"], writes=["accA"])

        for i in range(NCH_RUN):
            t0 = i * CH
            hnames = front(i, [xts[tt % 2] for tt in range(4)], ["xt%d" % (tt % 2) for tt in range(4)], hT, "hT", Wf, ["dve"])
            S.dma("sp", lambda e, i=i: e.dma_start(out=cs_t[:], in_=dr["cosT"][i, :, :]), writes=["cs_t"])
            S.dma("sp", lambda e, i=i: e.dma_start(out=sn_t[:], in_=dr["sinS"][i, :, :]), writes=["sn_t"])
            S.dma("sp", lambda e, i=i: e.dma_start(out=cmm_t[:], in_=dr["cmpmask"][i % 4, :, :]), writes=["cmm_t"])
            pp, ppn = proj_tile(512, 128, hnames)
            norm_rope(pp, ppn, gcol[1], False)
            S.op("dve", lambda e, t0=t0: e.tensor_copy(Kaug[0][0:64, t0:t0 + CH], qrp[0:64, :]), reads=["qrp"], writes=["Kaug0"])
            S.op("dve", lambda e, t0=t0: e.tensor_copy(Kaug[1][0:64, t0:t0 + CH], qrp[64:128, :]), reads=["qrp"], writes=["Kaug1"])
            pp, ppn = proj_tile(640, 128, hnames)
            norm_rope(pp, ppn, gcol[2], False)
            w0 = (i % 2) * 512
            S.op("dve", lambda e, w0=w0: e.tensor_copy(Kw[0:64, 0, w0:w0 + CH], qrp[0:64, :]), reads=["qrp"], writes=["Kw"])
            S.op("dve", lambda e, w0=w0: e.tensor_copy(Kw[0:64, 1, w0:w0 + CH], qrp[64:128, :]), reads=["qrp"], writes=["Kw"])
            for kv in range(2):
                X = Xs[kv]
                xn_ = "X%d" % kv
                S.op("dve", lambda e, X=X: e.tensor_copy(X[:, :, 0:512], X[:, :, 512:1024]), reads=[xn_], writes=[xn_])
                pp, ppn = proj_tile(768 + 128 * kv, 128, hnames)
                S.op("act", lambda e, X=X, pp=pp: e.copy(X[0:64, 0, 512:1024], pp[0:64, :]), reads=[ppn], writes=[xn_])
                S.op("act", lambda e, X=X, pp=pp: e.copy(X[64:128, 0, 511:1023], pp[0:64, :]), reads=[ppn], writes=[xn_])
                S.op("act", lambda e, X=X, pp=pp: e.copy(X[0:64, 1, 512:1024], pp[64:128, :]), reads=[ppn], writes=[xn_])
                S.op("act", lambda e, X=X, pp=pp: e.copy(X[64:128, 1, 511:1023], pp[64:128, :]), reads=[ppn], writes=[xn_])
            pp, ppn = proj_tile(1024, 32, hnames)
            S.op("act", lambda e, pp=pp: e.activation(gsb[:, :], pp[0:32, :], AF.Sigmoid), reads=[ppn], writes=["gsb"])
            S.dma("sp", lambda e, i=i: e.dma_start(out=dr["_gscr"][i % 2, :, :], in_=gsb[:, :]), reads=["gsb"], writes=["gscr%d" % (i % 2)])
            for ts in range(4):
                kt = 4 * i + ts
                pp, ppn = pps[cnt["pp"] % 2]
                cnt["pp"] += 1
                for kc in range(8):
                    S.op("pe", lambda e, kc=kc, ts=ts, pp=pp: e.matmul(pp[:, 0:256], hT[:, kc, ts * 128:(ts + 1) * 128], w1s[:, kc, 1056:1312], start=(kc == 0), stop=(kc == 7)),
                         reads=hnames, writes=[ppn], inc=(kc == 7))
                S.op("act", lambda e, kt=kt, pp=pp: e.copy(Vs[:, kt, :, 0:64], pp[:, 0:128].rearrange("p (g d) -> p g d", g=2)), reads=[ppn], writes=["Vs"])
                S.op("act", lambda e, kt=kt, pp=pp: e.copy(Vw[:, kt % 8, :, 0:64], pp[:, 128:256].rearrange("p (g d) -> p g d", g=2)), reads=[ppn], writes=["Vw"])
            for kv in range(2):
                X = Xs[kv]
                xn_ = "X%d" % kv
                for q in ([i - 1, i] if i > 0 else [i]):
                    pos0 = 0 if q == i - 1 else 512
                    for g in range(2):
                        for hc in range(2):
                            c0 = (g * 2 + hc) * 32
                            for lp in range(16):
                                S.op("pe", lambda e, kv=kv, X=X, g=g, hc=hc, lp=lp, pos0=pos0, c0=c0: e.matmul(pmisc[:, c0:c0 + 32], cw1[kv][:, lp, hc * 128:(hc + 1) * 128], X[:, g, pos0 + 2 * lp:pos0 + 2 * lp + 512:16], start=(lp == 0), stop=(lp == 15)),
                                     reads=[xn_], writes=["pmisc"], inc=(lp == 15))
                    for hc in range(2):
                        S.op("act", lambda e, kv=kv, hc=hc: e.activation(hidb[:, :, hc, :], pmisc[:, 0:128].rearrange("p (g h n) -> p g h n", g=2, h=2)[:, :, hc, :], AF.Gelu, bias=posb[:, kv, hc:hc + 1]),
                             reads=["pmisc", "posb"], writes=["hidb"])
                    for g in range(2):
                        for hc in range(2):
                            S.op("pe", lambda e, kv=kv, g=g, hc=hc: e.matmul(pmisc[0:32, 128 + g * 64:128 + (g + 1) * 64], hidb[:, g, hc, :], cw2[kv][:, hc, :], start=(hc == 0), stop=(hc == 1)),
                                 reads=["hidb"], writes=["pmisc"], inc=(hc == 1))
                    cq = q // 4
                    pq = 32 * (q % 4)
                    if kv == 1:
                        S.op("act", lambda e, cq=cq, pq=pq: e.copy(Vc[pq:pq + 32, cq, :, 0:64], pmisc[0:32, 128:256].rearrange("p (g d) -> p g d", g=2)), reads=["pmisc"], writes=["Vc"])
                    else:
                        for g in range(2):
                            S.op("act", lambda e, g=g: e.activation(kcn[:, g, :], pmisc[0:32, 128 + g * 64:128 + (g + 1) * 64], AF.Square, accum_out=kst[:, g:g + 1]), reads=["pmisc"], writes=["kcn", "kst"])
                        S.op("dve", lambda e: e.tensor_scalar(kst[:, 2:4], kst[:, 0:2], 1.0 / 64, EPS, ALU.mult, ALU.add), reads=["kst"], writes=["kst"])
                        S.op("act", lambda e: e.activation(kst[:, 4:6], kst[:, 2:4], AF.Ln), reads=["kst"], writes=["kst"])
                        S.op("act", lambda e: e.activation(kst[:, 6:8], kst[:, 4:6], AF.Exp, scale=-0.5), reads=["kst"], writes=["kst"])
                        for g in range(2):
                            S.op("dve", lambda e, g=g: e.scalar_tensor_tensor(kcn[:, g, :], pmisc[0:32, 128 + g * 64:128 + (g + 1) * 64], kst[:, 6 + g:7 + g], gkc_bc[0:32, :], ALU.mult, ALU.mult),
                                 reads=["pmisc", "kst"], writes=["kcn"])
                        for g in range(2):
                            S.op("pe", lambda e, g=g: e.matmul(pmisc[0:64, 256 + g * 32:256 + (g + 1) * 32], kcn[:, g, :], ident_bf[0:32, 0:32], start=True, stop=True), reads=["kcn"], writes=["pmisc"], inc=(g == 1))
                        n0 = 32 * q
                        S.op("dve", lambda e, n0=n0: e.tensor_copy(kcT[0:64, :, n0:n0 + 32], pmisc[0:64, 256:320].rearrange("p (g n) -> p g n", g=2)), reads=["pmisc"], writes=["kcT"])
            for g in range(2):
                for j2 in range(2):
                    pp, ppn = proj_tile((2 * g + j2) * 128, 128, hnames)
                    norm_rope(pp, ppn, gcol[0], True)
                    for e2 in range(2):
                        hh = 2 * j2 + e2
                        rows = slice(64 * e2, 64 * e2 + 64)
                        S.op("dve", lambda e, hh=hh, rows=rows: e.tensor_copy(Qaug[0:64, 0, hh, :], qrp[rows, :]), reads=["qrp"], writes=["Qaug"])
                        S.op("dve", lambda e, hh=hh, rows=rows: e.tensor_copy(Qaug[0:64, 1, hh, :], qrp[rows, :]), reads=["qrp"], writes=["Qaug"])
                        S.op("dve", lambda e, hh=hh, rows=rows: e.tensor_copy(qn[0:64, hh, :], qnp[rows, :]), reads=["qnp"], writes=["qn"])
                cmax = i // 4

                def new_acc():
                    j = cnt["acc"] % 2
                    cnt["acc"] += 1
                    return pacc[j], "pacc%d" % j

                def mk_epilogue(pa, pan, g, hh, br, first, clamp, final, t0):
                    h = 4 * g + hh
                    k = 3 * h + br
                    an = "accH%d" % hh
                    ei = cnt["ep"]
                    cnt["ep"] += 1
                    zs = ei % 2
                    zsl = ei % 4
                    rzb, rzn = rzbs[zs], "rzb%d" % zs
                    gj = cnt["gb"] % 3
                    cnt["gb"] += 1
                    gb_, gbn = gbs[gj], "gb%d" % gj
                    isl = (t0 // CH) % 2

                    def s0():
                        S.dma("sp", lambda e: e.dma_start(out=gb_[:], in_=dr["_gscr"][isl, k:k + 1, :].partition_broadcast(64)), reads=["gscr%d" % isl], writes=[gbn])
                        if clamp:
                            S.op("dve", lambda e: e.tensor_scalar(zrow[64:65, zs, :], pa[64:65, :], 1e-30, None, ALU.max), reads=[pan], writes=["zrow%d" % zs])
                            S.op("dve", lambda e: e.reciprocal(zrow[64:65, zs, :], zrow[64:65, zs, :]), reads=["zrow%d" % zs], writes=["zrow%d" % zs])
                        else:
                            S.op("dve", lambda e: e.reciprocal(zrow[64:65, zs, :], pa[64:65, :]), reads=[pan], writes=["zrow%d" % zs])
                        S.dma("sp", lambda e: e.dma_start(out=dr["_zscr"][zsl:zsl + 1, :], in_=zrow[64:65, zs, :]), reads=["zrow%d" % zs], writes=["zscr%d" % zsl])
                        S.dma("sp", lambda e: e.dma_start(out=rzb[:], in_=dr["_zscr"][zsl:zsl + 1, :].partition_broadcast(64)), reads=["zscr%d" % zsl], writes=[rzn])

                    def s1():
                        S.op("dve", lambda e: e.tensor_tensor(tO[:], pa[0:64, :], rzb[:], ALU.mult), reads=[pan, rzn], writes=["tO"])
                        if first:
                            S.op("dve", lambda e: e.tensor_tensor(accH[:, hh, :], tO[:], gb_[:], ALU.mult), reads=["tO", gbn], writes=[an])
                        else:
                            S.op("dve", lambda e: e.tensor_tensor(tO[:], tO[:], gb_[:], ALU.mult), reads=["tO", gbn], writes=["tO"])
                            S.op("pool", lambda e: e.tensor_tensor(accH[:, hh, :], accH[:, hh, :], tO[:], ALU.add), reads=[an, "tO"], writes=[an])
                        if final:
                            S.dma("sp", lambda e: e.dma_start(out=attn_scr[h * 64:(h + 1) * 64, t0:t0 + CH], in_=accH[:, hh, :]), reads=[an], writes=["attn_scr"])
                    return [(0, s0), (2, s1)]

                def mk_imp_post(hh):
                    def f():
                        S.op("dve", lambda e: e.tensor_reduce(rs[:, 0:4], pimp[:, :].rearrange("p (s j) -> p s j", s=4), AX.X, ALU.add), reads=["pimp"], writes=["rs"])
                        S.op("dve", lambda e: e.tensor_scalar(rs[:, 4:8], rs[:, 0:4], 0.5, 1e-30, ALU.mult, ALU.max), reads=["rs"], writes=["rs"])
                        S.op("dve", lambda e: e.reciprocal(rs[:, 4:8], rs[:, 4:8]), reads=["rs"], writes=["rs"])
                        tgt = impg if hh == 0 else impt
                        tgn = "impg" if hh == 0 else "impt"
                        S.op("dve", lambda e: e.tensor_tensor(tgt[:], pimp[:, :].rearrange("p (s j) -> p s j", s=4), rs[:, 4:8].rearrange("p (s o) -> p s o", o=1).to_broadcast([128, 4, 128]), ALU.mult),
                             reads=["pimp", "rs"], writes=[tgn])
                        if hh > 0:
                            S.op("pool", lambda e: e.tensor_tensor(impg[:], impg[:], impt[:], ALU.add), reads=["impg", "impt"], writes=["impg"])
                    return f

                def mk_item(kind, g, hh, idx, npairs, pa, pan, arg, i):
                    st = {}

                    def score():
                        ps_, psn = next_sc()
                        pt, ptn = next_pt()
                        st["pt"], st["ptn"] = pt, ptn
                        if kind == "cmp":
                            c = arg
                            last = (c == npairs - 1)
                            S.op("pe", lambda e: e.matmul(ps_[:, :], kcT[0:64, g, c * 128:(c + 1) * 128], qn[0:64, hh, :], start=True, stop=(not last)), reads=["kcT", "qn"], writes=[psn], inc=(not last))
                            if last:
                                S.op("pe", lambda e: e.matmul(ps_[:, :], ident_bf[:], cmm_t[:], start=False, stop=True), reads=["cmm_t"], writes=[psn])
                        elif kind == "slc":
                            kt = arg
                            H = kt // 32
                            diag = kt >= 4 * i
                            S.op("pe", lambda e: e.matmul(ps_[:, :], Kaug[g][:, kt * 128:(kt + 1) * 128], Qaug[:, H, hh, :], start=True, stop=(not diag)), reads=["Kaug%d" % g, "Qaug"], writes=[psn], inc=(not diag))
                            if diag:
                                S.op("pe", lambda e: e.matmul(ps_[:, :], ident_bf[:], cmask[:, kt - 4 * i, :], start=False, stop=True), writes=[psn])
                        else:
                            kt = arg
                            sl = (kt % 8) * 128
                            mk = cmask[:, kt - 4 * i, :] if kt >= 4 * i else wlo[:, kt - 4 * i + 4, :]
                            S.op("pe", lambda e: e.matmul(ps_[:, :], Kw[0:64, g, sl:sl + 128], Qaug[0:64, 0, hh, :], start=True, stop=False), reads=["Kw", "Qaug"], writes=[psn], inc=False)
                            S.op("pe", lambda e: e.matmul(ps_[:, :], ident_bf[:], mk, start=False, stop=True), writes=[psn])
                        S.op("act", lambda e: e.activation(pt[:], ps_[:, :], AF.Exp, scale=0.125), reads=[psn], writes=[ptn])

                    def pv():
                        pt, ptn = st["pt"], st["ptn"]
                        first_ = (idx == 0)
                        last_ = (idx == npairs - 1)
                        if kind == "cmp":
                            c = arg
                            S.op("pe", lambda e: e.matmul(pa[0:65, :], Vc[:, c, g, :], pt[:], start=first_, stop=last_), reads=[ptn, "Vc"], writes=[pan])
                            for ts in range(4):
                                S.op("pe", lambda e, ts=ts: e.matmul(pimp[:, ts * 128:(ts + 1) * 128], pt[:, ts * 128:(ts + 1) * 128], ZA[:, c, :], start=(first_ and ts == 0), stop=(last_ and ts == 3), skip_group_check=True),
                                     reads=[ptn], writes=["pimp"], inc=(ts == 3))
                        elif kind == "slc":
                            kt = arg
                            S.op("pe", lambda e: e.matmul(pa[0:65, :], Vs[:, kt, g, :], pt[:], start=first_, stop=last_), reads=[ptn, "Vs"], writes=[pan])
                        else:
                            kt = arg
                            S.op("pe", lambda e: e.matmul(pa[0:65, :], Vw[:, kt % 8, g, :], pt[:], start=first_, stop=last_), reads=[ptn, "Vw"], writes=[pan])
                    return {"score": score, "pv": pv, "post": []}

                def run_stream(items, D=2):
                    pending = []
                    N = len(items)
                    for n in range(N + D):
                        if n < N:
                            items[n]["score"]()
                        still = []
                        for (due, fn) in pending:
                            if due <= n:
                                fn()
                            else:
                                still.append((due, fn))
                        pending = still
                        if n - D >= 0:
                            it = items[n - D]
                            it["pv"]()
                            for (dl, fn) in it["post"]:
                                if dl == 0:
                                    fn()
                                else:
                                    pending.append((n + dl, fn))
                    for (due, fn) in pending:
                        fn()

                items = []
                for hh in range(4):
                    pa, pan = new_acc()
                    for c in range(cmax + 1):
                        it = mk_item("cmp", g, hh, c, cmax + 1, pa, pan, c, i)
                        if c == cmax:
                            it["post"] = [(0, mk_imp_post(hh))] + mk_epilogue(pa, pan, g, hh, 0, True, True, False, t0)
                        items.append(it)
                run_stream(items)
                for ts in range(4):
                    tsg = 4 * i + ts
                    off = 126 - 2 * tsg
                    S.op("dve", lambda e, ts=ts, off=off: e.tensor_tensor(sc[:], impg[:, ts, :], Fadd[:, off:off + 128], ALU.add), reads=["impg"], writes=["sc"])
                    S.op("dve", lambda e: e.tensor_scalar(sc[:, 0:1], sc[:, 0:1], 1e4, None, ALU.add), reads=["sc"], writes=["sc"])
                    S.op("dve", lambda e: e.max(out=m8[:, 0:8], in_=sc[:]), reads=["sc"], writes=["m8"])
                    S.op("dve", lambda e: e.match_replace(out=sc2[:], in_to_replace=m8[:, 0:8], in_values=sc[:], imm_value=-3e4), reads=["sc", "m8"], writes=["sc2"])
                    S.op("dve", lambda e: e.max(out=m8[:, 8:16], in_=sc2[:]), reads=["sc2"], writes=["m8"])
                    S.op("dve", lambda e: e.tensor_scalar(sc2[:], sc[:], m8[:, 15:16], BIG, ALU.is_ge, ALU.mult), reads=["sc", "m8"], writes=["sc2"])
                    S.op("dve", lambda e, off=off: e.scalar_tensor_tensor(sc2[:], sc2[:], -BIG, Finv[:, off:off + 128], ALU.add, ALU.add), reads=["sc2"], writes=["sc2"])
                    S.op("dve", lambda e, ts=ts: e.tensor_copy(selb[:, ts, 0, :], sc2[:]), reads=["sc2"], writes=["selb"])
                    S.op("dve", lambda e, ts=ts: e.tensor_copy(selb[:, ts, 1, 0:64], sc2[:, 64:128]), reads=["sc2"], writes=["selb"])
                    S.op("dve", lambda e, ts=ts: e.tensor_copy(selb[:, ts, 1, 64:128], sc2[:, 0:64]), reads=["sc2"], writes=["selb"])
                items = []
                kts = [kt for kt in range(4 * i - 4, 4 * i + 4) if kt >= 0]
                for hh in range(4):
                    pa, pan = new_acc()
                    for idx, kt in enumerate(kts):
                        it = mk_item("win", g, hh, idx, len(kts), pa, pan, kt, i)
                        if idx == len(kts) - 1:
                            it["post"] = mk_epilogue(pa, pan, g, hh, 2, False, False, False, t0)
                        items.append(it)
                run_stream(items)
                for ts in range(4):
                    S.op("pe", lambda e, ts=ts: e.matmul(pmisc[:, 0:128], selb[:, ts, 1, :], ident_bf[:], start=True, stop=True), reads=["selb"], writes=["pmisc"], inc=False)
                    S.op("pe", lambda e, ts=ts: e.matmul(pmisc[:, 128:256], selb[:, ts, 0, :], ident_bf[:], start=True, stop=True), reads=["selb"], writes=["pmisc"])
                    S.op("dve", lambda e, ts=ts: e.tensor_copy(selT[64:128, :, ts * 128:(ts + 1) * 128], pmisc[64:128, 0:256].rearrange("p (h t) -> p h t", h=2)), reads=["pmisc"], writes=["selT"])
                for hh in range(4):
                    S.op("dve", lambda e, hh=hh: e.tensor_copy(Qaug[64:128, :, hh, :], selT[64:128, :, :]), reads=["selT"], writes=["Qaug"])
                items = []
                nkt = 4 * i + 4
                for hh in range(4):
                    pa, pan = new_acc()
                    for kt in range(nkt):
                        it = mk_item("slc", g, hh, kt, nkt, pa, pan, kt, i)
                        if kt == nkt - 1:
                            it["post"] = mk_epilogue(pa, pan, g, hh, 1, False, False, True, t0)
                        items.append(it)
                run_stream(items)
        S.barrier()
        S.emit()


def pass2(nc, S, dr, out, attn_scr, dbg_out, front, load, load_cast, ident_bf, ident_f, gate_bc):
    PI = math.pi
    with ExitStack() as c2:
        def sb(name, shape, dt=F32):
            return c2.enter_context(nc.sbuf_tensor(name, list(shape), dt))

        def psb(name, shape=(128, 512), dt=F32):
            return c2.enter_context(nc.psum_tensor(name, list(shape), dt))

        csu = ExitStack()
        def sbt(name, shape, dt=F32):
            return csu.enter_context(nc.sbuf_tensor(name, list(shape), dt))
        w2 = sb("w2", [128, 8, 1536], BF16)
        wout = sb("wout", [128, 8, 1024], BF16)
        glw = sb("glw", [128, 4, 512], BF16)
        pm_f = load(c2, "pm_f", [128, 128], F32, dr["pm_f"][:, :])
        mrow = load(c2, "mrow", [128, 128], F32, dr["mrow"][:, :])
        are2 = load(c2, "are2", [128, 32], F32, dr["are2"][:, :])
        aim2 = load(c2, "aim2", [128, 32], F32, dr["aim2"][:, :])
        ldt2 = load(c2, "ldt2", [128, 32], F32, dr["ldt2"][:, :])
        cw_l = load(c2, "cw_l", [128, 32, 16], F32, dr["cw_l"][:, :, :])
        dcol = load(c2, "dcol", [128, 4], F32, dr["dcol"][:, :])
        glub = load(c2, "glub", [128, 4], F32, dr["glub_col"][:, :])

        Cm = sb("Cm", [128, 32, 128])
        Sm = sb("Sm", [128, 32, 128])
        rho = sb("rho", [128, 32])
        th = sb("th", [128, 32])
        dt2 = sb("dt2", [128, 32])
        c128 = sb("c128", [128, 32])
        s128 = sb("s128", [128, 32])
        BT = sb("BT", [128, 32, 128], BF16)
        BTs = sb("BTs", [128, 32, 128], BF16)
        Cw = sb("Cw", [128, 32, 16], BF16)
        carry = sb("carry", [128, 32])
        Rlast = sb("Rlast", [128, 32])

        stage = [sbt("stage0", [128, 2048]), sbt("stage1", [128, 2048])]
        _, w2n = load_cast(w2, "w2", [128, 8, 1536], dr["W2"], stage)
        _, woutn = load_cast(wout, "wout", [128, 8, 1024], dr["w_out"], stage)
        _, glwn = load_cast(glw, "glw", [128, 4, 512], dr["glu_w"], stage)
        tA = sbt("tA", [128, 1024])
        tB = sbt("tB", [128, 1024])
        tC = sbt("tC", [128, 1024])
        tD = sbt("tD", [128, 1024])
        tE = sbt("tE", [128, 1024])
        tF = sbt("tF", [128, 1024])
        tG = sbt("tG", [128, 1024])
        tI = sbt("tI", [128, 1024], I32)

        def sin_of(dst_ap, arg_ap, n, shift, names_r, name_w):
            a = tF[:, 0:n]
            S.op("dve", lambda e: e.tensor_scalar(a, arg_ap, 1.0, shift, ALU.mult, ALU.add), reads=names_r, writes=["tF"])
            S.op("dve", lambda e: e.tensor_scalar(tI[:, 0:n], a, 1.0 / (2 * PI), None, ALU.mult), reads=["tF"], writes=["tI"])
            S.op("dve", lambda e: e.tensor_copy(tG[:, 0:n], tI[:, 0:n]), reads=["tI"], writes=["tG"])
            S.op("dve", lambda e: e.scalar_tensor_tensor(tG[:, 0:n], tG[:, 0:n], -2 * PI, a, ALU.mult, ALU.add), reads=["tG", "tF"], writes=["tG"])
            S.op("dve", lambda e: e.tensor_scalar(tG[:, 0:n], tG[:, 0:n], 3.14159, -3.14159, ALU.min, ALU.max), reads=["tG"], writes=["tG"])
            S.op("act", lambda e: e.activation(dst_ap, tG[:, 0:n], AF.Sin), reads=["tG"], writes=[name_w])

        S.op("act", lambda e: e.activation(dt2[:], ldt2[:], AF.Exp), reads=["ldt2"], writes=["dt2"])
        S.op("dve", lambda e: e.tensor_tensor(th[:], aim2[:], dt2[:], ALU.mult), reads=["aim2", "dt2"], writes=["th"])
        S.op("dve", lambda e: e.tensor_tensor(rho[:], are2[:], dt2[:], ALU.mult), reads=["are2", "dt2"], writes=["rho"])
        S.op("act", lambda e: e.activation(rho[:], rho[:], AF.Exp), reads=["rho"], writes=["rho"])
        S.op("dve", lambda e: e.tensor_scalar(tA[:, 0:32], th[:], 128.0, None, ALU.mult), reads=["th"], writes=["tA"])
        sin_of(s128[:], tA[:, 0:32], 32, 0.0, ["tA"], "s128")
        sin_of(c128[:], tA[:, 0:32], 32, PI / 2, ["tA"], "c128")
        for gq in range(4):
            for gi in range(8):
                g = gq * 8 + gi
                S.op("dve", lambda e, g=g, gi=gi: e.tensor_scalar(tA[:, gi * 128:(gi + 1) * 128], mrow[:], th[:, g:g + 1], None, ALU.mult), reads=["mrow", "th"], writes=["tA"])
            sin_of(Sm[:, gq * 8:(gq + 1) * 8, :].rearrange("p g m -> p (g m)"), tA[:, 0:1024], 1024, 0.0, ["tA"], "Sm")
            sin_of(Cm[:, gq * 8:(gq + 1) * 8, :].rearrange("p g m -> p (g m)"), tA[:, 0:1024], 1024, PI / 2, ["tA"], "Cm")
        S.op("dve", lambda e: e.tensor_copy(Cw[0:64, :, :], cw_l[0:64, :, :]), reads=["cw_l"], writes=["Cw0"])
        S.op("dve", lambda e: e.tensor_scalar(Cw[64:128, :, :], cw_l[64:128, :, :], -1.0, None, ALU.mult), reads=["cw_l"], writes=["Cw1"])
        for gq in range(4):
            gsl = slice(gq * 8, (gq + 1) * 8)
            n = 512
            a_re_t, a_im_t, ldt_t, bre_t, bim_t = tA[:, 0:n], tA[:, n:2 * n], tB[:, 0:n], tB[:, n:2 * n], tC[:, 0:n]
            for (dst, nm, rn) in ((a_re_t, "b_are", "tA"), (a_im_t, "b_aim", "tA"), (ldt_t, "b_ldt", "tB"), (bre_t, "b_bre", "tB"), (bim_t, "b_bim", "tC")):
                S.dma("sp", lambda e, dst=dst, nm=nm, gsl=gsl: e.dma_start(out=dst.rearrange("p (g m) -> p g m", g=8), in_=dr[nm][:, gsl, :]), writes=[rn])
            dtt, rl, tht, ee = tC[:, n:2 * n], tD[:, 0:n], tD[:, n:2 * n], tE[:, 0:n]
            sn, cs = tE[:, n:2 * n], tC[:, n:2 * n]
            S.op("act", lambda e: e.activation(dtt, ldt_t, AF.Exp), reads=["tB"], writes=["tC"])
            S.op("dve", lambda e: e.tensor_tensor(rl, a_re_t, dtt, ALU.mult), reads=["tA", "tC"], writes=["tD"])
            S.op("dve", lambda e: e.tensor_tensor(tht, a_im_t, dtt, ALU.mult), reads=["tA", "tC"], writes=["tD"])
            S.op("act", lambda e: e.activation(ee, rl, AF.Exp), reads=["tD"], writes=["tE"])
            sin_of(sn, tht, n, 0.0, ["tD"], "tE")
            sin_of(cs, tht, n, PI / 2, ["tD"], "tC")
            lbr1, lbi = rl, tht
            S.op("dve", lambda e: e.tensor_tensor(lbi, ee, sn, ALU.mult), reads=["tE"], writes=["tD"])
            S.op("dve", lambda e: e.tensor_tensor(lbr1, ee, cs, ALU.mult), reads=["tE", "tC"], writes=["tD"])
            S.op("dve", lambda e: e.tensor_scalar(lbr1, lbr1, -1.0, None, ALU.add), reads=["tD"], writes=["tD"])
            den = ee
            S.op("dve", lambda e: e.tensor_tensor(den, a_re_t, a_re_t, ALU.mult), reads=["tA", "tE"], writes=["tE"])
            S.op("dve", lambda e: e.tensor_tensor(sn, a_im_t, a_im_t, ALU.mult), reads=["tA", "tE"], writes=["tE"])
            S.op("dve", lambda e: e.tensor_tensor(den, den, sn, ALU.add), reads=["tE"], writes=["tE"])
            S.op("dve", lambda e: e.reciprocal(den, den), reads=["tE"], writes=["tE"])
            fr, fi, tmp = sn, cs, ldt_t
            S.op("dve", lambda e: e.tensor_tensor(fr, lbr1, a_re_t, ALU.mult), reads=["tD", "tA"], writes=["tE"])
            S.op("dve", lambda e: e.tensor_tensor(tmp, lbi, a_im_t, ALU.mult), reads=["tD", "tA"], writes=["tB"])
            S.op("dve", lambda e: e.tensor_tensor(fr, fr, tmp, ALU.add), reads=["tE", "tB"], writes=["tE"])
            S.op("dve", lambda e: e.tensor_tensor(fr, fr, den, ALU.mult), reads=["tE"], writes=["tE"])
            S.op("dve", lambda e: e.tensor_tensor(fi, lbi, a_re_t, ALU.mult), reads=["tD", "tA"], writes=["tC"])
            S.op("dve", lambda e: e.tensor_tensor(tmp, lbr1, a_im_t, ALU.mult), reads=["tD", "tA"], writes=["tB"])
            S.op("dve", lambda e: e.tensor_tensor(fi, fi, tmp, ALU.subtract), reads=["tC", "tB"], writes=["tC"])
            S.op("dve", lambda e: e.tensor_tensor(fi, fi, den, ALU.mult), reads=["tC", "tE"], writes=["tC"])
            br_, bi_ = lbr1, lbi
            S.op("dve", lambda e: e.tensor_tensor(br_, fr, bre_t, ALU.mult), reads=["tE", "tB"], writes=["tD"])
            S.op("dve", lambda e: e.tensor_tensor(tmp, fi, bim_t, ALU.mult), reads=["tC"], writes=["tB"])
            S.op("dve", lambda e: e.tensor_tensor(br_, br_, tmp, ALU.subtract), reads=["tD", "tB"], writes=["tD"])
            S.op("dve", lambda e: e.tensor_tensor(bi_, fr, bim_t, ALU.mult), reads=["tE", "tC"], writes=["tD"])
            S.op("dve", lambda e: e.tensor_tensor(tmp, fi, bre_t, ALU.mult), reads=["tC", "tB"], writes=["tB"])
            S.op("dve", lambda e: e.tensor_tensor(bi_, bi_, tmp, ALU.add), reads=["tD", "tB"], writes=["tD"])
            v3 = lambda ap: ap.rearrange("p (g m) -> p g m", g=8)
            S.op("dve", lambda e, gsl=gsl: e.tensor_copy(BT[:, gsl, 0:64], v3(br_)), reads=["tD"], writes=["BT"])
            S.op("dve", lambda e, gsl=gsl: e.tensor_copy(BT[:, gsl, 64:128], v3(bi_)), reads=["tD"], writes=["BT"])
            S.op("dve", lambda e, gsl=gsl: e.tensor_copy(BTs[:, gsl, 0:64], v3(bi_)), reads=["tD"], writes=["BTs"])
            S.op("dve", lambda e, gsl=gsl: e.tensor_scalar(BTs[:, gsl, 64:128], v3(br_), -1.0, None, ALU.mult), reads=["tD"], writes=["BTs"])
        S.op("dve", lambda e: e.memset(carry[:], 0.0), writes=["carry"])
        S.barrier()
        S.emit()
        csu.close()

        xts = [sb("xt%d" % i, [128, 1024]) for i in range(2)]
        xrs = [sb("xr%d" % i, [128, 1024]) for i in range(1)]
        xn_t = sb("xn", [128, 1024], BF16)
        Wf = {"junk": xn_t, "ssq": sb("ssq", [128, 4]), "xn": xn_t,
              "ptr": None}
        hTs = [sb("hT%d" % j, [128, 8, 512], BF16) for j in range(2)]
        uT_bfs = [sb("uT_bf%d" % j, [128, 4, 512], BF16) for j in range(2)]
        uT_fs = uT_bfs
        szss = [sb("szs%d" % j, [128, 4, 512], BF16) for j in range(2)]
        szas = [sb("sza%d" % j, [128, 4, 512], BF16) for j in range(2)]
        bufA = [sb("bufA%d" % j, [128, 256]) for j in range(2)]
        bufB = [sb("bufB%d" % j, [128, 256]) for j in range(2)]
        bB16 = [sb("bB16_%d" % j, [128, 256], BF16) for j in range(2)]
        bC16 = [sb("bC16_%d" % j, [128, 256], BF16) for j in range(2)]
        bufC = [None, None]
        Rts = [sb("Rt%d" % j, [128, 2, 128]) for j in range(2)]
        Rtbs = [sb("Rtb%d" % j, [128, 256], BF16) for j in range(2)]
        pm_b = sb("pm_b", [128, 128], BF16)
        S.op("dve", lambda e: e.tensor_copy(pm_b[:], pm_f[:]), reads=["pm_f"], writes=["pm_b"])
        ysb = sb("ysb", [128, 512])
        y2 = sb("y2", [128, 4, 128])
        gy_bs = [sb("gy_b%d" % j, [128, 4, 512], BF16) for j in range(2)]
        sig = sb("sig", [128, 512])
        m1 = sb("m1", [128, 512])
        mixs = [sb("mix%d" % j, [128, 8, 512], BF16) for j in range(2)]
        res = sb("res", [128, 1024])
        ct1 = sb("ct1", [128, 32])
        ct2 = sb("ct2", [128, 32])
        pproj = psb("pproj")
        Wf["ptr"] = pproj
        Wf["ptrn"] = "pproj"
        pVW = [psb("pVW0"), psb("pVW1")]
        pRs = [psb("pR0"), psb("pR1")]
        pY = psb("pY")
        pT = psb("pT")
        pO = [psb("pO0"), pproj]
        pOn = ["pO0", "pproj"]

        if not WITH_ATTN:
            for j_ in range(2):
                S.op("pool", lambda e, j_=j_: e.memset(mixs[j_][:, 0:4, :], 0.0), writes=["mix_attn%d" % j_])

        def make_units(i):
            cp = i % 2
            hT, uT_bf, uT_f, szs, sza = hTs[cp], uT_bfs[cp], uT_fs[cp], szss[cp], szas[cp]
            hn = "hT%d" % cp
            xs_ = [xts[tt % 2] for tt in range(4)]
            xn_ = ["xt%d" % (tt % 2) for tt in range(4)]
            hnames = [(hn, "act"), (hn, "dve")]
            units = []
            for tt in range(4):
                units.append(lambda tt=tt: front(i, xs_, xn_, hT, hn, Wf, ["act"], tts=(tt,)))

            def proj_unit(ct):
                for kc in range(8):
                    S.op("pe", lambda e, kc=kc: e.matmul(pproj[:, :], w2[:, kc, ct * 128:(ct + 1) * 128], hT[:, kc, :], start=(kc == 0), stop=(kc == 7)),
                         reads=hnames, writes=["pproj"], inc=(kc == 7))
                if ct < 4:
                    S.op("act", lambda e: e.copy(uT_bf[:, ct, :], pproj[:, :]), reads=["pproj"], writes=["uT_bf%d" % cp])
                elif ct < 8:
                    S.op("act", lambda e: e.activation(szs[:, ct - 4, :], pproj[:, :], AF.Silu), reads=["pproj"], writes=["szs%d" % cp])
                else:
                    S.op("act", lambda e: e.activation(sza[:, ct - 8, :], pproj[:, :], AF.Silu), reads=["pproj"], writes=["sza%d" % cp])
            for ct in range(12):
                units.append(lambda ct=ct: proj_unit(ct))
            return units

        for u_ in make_units(0):
            u_()
        pending_tail = []
        for i in range(NCH_RUN):
            t0 = i * CH
            cp = i % 2
            hT, uT_bf, uT_f, szs, sza = hTs[cp], uT_bfs[cp], uT_fs[cp], szss[cp], szas[cp]
            uTbn, uTfn, szsn, szan = "uT_bf%d" % cp, "uT_bf%d" % cp, "szs%d" % cp, "sza%d" % cp
            gy_b = gy_bs[cp]
            gyn = "gy_b%d" % cp
            units = pending_tail + (make_units(i + 1) if i + 1 < NCH_RUN else [])
            pending_tail = []
            for f in range(4):
                fl = f * 128

                def stageA1(gb, fl=fl, uT_bf=uT_bf):
                    par = gb % 2
                    po = par * 256
                    g0 = gb * 2
                    An, Bn = "bA%d" % par, "bB%d" % par
                    pVn, pWn = "pV%d" % par, "pW%d" % par
                    bA, bB = bufA[par], bufB[par]
                    for gi in range(2):
                        g = g0 + gi
                        ctg = g // 8
                        base = 64 * ((g % 8) // 4)
                        S.op("pe", lambda e, g=g, gi=gi, ctg=ctg, base=base: e.matmul(pVW[par][:, gi * 128:(gi + 1) * 128], BT[base:base + 64, g, :], uT_bf[base:base + 64, ctg, fl:fl + 128], start=True, stop=True),
                             reads=[uTbn], writes=[pVn], inc=(gi == 1))
                    for gi in range(2):
                        g = g0 + gi
                        ctg = g // 8
                        base = 64 * ((g % 8) // 4)
                        S.op("pe", lambda e, g=g, gi=gi, ctg=ctg, base=base: e.matmul(pVW[par][:, 256 + gi * 128:256 + (gi + 1) * 128], BTs[base:base + 64, g, :], uT_bf[base:base + 64, ctg, fl:fl + 128], start=True, stop=True),
                             reads=[uTbn], writes=[pWn], inc=(gi == 1))
                    cmv = Cm[:, g0:g0 + 2, :].rearrange("p g m -> p (g m)")
                    smv = Sm[:, g0:g0 + 2, :].rearrange("p g m -> p (g m)")
                    S.op("dve", lambda e: e.tensor_tensor(bA[:], pVW[par][:, 0:256], cmv, ALU.mult), reads=[pVn, pWn], writes=[An])
                    S.op("dve", lambda e: e.tensor_tensor(bB[:], pVW[par][:, 256:512], smv, ALU.mult), reads=[pVn, pWn], writes=[Bn])
                    S.op("dve", lambda e: e.tensor_tensor(bA[:], bA[:], bB[:], ALU.add), reads=[An, Bn], writes=[An])

                def stageA2(gb):
                    par = gb % 2
                    g0 = gb * 2
                    An, Rn, Rbn = "bA%d" % par, "Rt%d" % par, "Rtb%d" % par
                    bA, Rt_, Rtb_ = bufA[par], Rts[par], Rtbs[par]
                    for gi in range(2):
                        g = g0 + gi
                        S.op("dve", lambda e, g=g, gi=gi: e.tensor_tensor_scan(Rt_[:, gi, :], rho[:, g:g + 1].to_broadcast([128, 128]), bA[:, gi * 128:(gi + 1) * 128], carry[:, g:g + 1], ALU.mult, ALU.add),
                             reads=[An, "carry"], writes=[Rn])
                    S.op("act", lambda e: e.copy(Rlast[:, g0:g0 + 2], Rt_[:, :, 127]), reads=[Rn], writes=["Rlast"])
                    rtv = Rt_[:, :, :].rearrange("p g m -> p (g m)")
                    S.op("act", lambda e: e.copy(Rtb_[:], rtv), reads=[Rn], writes=[Rbn])

                def stageB(gb):
                    par = gb % 2
                    po = par * 256
                    g0 = gb * 2
                    Rn, Rbn, pRn = "Rt%d" % par, "Rtb%d" % par, "pR%d" % par
                    Rt_, Rtb_ = Rts[par], Rtbs[par]
                    cmv = Cm[:, g0:g0 + 2, :].rearrange("p g m -> p (g m)")
                    smv = Sm[:, g0:g0 + 2, :].rearrange("p g m -> p (g m)")
                    rtv = Rt_[:, :, :].rearrange("p g m -> p (g m)")
                    S.op("pe", lambda e: e.matmul(pRs[par][:, 0:256], pm_b[:], Rtb_[:], start=True, stop=True), reads=[Rbn], writes=[pRn])
                    S.op("pool", lambda e: e.tensor_tensor(bB16[par][:], rtv, cmv, ALU.mult), reads=[Rn], writes=["bB16_%d" % par])
                    S.op("dve", lambda e: e.tensor_tensor(bC16[par][:], pRs[par][:, 0:256], smv, ALU.mult), reads=[pRn], writes=["bC16_%d" % par])
                    for gi in range(2):
                        g = g0 + gi
                        S.op("pe", lambda e, g=g, gi=gi: e.matmul(pY[:, g * 16:(g + 1) * 16], bB16[par][:, gi * 128:(gi + 1) * 128], Cw[:, g, :], start=True, stop=False),
                             reads=["bB16_%d" % par], writes=["pY"], inc=False)
                        S.op("pe", lambda e, g=g, gi=gi: e.matmul(pY[:, g * 16:(g + 1) * 16], bC16[par][:, gi * 128:(gi + 1) * 128], Cw[:, g, :], start=False, stop=True),
                             reads=["bC16_%d" % par], writes=["pY"], inc=(gi == 1))

                NB = 16
                stageA1(0)
                stageA2(0)
                for gb in range(NB):
                    if gb + 1 < NB:
                        stageA1(gb + 1)
                    stageB(gb)
                    if gb + 1 < NB:
                        stageA2(gb + 1)
                    if gb % 2 == 1 and units:
                        units.pop(0)()
                S.op("pe", lambda e: e.matmul(pRs[0][:, 0:32], pm_f[:], Rlast[:], start=True, stop=True), reads=["pm_f", "Rlast"], writes=["pR0"])
                S.op("dve", lambda e: e.tensor_tensor(ct1[:], Rlast[:], c128[:], ALU.mult), reads=["Rlast", "c128"], writes=["ct1"])
                S.op("dve", lambda e: e.tensor_tensor(ct2[:], pRs[0][:, 0:32], s128[:], ALU.mult), reads=["pR0", "s128"], writes=["ct2"])
                S.op("dve", lambda e: e.tensor_tensor(carry[:], ct1[:], ct2[:], ALU.add), reads=["ct1", "ct2"], writes=["carry"])
                S.op("act", lambda e: e.copy(ysb[:], pY[:, :]), reads=["pY"], writes=["ysb"])
                for ct in range(4):
                    S.op("pe", lambda e, ct=ct: e.transpose(pT[:, ct * 128:(ct + 1) * 128], ysb[:, ct * 128:(ct + 1) * 128], ident_f[:]), reads=["ysb", "ident_f"], writes=["pT"], inc=(ct == 3))
                for ct in range(4):
                    S.op("dve", lambda e, ct=ct, fl=fl, uT_f=uT_f: e.scalar_tensor_tensor(y2[:, ct, :], uT_f[:, ct, fl:fl + 128], dcol[:, ct:ct + 1], pT[:, ct * 128:(ct + 1) * 128], ALU.mult, ALU.add),
                         reads=[uTfn, "dcol", "pT"], writes=["y2"])
                S.op("act", lambda e, fl=fl, gy_b=gy_b: e.activation(gy_b[:, :, fl:fl + 128], y2[:, :, :], AF.Gelu), reads=["y2"], writes=[gyn])
            while units:
                units.pop(0)()

            def make_tail(i, cp, t0):
                gy_b, mix, szs, sza = gy_bs[cp], mixs[cp], szss[cp], szas[cp]
                gyn, szsn, szan = "gy_b%d" % cp, "szs%d" % cp, "sza%d" % cp
                man, msn = "mix_attn%d" % cp, "mix_ssm%d" % cp
                tl = []

                def attn_unit():
                    if WITH_ATTN:
                        av = attn_scr.rearrange("(kc p) t -> p kc t", p=128)
                        S.dma("sp", lambda e: e.dma_start(out=mix[:, 0:4, :], in_=av[:, :, t0:t0 + CH]), reads=["attn_scr"], writes=[man])
                        S.op("pool", lambda e: e.tensor_tensor(mix[:, 0:4, :], mix[:, 0:4, :], sza[:, :, :], ALU.mult), reads=[man, szan], writes=[man])
                tl.append(attn_unit)

                def glu_unit(co):
                    for ct in range(4):
                        S.op("pe", lambda e, ct=ct: e.matmul(pT[:, :], glw[:, ct, co * 128:(co + 1) * 128], gy_b[:, ct, :], start=(ct == 0), stop=(ct == 3)),
                             reads=glwn + [gyn], writes=["pT"], inc=(ct == 3))
                    S.op("act", lambda e: e.activation(sig[:], pT[:, :], AF.Sigmoid, bias=glub[:, co:co + 1]), reads=["pT", "glub"], writes=["sig"])
                    S.op("dve", lambda e: e.tensor_tensor(m1[:], gy_b[:, co, :], sig[:], ALU.mult), reads=[gyn, "sig"], writes=["m1"])
                    S.op("pool", lambda e: e.tensor_tensor(mix[:, 4 + co, :], m1[:], szs[:, co, :], ALU.mult), reads=["m1", szsn], writes=[msn])
                for co in range(4):
                    tl.append(lambda co=co: glu_unit(co))

                def out_unit(tt):
                    gt = 4 * i + tt
                    for hf in range(2):
                        for kc in range(8):
                            S.op("pe", lambda e, hf=hf, kc=kc: e.matmul(pO[hf][:, :], mix[:, kc, tt * 128:(tt + 1) * 128], wout[:, kc, hf * 512:(hf + 1) * 512], start=(kc == 0), stop=(kc == 7)),
                                 reads=woutn + [man, msn], writes=[pOn[hf]], inc=(kc == 7))
                        S.op("dve", lambda e, hf=hf: e.tensor_tensor(res[:, hf * 512:(hf + 1) * 512], pO[hf][:, :], gate_bc[:, hf * 512:(hf + 1) * 512], ALU.mult),
                             reads=[pOn[hf], "gate_bc"], writes=["res%d" % hf])
                    xr = xrs[0]
                    S.dma("sp", lambda e: e.dma_start(out=xr[:], in_=dr["x"][gt * 128:(gt + 1) * 128, :]), writes=["xr0"])
                    S.op("dve", lambda e: e.tensor_tensor(res[:], res[:], xr[:], ALU.add), reads=["res0", "res1", "xr0"], writes=["res0", "res1"])
                    S.dma("sp", lambda e: e.dma_start(out=out[gt * 128:(gt + 1) * 128, :], in_=res[:]), reads=["res0", "res1"], writes=["out"])
                for tt in range(4):
                    tl.append(lambda tt=tt: out_unit(tt))
                return tl

            pending_tail = make_tail(i, cp, t0)
            if i + 1 >= NCH_RUN:
                for u_ in pending_tail:
                    u_()
                pending_tail = []
        S.barrier()
        S.emit()


_CACHE = {}


def kernel(**inputs):
    consts = make_consts()
    params = make_params(inputs)
    specs = input_specs(consts, params)
    nc = build(specs)
    x = np.asarray(inputs["x"], np.float32)
    c = np.asarray(inputs["c"], np.float32)
    in_maps = []
    for b in range(8):
        m = {"x": np.ascontiguousarray(x[b]), "c_col": np.ascontiguousarray(c[b].reshape(8, 128).T)}
        m.update(consts)
        m.update(params)
        in_maps.append(m)
    res = run_bass_kernel_spmd(nc, in_maps, core_ids=list(range(8)))
    return np.stack([np.asarray(r["out"], np.float32) for r in res.results], 0)
```
